# Optimizing a Trainium2 kernel written in Bass

```python
import math
import jax, jax.numpy as jnp
from jax import lax
import numpy as np

D_MODEL = 1024
BATCH = 2
SEQ = 8192
DEPTH = 4
DEC_BATCH = 128
DEC_SEQ = 1
PAST_LEN = 8192
PAGE_SIZE = 128

N_MEM = 256
EPS = 1e-6
GLA_HEADS = 4
GLA_DK = 48
GLA_DV = 96
GLA_RANK = 16
GLA_TAU = 16.0
GLA_CHUNK = 64
RG_WIDTH = 384
RG_BLOCKS = 4
RG_BW = RG_WIDTH // RG_BLOCKS
RG_C = 8.0
CONV_W = 4
SWA_HEADS = 4
SWA_KV = 2
SWA_G = SWA_HEADS // SWA_KV
HEAD_DIM = 64
WINDOW = 128
MEM_HEADS = 4
MEM_HD = 64
D_FF = 4 * D_MODEL

GLA_QK = GLA_HEADS * GLA_DK
GLA_V = GLA_HEADS * GLA_DV
SWA_Q = SWA_HEADS * HEAD_DIM
SWA_KVW = SWA_KV * HEAD_DIM
MIX_WIDTH = GLA_V + RG_WIDTH + SWA_Q
IN_SPLITS = (GLA_QK, GLA_QK, GLA_V, GLA_RANK, GLA_V, RG_WIDTH, RG_WIDTH, SWA_Q, SWA_KVW, SWA_KVW)
IN_WIDTH = GLA_QK * 2 + GLA_V * 2 + GLA_RANK + RG_WIDTH * 2 + SWA_Q + SWA_KVW * 2

kernel_name = "hymba_gla_rglru_swa_step"


def rmsnorm(x, g):
    xf = x.astype(jnp.float32)
    y = xf * lax.rsqrt(jnp.mean(xf * xf, axis=-1, keepdims=True) + EPS)
    return (y * g.astype(jnp.float32)).astype(x.dtype)


def split_cols(h):
    outs, start = [], 0
    for w in IN_SPLITS:
        outs.append(h[..., start:start + w])
        start += w
    return outs


def alibi_slopes():
    return jnp.exp2(-8.0 * jnp.arange(1, SWA_HEADS + 1, dtype=jnp.float32) / SWA_HEADS)


def gla_chunk_step(S0, inp):
    q, k, v, g = inp
    C = q.shape[1]
    cum = jnp.cumsum(g, axis=1)
    causal = jnp.tril(jnp.ones((C, C), bool))
    rel = jnp.where(causal[None, :, :, None, None], cum[:, :, None] - cum[:, None, :], -jnp.inf)
    A = jnp.einsum('bthk,bshk,btshk->bhts', q, k, jnp.exp(rel))
    o = jnp.einsum('bthk,bhkv->bthv', q * jnp.exp(cum), S0) + jnp.einsum('bhts,bshv->bthv', A, v)
    last = cum[:, -1]
    S = jnp.exp(last)[..., None] * S0 + jnp.einsum('bshk,bshv->bhkv', k * jnp.exp(last[:, None] - cum), v)
    return S, o


def gla_scan(S0, q, k, v, g):
    B, T = q.shape[:2]
    C = math.gcd(T, GLA_CHUNK)

    def to_chunks(a):
        return jnp.moveaxis(a.reshape(B, T // C, C, *a.shape[2:]), 1, 0)

    S, o = lax.scan(gla_chunk_step, S0, (to_chunks(q), to_chunks(k), to_chunks(v), to_chunks(g)))
    return S, jnp.moveaxis(o, 0, 1).reshape(B, T, GLA_HEADS, GLA_DV)


def causal_dwconv(x, buf, w, b):
    T = x.shape[1]
    xp = jnp.concatenate([buf.astype(x.dtype), x], axis=1)
    y = b
    for j in range(CONV_W):
        y = y + xp[:, j:j + T] * w[j]
    return y, xp[:, -(CONV_W - 1):]


def rglru(xc, h0, w_a, b_a, w_x, b_x, lam):
    f32 = jnp.float32
    B, T, _ = xc.shape
    xf = xc.astype(f32)
    xb = xf.reshape(B, T, RG_BLOCKS, RG_BW)
    r = jax.nn.sigmoid(jnp.einsum('btnc,ncd->btnd', xb, w_a.astype(f32)).reshape(B, T, RG_WIDTH) + b_a.astype(f32))
    i = jax.nn.sigmoid(jnp.einsum('btnc,ncd->btnd', xb, w_x.astype(f32)).reshape(B, T, RG_WIDTH) + b_x.astype(f32))
    log_a = -RG_C * r * jax.nn.softplus(-lam.astype(f32))
    a = jnp.exp(log_a)
    bx = jnp.sqrt(-jnp.expm1(2.0 * log_a)) * (i * xf)
    bx = bx.at[:, 0].add(a[:, 0] * h0)

    def comb(l, rr):
        return (l[0] * rr[0], rr[0] * l[1] + rr[1])

    _, h = lax.associative_scan(comb, (a, bx), axis=1)
    return h, h[:, -1]


def swa_attend(q, k, v, key_valid, slopes, sinks):
    T, S = q.shape[2], k.shape[2]
    dist = (S - T + jnp.arange(T, dtype=jnp.float32))[:, None] - jnp.arange(S, dtype=jnp.float32)[None, :]
    allowed = (dist >= 0) & (dist <= WINDOW) & key_valid[:, None, :]
    s = jnp.einsum('bntkgd,bnskd->bnkgts', q, k).astype(jnp.float32) * HEAD_DIM ** -0.5
    s = s - slopes[:, :, None, None] * dist
    s = jnp.where(allowed[None, :, None, None], s, -jnp.inf)
    sink = jnp.broadcast_to(sinks.astype(jnp.float32)[:, :, None, None], s.shape[:-1] + (1,))
    p = jax.nn.softmax(jnp.concatenate([s, sink], axis=-1), axis=-1)[..., :-1]
    return jnp.einsum('bnkgts,bnskd->bntkgd', p.astype(v.dtype), v)


def mixer_block(x, gla_S0, rg_h0, rg_buf, swa_kbuf, swa_vbuf, g_mix, w_in, gla_w_gate2, gla_b_gate, gla_g_out,
                rg_conv_w, rg_conv_b, rg_w_a, rg_b_a, rg_w_x, rg_b_x, rg_lam, swa_g_q, swa_g_k, swa_sinks, w_out):
    f32 = jnp.float32
    B, T, _ = x.shape
    h = rmsnorm(x, g_mix) @ w_in
    gq, gk, gv, glr, gog, rx, ry, sq, sk, sv = split_cols(h)
    q = gq.reshape(B, T, GLA_HEADS, GLA_DK).astype(f32) * GLA_DK ** -0.5
    k = gk.reshape(B, T, GLA_HEADS, GLA_DK).astype(f32)
    v = gv.reshape(B, T, GLA_HEADS, GLA_DV).astype(f32)
    glog = jax.nn.log_sigmoid((glr @ gla_w_gate2 + gla_b_gate).astype(f32)).reshape(B, T, GLA_HEADS, GLA_DK) / GLA_TAU
    S, o = gla_scan(gla_S0.astype(f32), q, k, v, glog)
    o_gla = (rmsnorm(o, gla_g_out) * jax.nn.silu(gog.astype(f32)).reshape(B, T, GLA_HEADS, GLA_DV))
    o_gla = o_gla.reshape(B, T, GLA_V).astype(x.dtype)
    xc, rg_buf_new = causal_dwconv(rx, rg_buf, rg_conv_w, rg_conv_b)
    hr, h_last = rglru(xc, rg_h0.astype(f32), rg_w_a, rg_b_a, rg_w_x, rg_b_x, rg_lam)
    o_rg = (hr * jax.nn.gelu(ry.astype(f32))).astype(x.dtype)
    q = rmsnorm(sq.reshape(B, T, SWA_HEADS, HEAD_DIM), swa_g_q).reshape(B, T, SWA_KV, SWA_G, HEAD_DIM)
    k = rmsnorm(sk.reshape(B, T, SWA_KV, HEAD_DIM), swa_g_k)
    v = sv.reshape(B, T, SWA_KV, HEAD_DIM)
    slopes = alibi_slopes().reshape(SWA_KV, SWA_G)
    sinks = swa_sinks.reshape(SWA_KV, SWA_G)
    if swa_kbuf is None:
        NB = T // WINDOW
        qb = q.reshape(B, NB, WINDOW, SWA_KV, SWA_G, HEAD_DIM)

        def band(a):
            ab = a.reshape(B, NB, WINDOW, SWA_KV, HEAD_DIM)
            prev = jnp.concatenate([jnp.zeros_like(ab[:, :1]), ab[:, :-1]], axis=1)
            return jnp.concatenate([prev, ab], axis=2)

        valid = (jnp.arange(NB)[:, None] > 0) | (jnp.arange(2 * WINDOW)[None, :] >= WINDOW)
        o = swa_attend(qb, band(k), band(v), valid, slopes, sinks)
        kb_new, vb_new = k[:, -WINDOW:], v[:, -WINDOW:]
    else:
        kk = jnp.concatenate([swa_kbuf.astype(k.dtype), k], axis=1)
        vv = jnp.concatenate([swa_vbuf.astype(v.dtype), v], axis=1)
        valid = jnp.ones((1, WINDOW + T), bool)
        o = swa_attend(q[:, None], kk[:, None], vv[:, None], valid, slopes, sinks)
        kb_new, vb_new = kk[:, -WINDOW:], vv[:, -WINDOW:]
    o_swa = o.reshape(B, T, SWA_Q).astype(x.dtype)
    y = jnp.concatenate([o_gla, o_rg, o_swa], axis=-1) @ w_out
    return y, S.astype(x.dtype), h_last.astype(x.dtype), rg_buf_new, kb_new, vb_new


def mem_kv(mem, g_m, w_kv, g_k):
    B = mem.shape[0]
    kv = rmsnorm(mem, g_m) @ w_kv
    k, v = jnp.split(kv, 2, axis=-1)
    k = rmsnorm(k.reshape(B, -1, MEM_HEADS, MEM_HD), g_k)
    return k, v.reshape(B, -1, MEM_HEADS, MEM_HD)


def mem_attend(x, mk, mv, g_x, w_q, g_q, w_o):
    B, T, _ = x.shape
    q = rmsnorm((rmsnorm(x, g_x) @ w_q).reshape(B, T, MEM_HEADS, MEM_HD), g_q)
    s = jnp.einsum('bthd,bshd->bhts', q, mk.astype(q.dtype)).astype(jnp.float32) * MEM_HD ** -0.5
    p = jax.nn.softmax(s, axis=-1).astype(x.dtype)
    o = jnp.einsum('bhts,bshd->bthd', p, mv.astype(x.dtype)).reshape(B, T, MEM_HEADS * MEM_HD)
    return o @ w_o


def ffn(x, g, w1, w2):
    h = rmsnorm(x, g) @ w1
    return jnp.square(jax.nn.relu(h)) @ w2


def setup_inputs(seed: int = 0) -> dict:
    key = jax.random.key(seed)
    k = jax.random.split(key, 40)
    f32 = jnp.float32

    def nrm(kk, shape, scale):
        return jax.random.normal(kk, shape, f32) * scale

    def gain(kk, shape):
        return 1.0 + nrm(kk, shape, 0.02)

    u = jax.random.uniform(k[21], (DEPTH, RG_WIDTH), f32, minval=0.9, maxval=0.999)
    a_base = u ** (1.0 / RG_C)
    rg_lam = jnp.log(a_base) - jnp.log1p(-a_base)
    return {
        "x_prompt": nrm(k[0], (BATCH, SEQ, D_MODEL), 1.0),
        "x_sample": nrm(k[1], (DEC_BATCH, DEC_SEQ, D_MODEL), 1.0),
        "mem_prompt": nrm(k[2], (BATCH, N_MEM, D_MODEL), 1.0),
        "state_gla": nrm(k[3], (DEPTH, DEC_BATCH, GLA_HEADS, GLA_DK, GLA_DV), 0.3),
        "state_rg_h": nrm(k[4], (DEPTH, DEC_BATCH, RG_WIDTH), 0.5),
        "state_rg_conv": nrm(k[5], (DEPTH, DEC_BATCH, CONV_W - 1, RG_WIDTH), 1.0),
        "cache_swa_k": nrm(k[6], (DEPTH, DEC_BATCH, WINDOW, SWA_KV, HEAD_DIM), 1.0),
        "cache_swa_v": nrm(k[7], (DEPTH, DEC_BATCH, WINDOW, SWA_KV, HEAD_DIM), 1.0),
        "cache_mem_k": nrm(k[8], (DEPTH, DEC_BATCH, N_MEM, MEM_HEADS, MEM_HD), 1.0),
        "cache_mem_v": nrm(k[9], (DEPTH, DEC_BATCH, N_MEM, MEM_HEADS, MEM_HD), 1.0),
        "g_mix": gain(k[10], (DEPTH, D_MODEL)),
        "w_in": nrm(k[11], (DEPTH, D_MODEL, IN_WIDTH), D_MODEL ** -0.5),
        "gla_w_gate2": nrm(k[12], (DEPTH, GLA_RANK, GLA_QK), GLA_RANK ** -0.5),
        "gla_b_gate": nrm(k[13], (DEPTH, GLA_QK), 0.1),
        "gla_g_out": gain(k[14], (DEPTH, GLA_DV)),
        "rg_conv_w": nrm(k[15], (DEPTH, CONV_W, RG_WIDTH), CONV_W ** -0.5),
        "rg_conv_b": nrm(k[16], (DEPTH, RG_WIDTH), 0.02),
        "rg_w_a": nrm(k[17], (DEPTH, RG_BLOCKS, RG_BW, RG_BW), RG_BW ** -0.5),
        "rg_b_a": nrm(k[18], (DEPTH, RG_WIDTH), 0.02),
        "rg_w_x": nrm(k[19], (DEPTH, RG_BLOCKS, RG_BW, RG_BW), RG_BW ** -0.5),
        "rg_b_x": nrm(k[20], (DEPTH, RG_WIDTH), 0.02),
        "rg_lam": rg_lam,
        "swa_g_q": gain(k[22], (DEPTH, HEAD_DIM)),
        "swa_g_k": gain(k[23], (DEPTH, HEAD_DIM)),
        "swa_sinks": nrm(k[24], (DEPTH, SWA_HEADS), 0.5),
        "w_out": nrm(k[25], (DEPTH, MIX_WIDTH, D_MODEL), 0.5 * MIX_WIDTH ** -0.5),
        "g_mem_x": gain(k[26], (DEPTH, D_MODEL)),
        "g_mem_m": gain(k[27], (DEPTH, D_MODEL)),
        "mem_w_q": nrm(k[28], (DEPTH, D_MODEL, MEM_HEADS * MEM_HD), D_MODEL ** -0.5),
        "mem_w_kv": nrm(k[29], (DEPTH, D_MODEL, 2 * MEM_HEADS * MEM_HD), D_MODEL ** -0.5),
        "mem_g_q": gain(k[30], (DEPTH, MEM_HD)),
        "mem_g_k": gain(k[31], (DEPTH, MEM_HD)),
        "mem_w_o": nrm(k[32], (DEPTH, MEM_HEADS * MEM_HD, D_MODEL), 0.5 * (MEM_HEADS * MEM_HD) ** -0.5),
        "g_ffn": gain(k[33], (DEPTH, D_MODEL)),
        "ffn_w1": nrm(k[34], (DEPTH, D_MODEL, D_FF), D_MODEL ** -0.5),
        "ffn_w2": nrm(k[35], (DEPTH, D_FF, D_MODEL), 0.5 * D_FF ** -0.5),
    }


def reference(x_prompt, x_sample, mem_prompt, state_gla, state_rg_h, state_rg_conv, cache_swa_k, cache_swa_v,
              cache_mem_k, cache_mem_v, g_mix, w_in, gla_w_gate2, gla_b_gate, gla_g_out, rg_conv_w, rg_conv_b,
              rg_w_a, rg_b_a, rg_w_x, rg_b_x, rg_lam, swa_g_q, swa_g_k, swa_sinks, w_out, g_mem_x, g_mem_m,
              mem_w_q, mem_w_kv, mem_g_q, mem_g_k, mem_w_o, g_ffn, ffn_w1, ffn_w2):
    xp, xs = x_prompt, x_sample
    Bp = xp.shape[0]
    gla_p, gla_s, rgh_p, rgh_s, rgc_p, rgc_s = [], [], [], [], [], []
    swk_p, swk_s, swv_p, swv_s, memk_p, memv_p = [], [], [], [], [], []
    for l in range(DEPTH):
        mix_w = (g_mix[l], w_in[l], gla_w_gate2[l], gla_b_gate[l], gla_g_out[l], rg_conv_w[l], rg_conv_b[l],
                 rg_w_a[l], rg_b_a[l], rg_w_x[l], rg_b_x[l], rg_lam[l], swa_g_q[l], swa_g_k[l], swa_sinks[l], w_out[l])
        S0 = jnp.zeros((Bp, GLA_HEADS, GLA_DK, GLA_DV), jnp.float32)
        h0 = jnp.zeros((Bp, RG_WIDTH), jnp.float32)
        cb0 = jnp.zeros((Bp, CONV_W - 1, RG_WIDTH), xp.dtype)
        y, S, hl, cb, kb, vb = mixer_block(xp, S0, h0, cb0, None, None, *mix_w)
        xp = xp + y
        mk, mv = mem_kv(mem_prompt, g_mem_m[l], mem_w_kv[l], mem_g_k[l])
        xp = xp + mem_attend(xp, mk, mv, g_mem_x[l], mem_w_q[l], mem_g_q[l], mem_w_o[l])
        xp = xp + ffn(xp, g_ffn[l], ffn_w1[l], ffn_w2[l])
        gla_p.append(S); rgh_p.append(hl); rgc_p.append(cb); swk_p.append(kb); swv_p.append(vb)
        memk_p.append(mk); memv_p.append(mv)
        y, S, hl, cb, kb, vb = mixer_block(xs, state_gla[l], state_rg_h[l], state_rg_conv[l],
                                           cache_swa_k[l], cache_swa_v[l], *mix_w)
        xs = xs + y
        xs = xs + mem_attend(xs, cache_mem_k[l], cache_mem_v[l], g_mem_x[l], mem_w_q[l], mem_g_q[l], mem_w_o[l])
        xs = xs + ffn(xs, g_ffn[l], ffn_w1[l], ffn_w2[l])
        gla_s.append(S); rgh_s.append(hl); rgc_s.append(cb); swk_s.append(kb); swv_s.append(vb)
    return (xp, xs, jnp.stack(gla_p), jnp.stack(gla_s), jnp.stack(rgh_p), jnp.stack(rgh_s),
            jnp.stack(rgc_p), jnp.stack(rgc_s), jnp.stack(swk_p), jnp.stack(swk_s),
            jnp.stack(swv_p), jnp.stack(swv_s), jnp.stack(memk_p), jnp.stack(memv_p))
```

```python
import math
from contextlib import ExitStack
import numpy as np
import concourse.bass as bass
import concourse.mybir as mybir
from concourse.bass_utils import run_bass_kernel_spmd

F32 = mybir.dt.float32
BF16 = mybir.dt.bfloat16
AF = mybir.ActivationFunctionType
ALU = mybir.AluOpType
AX = mybir.AxisListType

D = 1024
DEPTH = 4
EPS = 1e-6
NEG = -30000.0
IN_W = 2448
C_GQ, C_GK, C_GV, C_GLR, C_GOG, C_RX, C_RY, C_SQ, C_SK, C_SV = 0, 192, 384, 768, 784, 1168, 1552, 1936, 2192, 2320
GLA_SCALE = 48 ** -0.5


class Prog:
    ENGS = ("pe", "act", "dve", "pool", "sp")

    def __init__(self, nc, n_dma_sems=8):
        self.nc = nc
        self.items = {e: [] for e in self.ENGS}
        self.cnt = {e: 0 for e in self.ENGS}
        self.last_w = {}
        self.readers = {}
        self.seen = {e: {} for e in self.ENGS}
        self.dma_cnt = {e: 0 for e in self.ENGS}
        self.n_dma_sems = n_dma_sems
        self.final_tokens = []

    @staticmethod
    def _merge(d, s, v):
        if d.get(s, 0) < v:
            d[s] = v

    def _deps(self, reads, writes):
        deps = {}
        for r in reads:
            for s, v in self.last_w.get(r, {}).items():
                self._merge(deps, s, v)
        for w in writes:
            for s, v in self.last_w.get(w, {}).items():
                self._merge(deps, s, v)
            for s, v in self.readers.get(w, {}).items():
                self._merge(deps, s, v)
        return deps

    def _commit(self, tok, reads, writes):
        s, v = tok
        for r in reads:
            self._merge(self.readers.setdefault(r, {}), s, v)
        for w in writes:
            self.last_w[w] = {s: v}
            self.readers[w] = {}

    def all_users(self, key):
        d = dict(self.last_w.get(key, {}))
        for s, v in self.readers.get(key, {}).items():
            self._merge(d, s, v)
        return d

    def inherit(self, key, deps):
        d = self.last_w.setdefault(key, {})
        for s, v in deps.items():
            self._merge(d, s, v)

    def op(self, eng, fn, reads=(), writes=()):
        deps = self._deps(reads, writes)
        self.cnt[eng] += 1
        tok = ("c_" + eng, self.cnt[eng])
        waits = []
        for s, v in deps.items():
            if s == tok[0] and eng == "pe":
                continue
            if self.seen[eng].get(s, 0) >= v:
                continue
            self.seen[eng][s] = v
            waits.append((s, v))
        self.items[eng].append((waits, fn, tok[0], 1))
        self._commit(tok, reads, writes)
        return tok

    def dma(self, q, fn, reads=(), writes=(), final=False):
        deps = self._deps(reads, writes)
        j = self.dma_cnt[q]
        self.dma_cnt[q] += 1
        sname = "d_%s_%d" % (q, j % self.n_dma_sems)
        val = 16 * (j // self.n_dma_sems + 1)
        if val > 16:
            self._merge(deps, sname, val - 16)
        tok = (sname, val)
        waits = []
        for s, v in deps.items():
            if self.seen[q].get(s, 0) >= v:
                continue
            self.seen[q][s] = v
            waits.append((s, v))
        self.items[q].append((waits, fn, sname, 16))
        self._commit(tok, reads, writes)
        if final:
            self.final_tokens.append(tok)
        return tok

    def emit(self):
        nc = self.nc
        names = set()
        for e in self.ENGS:
            for waits, fn, s, inc in self.items[e]:
                names.add(s)
                for ws, _ in waits:
                    names.add(ws)
        for s, _ in self.final_tokens:
            names.add(s)
        names = sorted(names)
        with ExitStack() as st:
            sem = {n: st.enter_context(nc.semaphore(n)) for n in names}
            block = st.enter_context(nc.Block())
            final = {}
            for s, v in self.final_tokens:
                self._merge(final, s, v)

            def run(engname, e):
                for waits, fn, s, inc in self.items[engname]:
                    for ws, wv in waits:
                        e.wait_ge(sem[ws], wv)
                    ins = fn(e)
                    ins.then_inc(sem[s], inc)
                if engname == "sp":
                    for s, v in final.items():
                        e.wait_ge(sem[s], v)

            @block.tensor
            def _(e):
                run("pe", e)

            @block.scalar
            def _(e):
                run("act", e)

            @block.vector
            def _(e):
                run("dve", e)

            @block.gpsimd
            def _(e):
                run("pool", e)

            @block.sync
            def _(e):
                run("sp", e)


class Arena:
    def __init__(self, P, tens, n_f32):
        self.P = P
        self.t = tens
        self.n = n_f32
        self.off = 0
        self.hist = []
        self.serial = 0

    def mark(self):
        return self.off

    def release(self, m):
        self.off = m

    def alloc(self, name, parts, shape, dt=F32):
        n = 1
        for s in shape:
            n *= s
        nf = n if dt == F32 else (n + 1) // 2
        nf = (nf + 15) // 16 * 16
        a = self.off
        b = a + nf
        assert b <= self.n, "arena overflow %s %d > %d" % (name, b, self.n)
        self.off = b
        self.serial += 1
        key = "%s#%d" % (name, self.serial)
        deps = {}
        keep = []
        for (k, ka, kb) in self.hist:
            if ka < b and a < kb:
                for s, v in self.P.all_users(k).items():
                    Prog._merge(deps, s, v)
                if not (a <= ka and kb <= b):
                    keep.append((k, ka, kb))
            else:
                keep.append((k, ka, kb))
        keep.append((key, a, b))
        self.hist = keep
        if deps:
            self.P.inherit(key, deps)
        ap = self.t[0:parts, a:a + nf]
        if dt != F32:
            ap = ap.bitcast(dt)[:, 0:n]
        else:
            ap = ap[:, 0:n]
        if len(shape) == 2:
            ap = ap.rearrange("p (a b) -> p a b", a=shape[0])
        elif len(shape) == 3:
            ap = ap.rearrange("p (a b c) -> p a b c", a=shape[0], b=shape[1])
        elif len(shape) == 4:
            ap = ap.rearrange("p (a b c d) -> p a b c d", a=shape[0], b=shape[1], c=shape[2])
        return ap, key


def host_consts():
    c = {}
    c["c_ident"] = np.eye(128, dtype=np.float32)
    s = np.arange(64)[:, None]
    t = np.arange(64)[None, :]
    c["c_u64"] = np.where(s <= t, -1.0 / 16.0, 0.0).astype(np.float32)
    c["c_l64"] = np.where(s > t, -1.0 / 16.0, 0.0).astype(np.float32)
    c["c_mask64"] = np.where(s <= t, GLA_SCALE, 0.0).astype(np.float32)
    slopes = np.array([2.0 ** (-2.0 * (h + 1)) for h in range(4)], np.float32)
    j = np.arange(128)[:, None, None, None]
    tt = np.arange(128)[None, None, None, :]
    h = slopes[None, None, :, None]
    bt = np.zeros((128, 2, 4, 128), np.float32)
    dist0 = (128 + tt - j).astype(np.float32)
    bt0 = np.where(dist0 <= 128, -h * dist0, NEG)
    dist1 = (tt - j).astype(np.float32)
    bt1 = np.where(dist1 >= 0, -h * dist1, NEG)
    bt[:, 0:1] = bt0
    bt[:, 1:2] = bt1
    c["c_bt"] = bt.astype(np.float32)
    bt_first = bt.copy()
    bt_first[:, 0] = NEG
    c["c_bt0"] = bt_first.astype(np.float32)
    p = np.arange(128)[:, None, None]
    jj = np.arange(16)[None, :, None]
    key = (p % 8) * 16 + jj
    c["c_alibi_dec"] = (-(128.0 - key) * slopes[None, None, :]).astype(np.float32)
    sel = np.zeros((16, 128), np.float32)
    for b in range(16):
        sel[b, b * 8:(b + 1) * 8] = 1.0
    c["c_sel"] = sel
    c["c_selT"] = np.ascontiguousarray(sel.T)
    return c


CONST_SHAPES = {"c_ident": [128, 128], "c_u64": [64, 64], "c_l64": [64, 64], "c_mask64": [64, 64],
                "c_bt": [128, 2, 4, 128], "c_bt0": [128, 2, 4, 128], "c_alibi_dec": [128, 16, 4],
                "c_sel": [16, 128], "c_selT": [128, 16]}


class KB:
    def __init__(self, T, L, NB, do_prompt=True, do_decode=True):
        self.T, self.L, self.NB = T, L, NB
        self.do_prompt, self.do_decode = do_prompt, do_decode
        self.nc = bass.Bass("TRN2", target_bir_lowering=False)
        self.P = Prog(self.nc)
        self.ps_i = 0
        self.ws_i = 0
        self.es = ExitStack()

    def din(self, name, shape):
        return self.nc.dram_tensor(name, list(shape), F32, kind="ExternalInput").ap()

    def dout(self, name, shape):
        return self.nc.dram_tensor(name, list(shape), F32, kind="ExternalOutput").ap()

    def sb(self, name, shape, dt=F32):
        return self.es.enter_context(self.nc.sbuf_tensor(name, list(shape), dt))

    def declare(self):
        T, L, NB = self.T, self.L, self.NB
        i = {}
        i["xp"] = self.din("xp", [T, D])
        i["xs"] = self.din("xs", [NB, D])
        i["memp"] = self.din("memp", [256, D])
        i["state_gla"] = self.din("state_gla", [L, NB, 4, 48, 96])
        i["state_rg_h"] = self.din("state_rg_h", [L, NB, 384])
        i["state_rg_conv"] = self.din("state_rg_conv", [L, NB, 3, 384])
        i["cache_swa_k"] = self.din("cache_swa_k", [L, NB, 128, 128])
        i["cache_swa_v"] = self.din("cache_swa_v", [L, NB, 128, 128])
        i["cache_mem_k"] = self.din("cache_mem_k", [L, NB, 256, 256])
        i["cache_mem_v"] = self.din("cache_mem_v", [L, NB, 256, 256])
        i["w_in"] = self.din("w_in", [L, D, IN_W])
        i["w_out"] = self.din("w_out", [L, D, D])
        i["mem_w_q"] = self.din("mem_w_q", [L, D, 256])
        i["mem_w_kv"] = self.din("mem_w_kv", [L, D, 512])
        i["mem_w_o"] = self.din("mem_w_o", [L, 256, D])
        i["ffn_w1"] = self.din("ffn_w1", [L, D, 4096])
        i["ffn_w2"] = self.din("ffn_w2", [L, 4096, D])
        i["gla_w_gate2"] = self.din("gla_w_gate2", [L, 16, 192])
        i["gla_b_gate"] = self.din("gla_b_gate", [L, 192])
        i["rg_w_a"] = self.din("rg_w_a", [L, 4, 96, 96])
        i["rg_w_x"] = self.din("rg_w_x", [L, 4, 96, 96])
        i["v_g4"] = self.din("v_g4", [128, 4, L, 8])
        i["v_bgT"] = self.din("v_bgT", [48, L, 4])
        i["v_gout"] = self.din("v_gout", [96, L])
        i["v_convw"] = self.din("v_convw", [96, L, 4, 4])
        i["v_rg4"] = self.din("v_rg4", [96, 4, L, 4])
        i["v_g64"] = self.din("v_g64", [64, 4, L])
        i["v_g64row"] = self.din("v_g64row", [4 * L * 64])
        i["v_sinks"] = self.din("v_sinks", [L * 4])
        for k, shp in CONST_SHAPES.items():
            i[k] = self.din(k, shp)
        self.i = i
        o = {}
        o["y_prompt"] = self.dout("y_prompt", [T, D])
        o["y_sample"] = self.dout("y_sample", [NB, D])
        o["gla_S_p"] = self.dout("gla_S_p", [L, 4, 48, 96])
        o["gla_S_s"] = self.dout("gla_S_s", [L, NB, 4, 48, 96])
        o["rg_h_p"] = self.dout("rg_h_p", [L, 384])
        o["rg_h_s"] = self.dout("rg_h_s", [L, NB, 384])
        o["rg_conv_p"] = self.dout("rg_conv_p", [L, 3, 384])
        o["rg_conv_s"] = self.dout("rg_conv_s", [L, NB, 3, 384])
        o["swa_k_p"] = self.dout("swa_k_p", [L, 128, 128])
        o["swa_k_s"] = self.dout("swa_k_s", [L, NB, 128, 128])
        o["swa_v_p"] = self.dout("swa_v_p", [L, 128, 128])
        o["swa_v_s"] = self.dout("swa_v_s", [L, NB, 128, 128])
        o["mem_k_p"] = self.dout("mem_k_p", [L, 256, 256])
        o["mem_v_p"] = self.dout("mem_v_p", [L, 256, 256])
        self.o = o
        self.scr_v = self.nc.dram_tensor("scr_v", [L, NB * 384], F32).ap()

    def ps(self):
        k = self.ps_i
        self.ps_i = (k + 1) % 8
        return self.psb[k], "ps%d" % k

    def act(self, fn, reads, writes):
        return self.P.op("act", fn, reads, writes)

    def dve(self, fn, reads, writes):
        return self.P.op("dve", fn, reads, writes)

    def pe(self, fn, reads, writes):
        return self.P.op("pe", fn, reads, writes)

    def load(self, out_ap, in_ap, writes, reads=(), q="sp"):
        return self.P.dma(q, lambda e: e.dma_start(out=out_ap, in_=in_ap), reads=reads, writes=writes)

    def store(self, out_ap, in_ap, reads, key, slow=False):
        if slow:
            fn = lambda e: e.dma_start(out=out_ap, in_=in_ap, allow_slow_non_contiguous=True)
        else:
            fn = lambda e: e.dma_start(out=out_ap, in_=in_ap)
        return self.P.dma("sp", fn, reads=reads, writes=[key], final=True)

    def dbg_store(self, name, ap, key, dt=F32):
        if not getattr(self, "debug", False):
            return
        shp = list(ap.shape)
        d = self.nc.dram_tensor(name, shp, F32, kind="ExternalOutput").ap()
        q = "sp" if dt == F32 else "pool"
        self.P.dma(q, lambda e: e.dma_start(out=d, in_=ap), reads=[key], writes=["dbg_" + name], final=True)

    def wload(self, dram_ap, parts, shape):
        s = self.ws_i
        self.ws_i = (s + 1) % len(self.wslots)
        n = 1
        for x in shape:
            n *= x
        assert n <= 4608
        v = self.wslots[s][0:parts, 0:n]
        if len(shape) == 2:
            v = v.rearrange("p (a b) -> p a b", a=shape[0])
        key = "wslot%d" % s
        self.P.dma("pool", lambda e: e.dma_start(out=v, in_=dram_ap), writes=[key])
        return v, key

    def mm_group(self, out_ap, pairs, reads, pkey):
        def fn(e):
            n = len(pairs)
            ins = None
            for i, (l, r) in enumerate(pairs):
                ins = e.matmul(out_ap, l, r, start=(i == 0), stop=(i == n - 1))
            return ins
        return self.pe(fn, reads, [pkey])

    def rstd_from_ps(self, out_ap, ps_ap, pkey, okey, dim):
        self.act(lambda e: e.activation(out_ap, ps_ap, AF.Sqrt, bias=self.eps_ap[0:out_ap.shape[0], :], scale=1.0 / dim),
                 [pkey, "consts"], [okey])
        self.dve(lambda e: e.reciprocal(out_ap, out_ap), [okey], [okey])

    def rmsnorm_fm(self, x_ap, xkeys, g_ap, out_ap, okeys, N):
        A = self.arena
        ntt = (N + 511) // 512
        for tt in range(ntt):
            n0 = tt * 512
            n = min(512, N - n0)
            m = A.mark()
            sq, sqk = A.alloc("sq", 128, [8, n], BF16)
            rs, rsk = A.alloc("rs", 128, [n], F32)
            xs_ = x_ap[:, :, n0:n0 + n]
            self.act(lambda e, sq=sq, xs_=xs_: e.activation(sq, xs_, AF.Square), xkeys[tt], [sqk])
            pst, pk = self.ps()
            self.mm_group(pst[:, 0:n], [(self.ones_bf[:, 0:128], sq[:, c, :]) for c in range(8)], [sqk, "consts"], pk)
            self.rstd_from_ps(rs, pst[:, 0:n], pk, rsk, float(D))
            for c in range(8):
                self.dve(lambda e, c=c, rs=rs, n0=n0, n=n: e.scalar_tensor_tensor(
                    out_ap[:, c, n0:n0 + n], x_ap[:, c, n0:n0 + n], g_ap[:, c:c + 1], rs, ALU.mult, ALU.mult),
                    list(xkeys[tt]) + [rsk, "consts"], [okeys[tt]])
            A.release(m)

    def setup(self):
        nc, P, L, NB = self.nc, self.P, self.L, self.NB
        i = self.i
        self.psb = [self.es.enter_context(nc.psum_tensor("psb%d" % k, [128, 512], F32)) for k in range(8)]
        self.wslots = [self.sb("wslot%d" % k, [128, 4608], BF16) for k in range(4)]
        self.x = self.sb("x", [128, 8, 1024], F32)
        self.xn = self.sb("xn", [128, 8, 1024], BF16)
        self.ident = self.sb("ident", [128, 128], F32)
        self.ones_bf = self.sb("ones_bf", [128, 128], BF16)
        self.ones_f = self.sb("ones_f", [128, 128], F32)
        self.eps_ap = self.sb("eps_ap", [128, 1], F32)
        self.u64 = self.sb("u64", [64, 64], F32)
        self.l64 = self.sb("l64", [64, 64], F32)
        self.mask64 = self.sb("mask64", [64, 64], F32)
        self.bt = self.sb("bt", [128, 2, 4, 128], F32)
        self.bt0 = self.sb("bt0", [128, 2, 4, 128], F32)
        self.alibi_dec = self.sb("alibi_dec", [128, 16, 4], F32)
        self.sel = self.sb("sel", [16, 128], F32)
        self.selT = self.sb("selT", [128, 16], F32)
        self.g4 = self.sb("g4", [128, 4, L, 8], F32)
        self.bgT = self.sb("bgT", [48, L, 4], F32)
        self.nbgT = self.sb("nbgT", [48, L, 4], F32)
        self.gout = self.sb("gout", [96, L], F32)
        self.convw = self.sb("convw", [96, L, 4, 4], F32)
        self.rg4 = self.sb("rg4", [96, 4, L, 4], F32)
        self.c8 = self.sb("c8", [96, L, 4], F32)
        self.c16 = self.sb("c16", [96, L, 4], F32)
        self.g64 = self.sb("g64", [64, 4, L], F32)
        self.g64row = self.sb("g64row", [128, 4, L, 64], F32)
        self.sinks = self.sb("sinks", [128, L, 4], F32)
        self.esink = self.sb("esink", [128, L, 4], F32)
        self.w2g = self.sb("w2g", [16, L, 192], BF16)
        self.bgrow = self.sb("bgrow", [1, L, 192], BF16)
        self.wa = self.sb("wa", [96, L, 4, 96], BF16)
        self.wx = self.sb("wx", [96, L, 4, 96], BF16)
        ld = self.load
        ld(self.ident[:], i["c_ident"], ["consts"])
        ld(self.u64[:], i["c_u64"], ["consts"])
        ld(self.l64[:], i["c_l64"], ["consts"])
        ld(self.mask64[:], i["c_mask64"], ["consts"])
        ld(self.bt[:], i["c_bt"], ["consts"])
        ld(self.bt0[:], i["c_bt0"], ["consts"])
        ld(self.alibi_dec[:], i["c_alibi_dec"], ["consts"])
        ld(self.sel[:], i["c_sel"], ["consts"])
        ld(self.selT[:], i["c_selT"], ["consts"])
        ld(self.g4[:], i["v_g4"], ["consts"])
        ld(self.bgT[:], i["v_bgT"], ["consts"])
        ld(self.gout[:], i["v_gout"], ["consts"])
        ld(self.convw[:], i["v_convw"], ["consts"])
        ld(self.rg4[:], i["v_rg4"], ["consts"])
        ld(self.g64[:], i["v_g64"], ["consts"])
        ld(self.g64row[:].rearrange("p a l d -> p (a l d)"), i["v_g64row"].partition_broadcast(128), ["consts"])
        ld(self.sinks[:].rearrange("p l h -> p (l h)"), i["v_sinks"].partition_broadcast(128), ["consts"])
        ld(self.w2g[:], i["gla_w_gate2"].rearrange("l r n -> r l n"), ["consts"], q="pool")
        ld(self.bgrow[:].rearrange("p l n -> p (l n)"), i["gla_b_gate"].rearrange("l n -> (l n)").partition_broadcast(1),
           ["consts"], q="pool")
        ld(self.wa[:], i["rg_w_a"].rearrange("l n c d -> c l n d"), ["consts"], q="pool")
        ld(self.wx[:], i["rg_w_x"].rearrange("l n c d -> c l n d"), ["consts"], q="pool")
        dv, ac = self.dve, self.act
        dv(lambda e: e.memset(self.ones_bf[:], 1.0), [], ["consts"])
        dv(lambda e: e.memset(self.ones_f[:], 1.0), [], ["consts"])
        dv(lambda e: e.memset(self.eps_ap[:], EPS), [], ["consts"])
        dv(lambda e: e.tensor_scalar(self.nbgT[:], self.bgT[:], -1.0, None, ALU.mult), ["consts"], ["consts"])
        ac(lambda e: e.activation(self.c8[:], self.rg4[:, 3], AF.Exp, scale=-1.0), ["consts"], ["consts"])
        ac(lambda e: e.activation(self.c8[:], self.c8[:], AF.Ln, bias=1.0, scale=1.0), ["consts"], ["consts"])
        dv(lambda e: e.tensor_scalar(self.c16[:], self.c8[:], -16.0, None, ALU.mult), ["consts"], ["consts"])
        dv(lambda e: e.tensor_scalar(self.c8[:], self.c8[:], -8.0, None, ALU.mult), ["consts"], ["consts"])
        ac(lambda e: e.activation(self.esink[:], self.sinks[:], AF.Exp), ["consts"], ["consts"])
        self.arena_t = self.sb("arena", [128, self.arena_n], F32)
        self.arena = Arena(P, self.arena_t, self.arena_n)

    def prompt_setup(self):
        L = self.L
        self.mkT = self.sb("mkT", [64, L, 4, 256], BF16)
        self.mv = self.sb("mv", [128, L, 2, 256], BF16)
        self.S = self.sb("S", [48, L, 4, 96], F32)
        self.hst = self.sb("hst", [96, L, 4], F32)
        self.convh = self.sb("convh", [96, L, 4, 3], F32)
        self.kTh = self.sb("kTh", [64, L, 2, 128], BF16)
        self.vh = self.sb("vh", [128, L, 128], BF16)
        dv = self.dve
        dv(lambda e: e.memset(self.S[:], 0.0), [], ["S%d" % l for l in range(L)])
        dv(lambda e: e.memset(self.hst[:], 0.0), [], ["hst%d" % l for l in range(L)])
        dv(lambda e: e.memset(self.convh[:], 0.0), [], ["convh%d" % l for l in range(L)])
        dv(lambda e: e.memset(self.kTh[:], 0.0), [], ["kTh%d" % l for l in range(L)])
        dv(lambda e: e.memset(self.vh[:], 0.0), [], ["vh%d" % l for l in range(L)])

    def tok_qknorm(self, src, skey, np_, H, grow, out, okey, tmpname):
        A = self.arena
        sq, sqk = A.alloc(tmpname + "sq", np_, [H, 64], F32)
        ss, ssk = A.alloc(tmpname + "ss", np_, [H], F32)
        self.dve(lambda e: e.tensor_tensor(sq, src, src, ALU.mult), [skey], [sqk])
        self.dve(lambda e: e.tensor_reduce(ss, sq, AX.X, ALU.add), [sqk], [ssk])
        self.act(lambda e: e.activation(ss, ss, AF.Sqrt, bias=self.eps_ap[0:np_, :], scale=1.0 / 64.0), [ssk, "consts"], [ssk])
        self.dve(lambda e: e.reciprocal(ss, ss), [ssk], [ssk])
        self.dve(lambda e: e.tensor_tensor(out, src, ss.unsqueeze(2).broadcast_to([np_, H, 64]), ALU.mult), [skey, ssk], [okey])
        self.dve(lambda e: e.tensor_tensor(out, out, grow.unsqueeze(1).broadcast_to([np_, H, 64]), ALU.mult), [okey, "consts"], [okey])

    def prompt_memkv(self):
        A, L, i = self.arena, self.L, self.i
        m0 = A.mark()
        mt, mtk = A.alloc("memtok", 128, [2, 1024], F32)
        self.load(mt, i["memp"].rearrange("(jb p) d -> p jb d", p=128), [mtk])
        mT, mTk = A.alloc("memT", 128, [8, 256], F32)
        for c in range(8):
            pst, pk = self.ps()
            for jb in range(2):
                self.pe(lambda e, c=c, jb=jb, pst=pst: e.transpose(pst[:, jb * 128:(jb + 1) * 128], mt[:, jb, c * 128:(c + 1) * 128], self.ident[:, :]),
                        [mtk, "consts"], [pk])
            self.act(lambda e, c=c, pst=pst: e.activation(mT[:, c, :], pst[:, 0:256], AF.Copy), [pk], [mTk])
        mn, mnk = A.alloc("memn", 128, [8, 256], BF16)
        for l in range(L):
            self.rmsnorm_fm(mT, [[mTk]], self.g4[:, 2, l, :], mn, [mnk], 256)
            wv, wk = self.wload(i["mem_w_kv"][l].rearrange("(kc p) n -> p kc n", p=128), 128, [8, 512])
            for jb in range(2):
                m1 = A.mark()
                pst, pk = self.ps()
                self.mm_group(pst[:, 0:512], [(mn[:, kc, jb * 128:(jb + 1) * 128], wv[:, kc, :]) for kc in range(8)], [mnk, wk], pk)
                kv, kvk = A.alloc("kvtok", 128, [512], F32)
                self.act(lambda e, pst=pst, kv=kv: e.activation(kv, pst[:, 0:512], AF.Copy), [pk], [kvk])
                kn, knk = A.alloc("kn", 128, [4, 64], F32)
                self.tok_qknorm(kv[:, 0:256].rearrange("p (h d) -> p h d", h=4), kvk, 128, 4, self.g64row[:, 3, l, :], kn, knk, "mk")
                self.store(self.o["mem_k_p"][l, jb * 128:(jb + 1) * 128, :], kn.rearrange("p h d -> p (h d)"), [knk], "o_memk%d_%d" % (l, jb))
                self.store(self.o["mem_v_p"][l, jb * 128:(jb + 1) * 128, :], kv[:, 256:512], [kvk], "o_memv%d_%d" % (l, jb))
                self.act(lambda e, l=l, jb=jb, kv=kv: e.activation(self.mv[:, l, jb, :], kv[:, 256:512], AF.Copy), [kvk], ["mv%d" % l])
                pst2, pk2 = self.ps()
                for h in range(4):
                    self.pe(lambda e, h=h, pst2=pst2, kn=kn: e.transpose(pst2[0:64, h * 128:(h + 1) * 128], kn[:, h, :], self.ident[:, :]),
                            [knk, "consts"], [pk2])
                self.act(lambda e, l=l, jb=jb, pst2=pst2: e.activation(
                    self.mkT[:, l, :, jb * 128:(jb + 1) * 128], pst2[0:64, :].rearrange("p (h j) -> p h j", h=4), AF.Copy),
                    [pk2], ["mkT%d" % l])
                A.release(m1)
        A.release(m0)

    def gla_half(self, l, hf, w1, w1k, w3, w3k, w2, w2k, last_st):
        A, P = self.arena, self.P
        t0 = hf * 512
        xk = "xn%d" % hf
        xn = self.xn
        m0 = A.mark()
        glr, glrk = A.alloc("glrT", 16, [512], BF16)
        pst, pk = self.ps()
        self.mm_group(pst[0:16, :], [(w3[:, kc, 0:16], xn[:, kc, t0:t0 + 512]) for kc in range(8)], [w3k, xk], pk)
        self.act(lambda e, pst=pst: e.activation(glr, pst[0:16, :], AF.Copy), [pk], [glrk])
        sg, sgk = A.alloc("sg", 96, [4, 512], BF16)
        for h in range(4):
            pst, pk = self.ps()
            self.mm_group(pst[0:96, :], [(w3[:, kc, 16 + h * 96:16 + (h + 1) * 96], xn[:, kc, t0:t0 + 512]) for kc in range(8)], [w3k, xk], pk)
            self.act(lambda e, pst=pst, h=h: e.activation(sg[:, h, :], pst[0:96, :], AF.Silu), [pk], [sgk])
        sp, spk = A.alloc("sp", 64, [8, 192], F32)
        for cp in range(4):
            pst, pk = self.ps()
            for q in range(2):
                c = cp * 2 + q
                self.mm_group(pst[0:64, q * 192:(q + 1) * 192],
                              [(glr[:, c * 64:(c + 1) * 64], self.w2g[:, l, :]), (self.ones_bf[0:1, 0:64], self.bgrow[0:1, l, :])],
                              [glrk, "consts"], pk)
            spv = sp[:, cp * 2:cp * 2 + 2, :].rearrange("p a b -> p (a b)")
            self.act(lambda e, pst=pst, spv=spv: e.activation(spv, pst[0:64, 0:384], AF.Exp, scale=-1.0), [pk], [spk])
            self.act(lambda e, spv=spv: e.activation(spv, spv, AF.Ln, bias=1.0, scale=1.0), [spk], [spk])
        kh, khk = A.alloc("khat", 64, [8, 192], BF16)
        vt, vtk = A.alloc("vtok", 64, [8, 384], BF16)
        for c in range(8):
            pst, pk = self.ps()
            self.mm_group(pst[0:64, 0:192], [(self.l64[:, :], sp[:, c, :])], [spk, "consts"], pk)
            m1 = A.mark()
            ek, ekk = A.alloc("ek", 64, [192], F32)
            self.act(lambda e, pst=pst, ek=ek: e.activation(ek, pst[0:64, 0:192], AF.Exp), [pk], [ekk])
            pst2, pk2 = self.ps()
            self.mm_group(pst2[0:64, 0:192], [(xn[:, kc, t0 + c * 64:t0 + (c + 1) * 64], w2[:, kc, 0:192]) for kc in range(8)], [w2k, xk], pk2)
            self.dve(lambda e, c=c, pst2=pst2, ek=ek: e.tensor_tensor(kh[:, c, :], pst2[0:64, 0:192], ek, ALU.mult), [pk2, ekk], [khk])
            pst3, pk3 = self.ps()
            self.mm_group(pst3[0:64, 0:384], [(xn[:, kc, t0 + c * 64:t0 + (c + 1) * 64], w2[:, kc, 192:576]) for kc in range(8)], [w2k, xk], pk3)
            self.act(lambda e, c=c, pst3=pst3: e.activation(vt[:, c, :], pst3[0:64, 0:384], AF.Copy), [pk3], [vtk])
            A.release(m1)
        qt, qtk = A.alloc("qtil", 48, [4, 512], BF16)
        kt, ktk = A.alloc("ktil", 48, [4, 512], BF16)
        Dk, Dkk = A.alloc("Dk", 48, [4, 8], F32)
        for h in range(4):
            m1 = A.mark()
            pst, pk = self.ps()
            for c in range(8):
                self.mm_group(pst[0:48, c * 64:(c + 1) * 64], [(sp[:, c, h * 48:(h + 1) * 48], self.u64[:, :])], [spk, "consts"], pk)
            ep, epk = A.alloc("Ep", 48, [512], F32)
            em, emk = A.alloc("Em", 48, [512], F32)
            self.act(lambda e, pst=pst, ep=ep: e.activation(ep, pst[0:48, :], AF.Exp), [pk], [epk])
            self.act(lambda e, pst=pst, em=em: e.activation(em, pst[0:48, :], AF.Exp, scale=-1.0), [pk], [emk])
            self.act(lambda e, ep=ep, h=h: e.activation(Dk[:, h, :], ep.rearrange("p (c t) -> p c t", c=8)[:, :, 63], AF.Copy), [epk], [Dkk])
            pq, pqk = self.ps()
            self.mm_group(pq[0:48, :], [(w1[:, kc, h * 48:(h + 1) * 48], xn[:, kc, t0:t0 + 512]) for kc in range(8)], [w1k, xk], pqk)
            self.dve(lambda e, pq=pq, ep=ep, h=h: e.tensor_tensor(qt[:, h, :], pq[0:48, :], ep, ALU.mult), [pqk, epk], [qtk])
            pk_, pkk = self.ps()
            self.mm_group(pk_[0:48, :], [(w1[:, kc, 192 + h * 48:192 + (h + 1) * 48], xn[:, kc, t0:t0 + 512]) for kc in range(8)], [w1k, xk], pkk)
            self.dve(lambda e, pk_=pk_, em=em, h=h: e.tensor_tensor(kt[:, h, :], pk_[0:48, :], em, ALU.mult), [pkk, emk], [ktk])
            A.release(m1)
        oraw, ork = A.alloc("oraw", 96, [4, 512], F32)
        Sk = "S%d" % l
        S = self.S[:, l]
        for c in range(8):
            m1 = A.mark()
            sbf, sbk = A.alloc("Sbf", 48, [4, 96], BF16)
            self.act(lambda e, sbf=sbf: e.activation(sbf, S, AF.Copy, scale=GLA_SCALE), [Sk], [sbk])
            pa, pak = self.ps()
            for h in range(4):
                self.mm_group(pa[0:64, h * 64:(h + 1) * 64], [(kt[:, h, c * 64:(c + 1) * 64], qt[:, h, c * 64:(c + 1) * 64])], [ktk, qtk], pak)
            atm, atk = A.alloc("ATm", 64, [4, 64], BF16)
            self.dve(lambda e, pa=pa, atm=atm: e.tensor_tensor(
                atm, pa[0:64, 0:256].rearrange("p (h t) -> p h t", h=4), self.mask64[:, :].unsqueeze(1).broadcast_to([64, 4, 64]), ALU.mult),
                [pak, "consts"], [atk])
            po, pok = self.ps()
            for h in range(4):
                self.mm_group(po[0:96, h * 64:(h + 1) * 64],
                              [(vt[:, c, h * 96:(h + 1) * 96], atm[:, h, :]), (sbf[:, h, :], qt[:, h, c * 64:(c + 1) * 64])],
                              [vtk, atk, sbk, qtk], pok)
            self.act(lambda e, po=po, c=c: e.activation(oraw[:, :, c * 64:(c + 1) * 64], po[0:96, 0:256].rearrange("p (h t) -> p h t", h=4), AF.Copy),
                     [pok], [ork])
            psu, psuk = self.ps()
            for h in range(4):
                self.mm_group(psu[0:48, h * 96:(h + 1) * 96], [(kh[:, c, h * 48:(h + 1) * 48], vt[:, c, h * 96:(h + 1) * 96])], [khk, vtk], psuk)
            self.dve(lambda e, c=c: e.tensor_tensor(S, S, Dk[:, :, c:c + 1].broadcast_to([48, 4, 96]), ALU.mult), [Sk, Dkk], [Sk])
            self.dve(lambda e, psu=psu: e.tensor_tensor(S, S, psu[0:48, 0:384].rearrange("p (h v) -> p h v", h=4), ALU.add), [Sk, psuk], [Sk])
            A.release(m1)
        self.dbg_store("dbg_oraw%d" % hf, oraw, ork)
        self.dbg_store("dbg_qt%d" % hf, qt, qtk, BF16)
        self.dbg_store("dbg_kt%d" % hf, kt, ktk, BF16)
        self.dbg_store("dbg_vt%d" % hf, vt, vtk, BF16)
        self.dbg_store("dbg_kh%d" % hf, kh, khk, BF16)
        self.dbg_store("dbg_Dk%d" % hf, Dk, Dkk)
        if last_st and hf == 1:
            self.store(self.o["gla_S_p"][l].rearrange("h k v -> k h v"), S, [Sk], "o_glaS%d" % l)
        sqo, sqok = A.alloc("sqo", 96, [4, 512], BF16)
        self.act(lambda e: e.activation(sqo, oraw, AF.Square), [ork], [sqok])
        for h in range(4):
            m1 = A.mark()
            pst, pk = self.ps()
            self.mm_group(pst[0:96, :], [(self.ones_bf[0:96, 0:96], sqo[:, h, :])], [sqok, "consts"], pk)
            rs, rsk = A.alloc("rso", 96, [512], F32)
            self.rstd_from_ps(rs, pst[0:96, :], pk, rsk, 96.0)
            tmp, tmpk = A.alloc("otmp", 96, [512], F32)
            self.dve(lambda e, h=h, rs=rs, tmp=tmp: e.scalar_tensor_tensor(tmp, oraw[:, h, :], self.gout[:, l:l + 1], rs, ALU.mult, ALU.mult),
                     [ork, rsk, "consts"], [tmpk])
            self.dve(lambda e, h=h, tmp=tmp, mg=self.mixg: e.tensor_tensor(mg[:, h, t0:t0 + 512], tmp, sg[:, h, :], ALU.mult), [tmpk, sgk], [self.mixgk])
            if h == 0:
                self.dbg_store("dbg_tmp%d" % hf, tmp, tmpk)
                self.dbg_store("dbg_rs%d" % hf, rs, rsk)
            A.release(m1)
        self.dbg_store("dbg_sg%d" % hf, sg, sgk, BF16)
        self.dbg_store("dbg_mixg%d" % hf, self.mixg[:, :, t0:t0 + 512], self.mixgk, BF16)
        A.release(m0)

    def rg_phase(self, l, w4, w4k, w5, w5k, last_st):
        A = self.arena
        xn = self.xn
        m0 = A.mark()
        for n in range(4):
            m1 = A.mark()
            rx, rxk = A.alloc("rxT", 96, [1027], F32)
            gy, gyk = A.alloc("gy", 96, [1024], BF16)
            self.act(lambda e, n=n, rx=rx: e.activation(rx[:, 0:3], self.convh[:, l, n, :], AF.Copy), ["convh%d" % l], [rxk])
            for tt in range(2):
                pst, pk = self.ps()
                self.mm_group(pst[0:96, :], [(w4[:, kc, n * 96:(n + 1) * 96], xn[:, kc, tt * 512:(tt + 1) * 512]) for kc in range(8)], [w4k, "xn%d" % tt], pk)
                self.act(lambda e, pst=pst, tt=tt, rx=rx: e.activation(rx[:, 3 + tt * 512:3 + (tt + 1) * 512], pst[0:96, :], AF.Copy), [pk], [rxk])
                pst2, pk2 = self.ps()
                self.mm_group(pst2[0:96, :], [(w5[:, kc, n * 96:(n + 1) * 96], xn[:, kc, tt * 512:(tt + 1) * 512]) for kc in range(8)], [w5k, "xn%d" % tt], pk2)
                self.act(lambda e, pst2=pst2, tt=tt, gy=gy: e.activation(gy[:, tt * 512:(tt + 1) * 512], pst2[0:96, :], AF.Gelu_apprx_tanh), [pk2], [gyk])
            self.act(lambda e, n=n, rx=rx: e.activation(self.convh[:, l, n, :], rx[:, 1024:1027], AF.Copy), [rxk], ["convh%d" % l])
            xc, xck = A.alloc("xc", 96, [1024], F32)
            cw = self.convw
            self.dve(lambda e, n=n, rx=rx, xc=xc: e.tensor_scalar(xc, rx[:, 0:1024], cw[:, l, n, 0:1], self.rg4[:, 0, l, n:n + 1], ALU.mult, ALU.add),
                     [rxk, "consts"], [xck])
            for j in range(1, 4):
                self.dve(lambda e, n=n, j=j, rx=rx, xc=xc: e.scalar_tensor_tensor(xc, rx[:, j:j + 1024], cw[:, l, n, j:j + 1], xc, ALU.mult, ALU.add),
                         [rxk, xck, "consts"], [xck])
            xcb, xcbk = A.alloc("xcb", 96, [1024], BF16)
            self.act(lambda e, xc=xc, xcb=xcb: e.activation(xcb, xc, AF.Copy), [xck], [xcbk])
            r, rk = A.alloc("r", 96, [1024], F32)
            ig, igk = A.alloc("ig", 96, [1024], F32)
            for tt in range(2):
                pst, pk = self.ps()
                self.mm_group(pst[0:96, :], [(self.wa[:, l, n, :], xcb[:, tt * 512:(tt + 1) * 512])], [xcbk, "consts"], pk)
                self.act(lambda e, pst=pst, tt=tt, r=r, n=n: e.activation(r[:, tt * 512:(tt + 1) * 512], pst[0:96, :], AF.Sigmoid, bias=self.rg4[:, 1, l, n:n + 1]),
                         [pk, "consts"], [rk])
                pst2, pk2 = self.ps()
                self.mm_group(pst2[0:96, :], [(self.wx[:, l, n, :], xcb[:, tt * 512:(tt + 1) * 512])], [xcbk, "consts"], pk2)
                self.act(lambda e, pst2=pst2, tt=tt, ig=ig, n=n: e.activation(ig[:, tt * 512:(tt + 1) * 512], pst2[0:96, :], AF.Sigmoid, bias=self.rg4[:, 2, l, n:n + 1]),
                         [pk2, "consts"], [igk])
            a, ak = A.alloc("a", 96, [1024], F32)
            s2, s2k = A.alloc("s2", 96, [1024], F32)
            self.act(lambda e, a=a, r=r, n=n: e.activation(a, r, AF.Exp, scale=self.c8[:, l, n:n + 1]), [rk, "consts"], [ak])
            self.act(lambda e, s2=s2, r=r, n=n: e.activation(s2, r, AF.Exp, scale=self.c16[:, l, n:n + 1]), [rk, "consts"], [s2k])
            self.act(lambda e, s2=s2: e.activation(s2, s2, AF.Sqrt, bias=1.0, scale=-1.0), [s2k], [s2k])
            self.dve(lambda e, ig=ig, xc=xc: e.tensor_tensor(ig, ig, xc, ALU.mult), [igk, xck], [igk])
            self.dve(lambda e, ig=ig, s2=s2: e.tensor_tensor(ig, ig, s2, ALU.mult), [igk, s2k], [igk])
            hh, hk = A.alloc("h", 96, [1024], F32)
            hs = "hst%d" % l
            self.dve(lambda e, hh=hh, a=a, ig=ig, n=n: e.tensor_tensor_scan(hh, a, ig, self.hst[:, l, n:n + 1], ALU.mult, ALU.add),
                     [ak, igk, hs], [hk])
            self.act(lambda e, hh=hh, n=n: e.activation(self.hst[:, l, n:n + 1], hh[:, 1023:1024], AF.Copy), [hk], [hs])
            self.dve(lambda e, hh=hh, gy=gy, n=n, mr=self.mixr: e.tensor_tensor(mr[:, n, :], hh, gy, ALU.mult), [hk, gyk], [self.mixrk])
            if last_st:
                self.store(self.o["rg_conv_p"][l, :, n * 96:(n + 1) * 96].rearrange("j c -> c j"), rx[:, 1024:1027], [rxk], "o_rgconv%d_%d" % (l, n), slow=True)
                self.store(self.o["rg_h_p"][l, n * 96:(n + 1) * 96].rearrange("(c o) -> c o", o=1), hh[:, 1023:1024], [hk], "o_rgh%d_%d" % (l, n), slow=True)
            A.release(m1)
        A.release(m0)

    def fm_qknorm(self, pst, pk, n, gain_ap, out_ap, okey, tag):
        A = self.arena
        m = A.mark()
        raw, rawk = A.alloc(tag + "raw", 64, [n], F32)
        sq, sqk = A.alloc(tag + "sq", 64, [n], BF16)
        self.act(lambda e: e.activation(raw, pst[0:64, 0:n], AF.Copy), [pk], [rawk])
        self.act(lambda e: e.activation(sq, pst[0:64, 0:n], AF.Square), [pk], [sqk])
        p2, p2k = self.ps()
        self.mm_group(p2[0:64, 0:n], [(self.ones_bf[0:64, 0:64], sq)], [sqk, "consts"], p2k)
        rs, rsk = A.alloc(tag + "rs", 64, [n], F32)
        self.rstd_from_ps(rs, p2[0:64, 0:n], p2k, rsk, 64.0)
        self.dve(lambda e: e.scalar_tensor_tensor(out_ap, raw, gain_ap, rs, ALU.mult, ALU.mult), [rawk, rsk, "consts"], [okey])
        A.release(m)

    def swa_phase(self, l, st, w6, w6k, last_st):
        A = self.arena
        xn = self.xn
        m0 = A.mark()
        qT, qTk = A.alloc("sqT", 64, [4, 1024], BF16)
        kT, kTk = A.alloc("skT", 64, [2, 1152], BF16)
        vt, vtk = A.alloc("svt", 128, [9, 128], BF16)
        self.act(lambda e: e.activation(kT[:, :, 0:128], self.kTh[:, l], AF.Copy), ["kTh%d" % l], [kTk])
        self.act(lambda e: e.activation(vt[:, 0, :], self.vh[:, l, :], AF.Copy), ["vh%d" % l], [vtk])
        for tt in range(2):
            for h in range(4):
                pst, pk = self.ps()
                self.mm_group(pst[0:64, :], [(w6[:, kc, h * 64:(h + 1) * 64], xn[:, kc, tt * 512:(tt + 1) * 512]) for kc in range(8)], [w6k, "xn%d" % tt], pk)
                self.fm_qknorm(pst, pk, 512, self.g64[:, 0, l:l + 1], qT[:, h, tt * 512:(tt + 1) * 512], qTk, "sq")
            for kv in range(2):
                pst, pk = self.ps()
                self.mm_group(pst[0:64, :], [(w6[:, kc, 256 + kv * 64:256 + (kv + 1) * 64], xn[:, kc, tt * 512:(tt + 1) * 512]) for kc in range(8)], [w6k, "xn%d" % tt], pk)
                self.fm_qknorm(pst, pk, 512, self.g64[:, 1, l:l + 1], kT[:, kv, 128 + tt * 512:128 + (tt + 1) * 512], kTk, "sk")
        for i in range(8):
            pst, pk = self.ps()
            self.mm_group(pst[:, 0:128], [(xn[:, kc, i * 128:(i + 1) * 128], w6[:, kc, 384:512]) for kc in range(8)], [w6k, "xn%d" % (i // 4)], pk)
            self.act(lambda e, pst=pst, i=i: e.activation(vt[:, 1 + i, :], pst[:, 0:128], AF.Copy), [pk], [vtk])
        if last_st:
            m1 = A.mark()
            pst, pk = self.ps()
            self.mm_group(pst[:, 0:256], [(xn[:, kc, 896:1024], w6[:, kc, 256:512]) for kc in range(8)], [w6k, "xn1"], pk)
            kvr, kvrk = A.alloc("kvr", 128, [256], F32)
            self.act(lambda e, pst=pst: e.activation(kvr, pst[:, 0:256], AF.Copy), [pk], [kvrk])
            kn, knk = A.alloc("knr", 128, [2, 64], F32)
            self.tok_qknorm(kvr[:, 0:128].rearrange("p (h d) -> p h d", h=2), kvrk, 128, 2, self.g64row[:, 1, l, :], kn, knk, "swk")
            self.store(self.o["swa_k_p"][l], kn.rearrange("p h d -> p (h d)"), [knk], "o_swak%d" % l)
            self.store(self.o["swa_v_p"][l], kvr[:, 128:256], [kvrk], "o_swav%d" % l)
            A.release(m1)
        self.act(lambda e: e.activation(self.kTh[:, l], kT[:, :, 1024:1152], AF.Copy), [kTk], ["kTh%d" % l])
        self.act(lambda e: e.activation(self.vh[:, l, :], vt[:, 8, :], AF.Copy), [vtk], ["vh%d" % l])
        esk = self.esink
        for i in range(8):
            m1 = A.mark()
            bt = self.bt0 if (st == 0 and i == 0) else self.bt
            pa, pak = self.ps()
            pb, pbk = self.ps()
            for h in range(4):
                kv = h // 2
                self.mm_group(pa[:, h * 128:(h + 1) * 128], [(kT[:, kv, i * 128:(i + 1) * 128], qT[:, h, i * 128:(i + 1) * 128])], [kTk, qTk], pak)
                self.mm_group(pb[:, h * 128:(h + 1) * 128], [(kT[:, kv, (i + 1) * 128:(i + 2) * 128], qT[:, h, i * 128:(i + 1) * 128])], [kTk, qTk], pbk)
            sb_, sbk = A.alloc("ssb", 128, [2, 512], F32)
            self.dve(lambda e, pa=pa, sb_=sb_, bt=bt: e.scalar_tensor_tensor(sb_[:, 0, :], pa[:, :], 0.125, bt[:, 0].rearrange("p h t -> p (h t)"), ALU.mult, ALU.add), [pak, "consts"], [sbk])
            self.dve(lambda e, pb=pb, sb_=sb_, bt=bt: e.scalar_tensor_tensor(sb_[:, 1, :], pb[:, :], 0.125, bt[:, 1].rearrange("p h t -> p (h t)"), ALU.mult, ALU.add), [pbk, "consts"], [sbk])
            pt, ptk = A.alloc("sPT", 128, [2, 4, 128], BF16)
            self.act(lambda e, sb_=sb_, pt=pt: e.activation(pt.rearrange("p a h t -> p (a h t)"), sb_.rearrange("p a n -> p (a n)"), AF.Exp), [sbk], [ptk])
            po, pok = self.ps()
            pd, pdk = self.ps()
            for h in range(4):
                kv = h // 2
                self.mm_group(po[0:64, h * 128:(h + 1) * 128],
                              [(vt[:, i, kv * 64:(kv + 1) * 64], pt[:, 0, h, :]), (vt[:, i + 1, kv * 64:(kv + 1) * 64], pt[:, 1, h, :])], [vtk, ptk], pok)
                self.mm_group(pd[0:64, h * 128:(h + 1) * 128],
                              [(self.ones_bf[:, 0:64], pt[:, 0, h, :]), (self.ones_bf[:, 0:64], pt[:, 1, h, :])], [ptk, "consts"], pdk)
            dn, dnk = A.alloc("sdn", 64, [4, 128], F32)
            self.dve(lambda e, pd=pd, dn=dn: e.tensor_tensor(dn, pd[0:64, :].rearrange("p (h t) -> p h t", h=4),
                                                          esk[0:64, l, :].unsqueeze(2).broadcast_to([64, 4, 128]), ALU.add), [pdk, "consts"], [dnk])
            self.dve(lambda e, dn=dn: e.reciprocal(dn, dn), [dnk], [dnk])
            self.dve(lambda e, po=po, dn=dn, i=i, ms=self.mixs: e.tensor_tensor(ms[:, :, i * 128:(i + 1) * 128], po[0:64, :].rearrange("p (h t) -> p h t", h=4), dn, ALU.mult),
                     [pok, dnk], [self.mixsk])
            A.release(m1)
        A.release(m0)

    def proj_residual(self, groups, n_tok_tiles, N, xt=None, xkf=None):
        if xt is None:
            xt = self.x
            xkf = lambda mc, tt: "x_%d_%d" % (mc, tt)
        for tt in range(n_tok_tiles):
            n0 = tt * 512
            n = min(512, N - n0)
            for mc in range(8):
                pst, pk = self.ps()
                pairs = []
                reads = []
                for (w, wk, r, rk, nk) in groups:
                    for k in range(nk):
                        pairs.append((w[:, k, mc * 128:(mc + 1) * 128], r[:, k, n0:n0 + n]))
                    reads += [wk, rk]
                self.mm_group(pst[:, 0:n], pairs, reads, pk)
                xk = xkf(mc, tt)
                self.dve(lambda e, pst=pst, mc=mc, n0=n0, n=n: e.tensor_tensor(xt[:, mc, n0:n0 + n], xt[:, mc, n0:n0 + n], pst[:, 0:n], ALU.add),
                         [pk, xk], [xk])

    def xkeys(self, ntt):
        return [["x_%d_%d" % (mc, tt) for mc in range(8)] for tt in range(ntt)]

    def prompt_layer(self, st, l, last_st):
        A, i = self.arena, self.i
        xn = self.xn
        self.rmsnorm_fm(self.x, self.xkeys(2), self.g4[:, 0, l, :], xn, ["xn0", "xn1"], 1024)
        m0 = A.mark()
        self.mixg, self.mixgk = A.alloc("mixg", 96, [4, 1024], BF16)
        self.mixr, self.mixrk = A.alloc("mixr", 96, [4, 1024], BF16)
        self.mixs, self.mixsk = A.alloc("mixs", 64, [4, 1024], BF16)
        win = i["w_in"][l].rearrange("(kc p) n -> p kc n", p=128)
        w1, w1k = self.wload(win[:, :, C_GQ:C_GQ + 384], 128, [8, 384])
        w2, w2k = self.wload(win[:, :, C_GK:C_GK + 576], 128, [8, 576])
        w3, w3k = self.wload(win[:, :, C_GLR:C_GLR + 400], 128, [8, 400])
        for hf in range(2):
            self.gla_half(l, hf, w1, w1k, w3, w3k, w2, w2k, last_st)
        w4, w4k = self.wload(win[:, :, C_RX:C_RX + 384], 128, [8, 384])
        w5, w5k = self.wload(win[:, :, C_RY:C_RY + 384], 128, [8, 384])
        self.rg_phase(l, w4, w4k, w5, w5k, last_st)
        w6, w6k = self.wload(win[:, :, C_SQ:C_SQ + 512], 128, [8, 512])
        self.swa_phase(l, st, w6, w6k, last_st)
        wo = i["w_out"][l]
        wog, wogk = self.wload(wo[0:384, :].rearrange("(k p) n -> p k n", p=96), 96, [4, 1024])
        wor, work = self.wload(wo[384:768, :].rearrange("(k p) n -> p k n", p=96), 96, [4, 1024])
        wos, wosk = self.wload(wo[768:1024, :].rearrange("(k p) n -> p k n", p=64), 64, [4, 1024])
        self.dbg_store("dbg_mixg_late", self.mixg, self.mixgk, BF16)
        self.dbg_store("dbg_wog", wog, wogk, BF16)
        self.proj_residual([(wog, wogk, self.mixg, self.mixgk, 4), (wor, work, self.mixr, self.mixrk, 4), (wos, wosk, self.mixs, self.mixsk, 4)], 2, 1024)
        A.release(m0)
        self.rmsnorm_fm(self.x, self.xkeys(2), self.g4[:, 1, l, :], xn, ["xn0", "xn1"], 1024)
        m0 = A.mark()
        wq, wqk = self.wload(i["mem_w_q"][l].rearrange("(kc p) n -> p kc n", p=128), 128, [8, 256])
        wmo, wmok = self.wload(i["mem_w_o"][l].rearrange("(k p) n -> p k n", p=64), 64, [4, 1024])
        qm, qmk = A.alloc("qmT", 64, [4, 1024], BF16)
        om, omk = A.alloc("omT", 64, [4, 1024], BF16)
        for tt in range(2):
            for h in range(4):
                pst, pk = self.ps()
                self.mm_group(pst[0:64, :], [(wq[:, kc, h * 64:(h + 1) * 64], xn[:, kc, tt * 512:(tt + 1) * 512]) for kc in range(8)], [wqk, "xn%d" % tt], pk)
                self.fm_qknorm(pst, pk, 512, self.g64[:, 2, l:l + 1], qm[:, h, tt * 512:(tt + 1) * 512], qmk, "mq")
        for tt in range(2):
            for h in range(4):
                m1 = A.mark()
                pm, pmk = A.alloc("PmT", 128, [2, 512], BF16)
                for jb in range(2):
                    pst, pk = self.ps()
                    self.mm_group(pst[:, :], [(self.mkT[:, l, h, jb * 128:(jb + 1) * 128], qm[:, h, tt * 512:(tt + 1) * 512])], ["mkT%d" % l, qmk], pk)
                    self.act(lambda e, pst=pst, pm=pm, jb=jb: e.activation(pm[:, jb, :], pst[:, :], AF.Exp, scale=0.125), [pk], [pmk])
                po, pok = self.ps()
                self.mm_group(po[0:64, :], [(self.mv[:, l, jb, h * 64:(h + 1) * 64], pm[:, jb, :]) for jb in range(2)], ["mv%d" % l, pmk], pok)
                pd, pdk = self.ps()
                self.mm_group(pd[0:64, :], [(self.ones_bf[:, 0:64], pm[:, jb, :]) for jb in range(2)], ["consts", pmk], pdk)
                rc, rck = A.alloc("mrc", 64, [512], F32)
                self.dve(lambda e, pd=pd, rc=rc: e.reciprocal(rc, pd[0:64, :]), [pdk], [rck])
                self.dve(lambda e, po=po, rc=rc, h=h, tt=tt: e.tensor_tensor(om[:, h, tt * 512:(tt + 1) * 512], po[0:64, :], rc, ALU.mult), [pok, rck], [omk])
                A.release(m1)
        self.proj_residual([(wmo, wmok, om, omk, 4)], 2, 1024)
        A.release(m0)
        self.rmsnorm_fm(self.x, self.xkeys(2), self.g4[:, 3, l, :], xn, ["xn0", "xn1"], 1024)
        m0 = A.mark()
        hT, hTk0 = A.alloc("hT", 128, [16, 1024], BF16)
        w1d = i["ffn_w1"][l].rearrange("(kc p) n -> p kc n", p=128)
        w2d = i["ffn_w2"][l]
        for fc in range(2):
            hk = [hTk0 + "_%d" % tt for tt in range(2)]
            if fc == 0:
                for tt in range(2):
                    self.P.inherit(hk[tt], self.P.all_users(hTk0))
            for p4 in range(4):
                wp, wpk = self.wload(w1d[:, :, fc * 2048 + p4 * 512:fc * 2048 + (p4 + 1) * 512], 128, [8, 512])
                for tt in range(2):
                    for s4 in range(4):
                        sub = p4 * 4 + s4
                        m1 = A.mark()
                        pst, pk = self.ps()
                        self.mm_group(pst[:, :], [(wp[:, kc, s4 * 128:(s4 + 1) * 128], xn[:, kc, tt * 512:(tt + 1) * 512]) for kc in range(8)], [wpk, "xn%d" % tt], pk)
                        sq, sqk = A.alloc("fsq", 128, [512], F32)
                        self.act(lambda e, pst=pst, sq=sq: e.activation(sq, pst[:, :], AF.Square), [pk], [sqk])
                        self.dve(lambda e, pst=pst, sq=sq, sub=sub, tt=tt: e.scalar_tensor_tensor(
                            hT[:, sub, tt * 512:(tt + 1) * 512], pst[:, :], 0.0, sq, ALU.is_gt, ALU.mult), [pk, sqk], [hk[tt]])
                        A.release(m1)
            w2p = []
            for p4 in range(4):
                r0 = fc * 2048 + p4 * 512
                w2p.append(self.wload(w2d[r0:r0 + 512, :].rearrange("(k p) n -> p k n", p=128), 128, [4, 1024]))
            for tt in range(2):
                for mc in range(8):
                    pst, pk = self.ps()
                    pairs = []
                    for p4 in range(4):
                        for k in range(4):
                            pairs.append((w2p[p4][0][:, k, mc * 128:(mc + 1) * 128], hT[:, p4 * 4 + k, tt * 512:(tt + 1) * 512]))
                    self.mm_group(pst[:, :], pairs, [w2p[p4][1] for p4 in range(4)] + [hk[tt]], pk)
                    xk = "x_%d_%d" % (mc, tt)
                    self.dve(lambda e, pst=pst, mc=mc, tt=tt: e.tensor_tensor(
                        self.x[:, mc, tt * 512:(tt + 1) * 512], self.x[:, mc, tt * 512:(tt + 1) * 512], pst[:, :], ALU.add), [pk, xk], [xk])
        for tt in range(2):
            self.P.inherit(hTk0, self.P.all_users(hTk0 + "_%d" % tt))
        A.release(m0)

    def prompt(self):
        T, L = self.T, self.L
        nst = T // 1024
        self.prompt_setup()
        self.prompt_memkv()
        xkeys_all = [k for ks in self.xkeys(2) for k in ks]
        for st in range(nst):
            A = self.arena
            for i8 in range(8):
                m = A.mark()
                xt, xtk = A.alloc("xtok", 128, [1024], F32)
                self.load(xt, self.i["xp"][st * 1024 + i8 * 128: st * 1024 + (i8 + 1) * 128, :], [xtk])
                for c2 in range(2):
                    pst, pk = self.ps()
                    for q in range(4):
                        c = c2 * 4 + q
                        self.pe(lambda e, pst=pst, q=q, c=c, xt=xt: e.transpose(pst[:, q * 128:(q + 1) * 128], xt[:, c * 128:(c + 1) * 128], self.ident[:, :]),
                                [xtk, "consts"], [pk])
                    tt = i8 // 4
                    self.act(lambda e, pst=pst, c2=c2, i8=i8: e.activation(
                        self.x[:, c2 * 4:(c2 + 1) * 4, i8 * 128:(i8 + 1) * 128], pst[:, :].rearrange("p (q t) -> p q t", q=4), AF.Copy),
                        [pk], ["x_%d_%d" % (c2 * 4 + q, tt) for q in range(4)])
                A.release(m)
            for l in range(L):
                self.prompt_layer(st, l, st == nst - 1)
            for i8 in range(8):
                m = A.mark()
                yt, ytk = A.alloc("ytok", 128, [1024], F32)
                tt = i8 // 4
                for c2 in range(2):
                    pst, pk = self.ps()
                    for q in range(4):
                        c = c2 * 4 + q
                        self.pe(lambda e, pst=pst, q=q, c=c, i8=i8: e.transpose(pst[:, q * 128:(q + 1) * 128], self.x[:, c, i8 * 128:(i8 + 1) * 128], self.ident[:, :]),
                                ["x_%d_%d" % (c, tt), "consts"], [pk])
                    self.act(lambda e, pst=pst, c2=c2, yt=yt: e.activation(yt[:, c2 * 512:(c2 + 1) * 512], pst[:, :], AF.Copy), [pk], [ytk])
                self.store(self.o["y_prompt"][st * 1024 + i8 * 128: st * 1024 + (i8 + 1) * 128, :], yt, [ytk], "o_y_%d_%d" % (st, i8))
                A.release(m)


    def decode(self):
        A, i, o, L, NB = self.arena, self.i, self.o, self.L, self.NB
        self.xsT = self.sb("xsT", [128, 8, NB], F32)
        xsT = self.xsT
        idn = self.ident
        m = A.mark()
        xt, xtk = A.alloc("xstok", NB, [1024], F32)
        self.load(xt, i["xs"], [xtk])
        for c2 in range(2):
            pst, pk = self.ps()
            for q in range(4):
                c = c2 * 4 + q
                self.pe(lambda e, pst=pst, q=q, c=c: e.transpose(pst[:, q * NB:(q + 1) * NB], xt[:, c * 128:(c + 1) * 128], idn[0:NB, 0:NB]), [xtk, "consts"], [pk])
            self.act(lambda e, pst=pst, c2=c2: e.activation(xsT[:, c2 * 4:(c2 + 1) * 4, :], pst[:, 0:4 * NB].rearrange("p (q t) -> p q t", q=4), AF.Copy),
                     [pk], ["xs_%d" % (c2 * 4 + q) for q in range(4)])
        A.release(m)
        for l in range(L):
            self.decode_layer(l)
        m = A.mark()
        yt, ytk = A.alloc("ystok", NB, [1024], F32)
        for c2 in range(2):
            pst, pk = self.ps()
            for q in range(4):
                c = c2 * 4 + q
                self.pe(lambda e, pst=pst, q=q, c=c: e.transpose(pst[0:NB, q * 128:(q + 1) * 128], xsT[:, c, :], idn[:, :]), ["xs_%d" % c, "consts"], [pk])
            self.act(lambda e, pst=pst, c2=c2: e.activation(yt[:, c2 * 512:(c2 + 1) * 512], pst[0:NB, :], AF.Copy), [pk], [ytk])
        self.store(o["y_sample"], yt, [ytk], "o_ys")
        A.release(m)

    def dram_copy(self, out_ap, in_ap, key):
        return self.P.dma("sp", lambda e: e.dma_start(out=out_ap, in_=in_ap), reads=[], writes=[key], final=True)

    def tok_to_fm(self, src, skey, np_, width, nblk, dst_ps):
        pst, pk = dst_ps
        for b in range(nblk):
            self.pe(lambda e, b=b: e.transpose(pst[0:width, b * np_:(b + 1) * np_], src[:, b * width:(b + 1) * width], self.ident[0:np_, 0:np_]),
                    [skey, "consts"], [pk])

    def decode_layer(self, l):
        A, i, o, NB = self.arena, self.i, self.o, self.NB
        xn, xk, xsT = self.xn, "xn0", self.xsT
        xskeys = [["xs_%d" % c for c in range(8)]]
        xsk = lambda mc, tt: "xs_%d" % mc
        xnb = lambda kc: xn[:, kc, 0:NB]
        dve, act = self.dve, self.act
        self.rmsnorm_fm(xsT, xskeys, self.g4[:, 0, l, :], xn[:, :, 0:NB], ["xn0"], NB)
        m0 = A.mark()
        mixg, mixgk = A.alloc("dmixg", 96, [4, NB], BF16)
        mixr, mixrk = A.alloc("dmixr", 96, [4, NB], BF16)
        mixs, mixsk = A.alloc("dmixs", 64, [4, NB], BF16)
        win = i["w_in"][l].rearrange("(kc p) n -> p kc n", p=128)
        w1, w1k = self.wload(win[:, :, C_GQ:C_GQ + 384], 128, [8, 384])
        w2, w2k = self.wload(win[:, :, C_GK:C_GK + 576], 128, [8, 576])
        w3, w3k = self.wload(win[:, :, C_GLR:C_GLR + 400], 128, [8, 400])
        m1 = A.mark()
        q32, q32k = A.alloc("q32", 48, [NB, 4], F32)
        k32, k32k = A.alloc("k32", 48, [NB, 4], F32)
        for (dst, dk_, c0) in ((q32, q32k, 0), (k32, k32k, 192)):
            pst, pk = self.ps()
            for h in range(4):
                self.mm_group(pst[0:48, h * NB:(h + 1) * NB], [(w1[:, kc, c0 + h * 48:c0 + (h + 1) * 48], xnb(kc)) for kc in range(8)], [w1k, xk], pk)
            act(lambda e, pst=pst, dst=dst: e.activation(dst.rearrange("p b h -> p h b"), pst[0:48, 0:4 * NB].rearrange("p (h b) -> p h b", h=4), AF.Copy), [pk], [dk_])
        glr, glrk = A.alloc("dglr", 16, [NB], BF16)
        pst, pk = self.ps()
        self.mm_group(pst[0:16, 0:NB], [(w3[:, kc, 0:16], xnb(kc)) for kc in range(8)], [w3k, xk], pk)
        act(lambda e, pst=pst: e.activation(glr, pst[0:16, 0:NB], AF.Copy), [pk], [glrk])
        pst, pk = self.ps()
        for h in range(4):
            self.mm_group(pst[0:48, h * NB:(h + 1) * NB], [(self.w2g[:, l, h * 48:(h + 1) * 48], glr)], [glrk, "consts"], pk)
        Dd, Ddk = A.alloc("Dd", 48, [NB, 4], F32)
        for h in range(4):
            act(lambda e, pst=pst, h=h: e.activation(Dd[:, :, h], pst[0:48, h * NB:(h + 1) * NB], AF.Exp, bias=self.nbgT[:, l, h:h + 1], scale=-1.0), [pk, "consts"], [Ddk])
        act(lambda e: e.activation(Dd, Dd, AF.Ln, bias=1.0, scale=1.0), [Ddk], [Ddk])
        act(lambda e: e.activation(Dd, Dd, AF.Exp, scale=-1.0 / 16.0), [Ddk], [Ddk])
        pst, pk = self.ps()
        self.mm_group(pst[0:NB, 0:384], [(xnb(kc), w2[:, kc, 192:576]) for kc in range(8)], [xk, w2k], pk)
        vtok, vtokk = A.alloc("dvtok", NB, [384], F32)
        act(lambda e, pst=pst: e.activation(vtok, pst[0:NB, 0:384], AF.Copy), [pk], [vtokk])
        scrk = "scr_v%d" % l
        self.P.dma("sp", lambda e: e.dma_start(out=self.scr_v[l].rearrange("(b n) -> b n", b=NB), in_=vtok), reads=[vtokk], writes=[scrk])
        vb, vbk = A.alloc("dvb", 48, [NB * 4, 96], F32)
        self.load(vb.rearrange("p a v -> p (a v)"), self.scr_v[l].partition_broadcast(48), [vbk], reads=[scrk])
        S0, S0k = A.alloc("dS0", 48, [NB, 4, 96], F32)
        self.load(S0, i["state_gla"][l].rearrange("b h k v -> k b h v"), [S0k])
        S3 = S0.rearrange("p b h v -> p (b h) v")
        bc = lambda t: t.rearrange("p b h -> p (b h)").unsqueeze(2).broadcast_to([48, NB * 4, 96])
        dve(lambda e: e.tensor_tensor(vb, vb, bc(k32), ALU.mult), [vbk, k32k], [vbk])
        dve(lambda e: e.tensor_tensor(S3, S3, bc(Dd), ALU.mult), [S0k, Ddk], [S0k])
        dve(lambda e: e.tensor_tensor(S3, S3, vb, ALU.add), [S0k, vbk], [S0k])
        self.store(o["gla_S_s"][l].rearrange("b h k v -> k b h v"), S0, [S0k], "o_glaSs%d" % l)
        g_po, g_pok = self.ps()

        def fn_o(e):
            ins = None
            for b in range(NB):
                for h in range(4):
                    j = b * 4 + h
                    ins = e.matmul(g_po[0:96, j:j + 1], S0[:, b, h, :], q32[:, b, h:h + 1], start=True, stop=True)
            return ins
        self.pe(fn_o, [S0k, q32k], [g_pok])
        o32, o32k = A.alloc("do32", 96, [NB * 4], F32)
        act(lambda e: e.activation(o32, g_po[0:96, 0:NB * 4], AF.Copy, scale=GLA_SCALE), [g_pok], [o32k])
        sqo, sqok = A.alloc("dsqo", 96, [NB * 4], BF16)
        act(lambda e: e.activation(sqo, o32, AF.Square), [o32k], [sqok])
        pn, pnk = self.ps()
        self.mm_group(pn[0:96, 0:NB * 4], [(self.ones_bf[0:96, 0:96], sqo)], [sqok, "consts"], pnk)
        rs, rsk = A.alloc("drs", 96, [NB * 4], F32)
        self.rstd_from_ps(rs, pn[0:96, 0:NB * 4], pnk, rsk, 96.0)
        pg, pgk = self.ps()
        for h in range(4):
            self.mm_group(pg[0:96, h * NB:(h + 1) * NB], [(w3[:, kc, 16 + h * 96:16 + (h + 1) * 96], xnb(kc)) for kc in range(8)], [w3k, xk], pgk)
        sg, sgk = A.alloc("dsg", 96, [4, NB], F32)
        act(lambda e: e.activation(sg.rearrange("p h b -> p (h b)"), pg[0:96, 0:4 * NB], AF.Silu), [pgk], [sgk])
        dve(lambda e: e.scalar_tensor_tensor(o32, o32, self.gout[:, l:l + 1], rs, ALU.mult, ALU.mult), [o32k, rsk, "consts"], [o32k])
        dve(lambda e: e.tensor_tensor(mixg, o32.rearrange("p (b h) -> p h b", h=4), sg, ALU.mult), [o32k, sgk], [mixgk])
        A.release(m1)
        w4, w4k = self.wload(win[:, :, C_RX:C_RX + 384], 128, [8, 384])
        w5, w5k = self.wload(win[:, :, C_RY:C_RY + 384], 128, [8, 384])
        m1 = A.mark()
        rx32, rx32k = A.alloc("drx", 96, [4, NB], F32)
        gy, gyk = A.alloc("dgy", 96, [4, NB], F32)
        for (dst, dk_, w_, wk_, fnc) in ((rx32, rx32k, w4, w4k, AF.Copy), (gy, gyk, w5, w5k, AF.Gelu_apprx_tanh)):
            pst, pk = self.ps()
            for n in range(4):
                self.mm_group(pst[0:96, n * NB:(n + 1) * NB], [(w_[:, kc, n * 96:(n + 1) * 96], xnb(kc)) for kc in range(8)], [wk_, xk], pk)
            act(lambda e, pst=pst, dst=dst, fnc=fnc: e.activation(dst.rearrange("p n b -> p (n b)"), pst[0:96, 0:4 * NB], fnc), [pk], [dk_])
        pst, pk = self.ps()
        self.mm_group(pst[0:NB, 0:384], [(xnb(kc), w4[:, kc, :]) for kc in range(8)], [xk, w4k], pk)
        rxtok, rxtokk = A.alloc("drxtok", NB, [384], F32)
        act(lambda e, pst=pst: e.activation(rxtok, pst[0:NB, 0:384], AF.Copy), [pk], [rxtokk])
        self.store(o["rg_conv_s"][l, :, 2, :], rxtok, [rxtokk], "o_rgconvs_new%d" % l)
        self.dram_copy(o["rg_conv_s"][l, :, 0:2, :], i["state_rg_conv"][l, :, 1:3, :], "o_rgconvs_old%d" % l)
        cbt, cbtk = A.alloc("dcbt", NB, [3 * 384], F32)
        self.load(cbt, i["state_rg_conv"][l].rearrange("b j c -> b (j c)"), [cbtk])
        pcb = self.ps()
        self.tok_to_fm(cbt, cbtk, NB, 96, 12, pcb)
        cbT, cbTk = A.alloc("dcbT", 96, [3, 4, NB], F32)
        act(lambda e: e.activation(cbT.rearrange("p j n b -> p (j n b)"), pcb[0][0:96, 0:12 * NB], AF.Copy), [pcb[1]], [cbTk])
        h0t, h0tk = A.alloc("dh0t", NB, [384], F32)
        self.load(h0t, i["state_rg_h"][l], [h0tk])
        ph0 = self.ps()
        self.tok_to_fm(h0t, h0tk, NB, 96, 4, ph0)
        h0, h0k = A.alloc("dh0", 96, [4, NB], F32)
        act(lambda e: e.activation(h0.rearrange("p n b -> p (n b)"), ph0[0][0:96, 0:4 * NB], AF.Copy), [ph0[1]], [h0k])
        xc, xck = A.alloc("dxc", 96, [4, NB], F32)
        cw = self.convw
        for n in range(4):
            dve(lambda e, n=n: e.tensor_scalar(xc[:, n, :], rx32[:, n, :], cw[:, l, n, 3:4], self.rg4[:, 0, l, n:n + 1], ALU.mult, ALU.add), [rx32k, "consts"], [xck])
            for j in range(3):
                dve(lambda e, n=n, j=j: e.scalar_tensor_tensor(xc[:, n, :], cbT[:, j, n, :], cw[:, l, n, j:j + 1], xc[:, n, :], ALU.mult, ALU.add), [cbTk, xck, "consts"], [xck])
        xcb, xcbk = A.alloc("dxcb", 96, [4, NB], BF16)
        act(lambda e: e.activation(xcb, xc, AF.Copy), [xck], [xcbk])
        r, rk = A.alloc("dr", 96, [4, NB], F32)
        ig, igk = A.alloc("dig", 96, [4, NB], F32)
        for (dst, dk_, w_, kind) in ((r, rk, self.wa, 1), (ig, igk, self.wx, 2)):
            pst, pk = self.ps()
            for n in range(4):
                self.mm_group(pst[0:96, n * NB:(n + 1) * NB], [(w_[:, l, n, :], xcb[:, n, :])], [xcbk, "consts"], pk)
            for n in range(4):
                act(lambda e, pst=pst, dst=dst, n=n, kind=kind: e.activation(dst[:, n, :], pst[0:96, n * NB:(n + 1) * NB], AF.Sigmoid, bias=self.rg4[:, kind, l, n:n + 1]),
                    [pk, "consts"], [dk_])
        a, ak = A.alloc("da", 96, [4, NB], F32)
        s2, s2k = A.alloc("ds2", 96, [4, NB], F32)
        for n in range(4):
            act(lambda e, n=n: e.activation(a[:, n, :], r[:, n, :], AF.Exp, scale=self.c8[:, l, n:n + 1]), [rk, "consts"], [ak])
            act(lambda e, n=n: e.activation(s2[:, n, :], r[:, n, :], AF.Exp, scale=self.c16[:, l, n:n + 1]), [rk, "consts"], [s2k])
        act(lambda e: e.activation(s2, s2, AF.Sqrt, bias=1.0, scale=-1.0), [s2k], [s2k])
        dve(lambda e: e.tensor_tensor(ig, ig, xc, ALU.mult), [igk, xck], [igk])
        dve(lambda e: e.tensor_tensor(ig, ig, s2, ALU.mult), [igk, s2k], [igk])
        dve(lambda e: e.tensor_tensor(a, a, h0, ALU.mult), [ak, h0k], [ak])
        dve(lambda e: e.tensor_tensor(a, a, ig, ALU.add), [ak, igk], [ak])
        dve(lambda e: e.tensor_tensor(mixr, a, gy, ALU.mult), [ak, gyk], [mixrk])
        pht, phtk = self.ps()
        for n in range(4):
            self.pe(lambda e, n=n: e.transpose(pht[0:NB, n * 96:(n + 1) * 96], a[:, n, :], self.ident[0:96, 0:96]), [ak, "consts"], [phtk])
        htok, htokk = A.alloc("dhtok", NB, [384], F32)
        act(lambda e: e.activation(htok, pht[0:NB, 0:384], AF.Copy), [phtk], [htokk])
        self.store(o["rg_h_s"][l], htok, [htokk], "o_rghs%d" % l)
        A.release(m1)
        w6, w6k = self.wload(win[:, :, C_SQ:C_SQ + 512], 128, [8, 512])
        m1 = A.mark()
        pst, pk = self.ps()
        self.mm_group(pst[0:NB, 0:512], [(xnb(kc), w6[:, kc, :]) for kc in range(8)], [xk, w6k], pk)
        qkv, qkvk = A.alloc("dqkv", NB, [512], F32)
        act(lambda e, pst=pst: e.activation(qkv, pst[0:NB, 0:512], AF.Copy), [pk], [qkvk])
        qn, qnk = A.alloc("dqn", NB, [4, 64], F32)
        kn, knk = A.alloc("dkn", NB, [2, 64], F32)
        self.tok_qknorm(qkv[:, 0:256].rearrange("p (h d) -> p h d", h=4), qkvk, NB, 4, self.g64row[0:NB, 0, l, :], qn, qnk, "dsq")
        self.tok_qknorm(qkv[:, 256:384].rearrange("p (h d) -> p h d", h=2), qkvk, NB, 2, self.g64row[0:NB, 1, l, :], kn, knk, "dsk")
        vnew = qkv[:, 384:512]
        self.store(o["swa_k_s"][l, :, 127, :], kn.rearrange("p h d -> p (h d)"), [knk], "o_swaks_new%d" % l)
        self.store(o["swa_v_s"][l, :, 127, :], vnew, [qkvk], "o_swavs_new%d" % l)
        self.dram_copy(o["swa_k_s"][l, :, 0:127, :], i["cache_swa_k"][l, :, 1:128, :], "o_swaks_old%d" % l)
        self.dram_copy(o["swa_v_s"][l, :, 0:127, :], i["cache_swa_v"][l, :, 1:128, :], "o_swavs_old%d" % l)
        pq, pqk = self.ps()
        self.mm_group(pq[:, 0:256], [(self.sel[:, :], qn.rearrange("p h d -> p (h d)"))], [qnk, "consts"], pqk)
        qrep, qrepk = A.alloc("dqrep", 128, [4, 64], F32)
        act(lambda e: e.activation(qrep.rearrange("p h d -> p (h d)"), pq[:, 0:256], AF.Copy), [pqk], [qrepk])
        Kc, Kck = A.alloc("dKc", 128, [16, 128], F32)
        Vc, Vck = A.alloc("dVc", 128, [16, 128], F32)
        self.load(Kc, i["cache_swa_k"][l].rearrange("b (jc j) n -> (b jc) j n", jc=8), [Kck])
        self.load(Vc, i["cache_swa_v"][l].rearrange("b (jc j) n -> (b jc) j n", jc=8), [Vck])
        tmp, tmpk = A.alloc("dtmp", 128, [16, 4, 64], F32)
        for kv in range(2):
            dve(lambda e, kv=kv: e.tensor_tensor(tmp[:, :, kv * 2:(kv + 1) * 2, :], Kc[:, :, kv * 64:(kv + 1) * 64].unsqueeze(2).broadcast_to([128, 16, 2, 64]),
                                                  qrep[:, kv * 2:(kv + 1) * 2, :].unsqueeze(1).broadcast_to([128, 16, 2, 64]), ALU.mult), [Kck, qrepk], [tmpk])
        s_, s_k = A.alloc("ds", 128, [64], F32)
        dve(lambda e: e.tensor_reduce(s_, tmp.rearrange("p j h d -> p (j h) d"), AX.X, ALU.add), [tmpk], [s_k])
        dve(lambda e: e.scalar_tensor_tensor(s_, s_, 0.125, self.alibi_dec[:, :, :].rearrange("p j h -> p (j h)"), ALU.mult, ALU.add), [s_k, "consts"], [s_k])
        e_, e_k = A.alloc("de", 128, [64], F32)
        act(lambda e: e.activation(e_, s_, AF.Exp), [s_k], [e_k])
        t2, t2k = A.alloc("dt2", NB, [4, 64], F32)
        for kv in range(2):
            dve(lambda e, kv=kv: e.tensor_tensor(t2[:, kv * 2:(kv + 1) * 2, :], qn[:, kv * 2:(kv + 1) * 2, :], kn[:, kv:kv + 1, :].broadcast_to([NB, 2, 64]), ALU.mult), [qnk, knk], [t2k])
        en, enk = A.alloc("den", NB, [4], F32)
        dve(lambda e: e.tensor_reduce(en, t2, AX.X, ALU.add), [t2k], [enk])
        act(lambda e: e.activation(en, en, AF.Exp, scale=0.125), [enk], [enk])
        dp, dpk = A.alloc("ddp", 128, [4], F32)
        dve(lambda e: e.tensor_reduce(dp, e_.rearrange("p (j h) -> p h j", h=4), AX.X, ALU.add), [e_k], [dpk])
        pd, pdk = self.ps()
        self.mm_group(pd[0:NB, 0:4], [(self.selT[:, :], dp)], [dpk, "consts"], pdk)
        den, denk = A.alloc("dden", NB, [4], F32)
        dve(lambda e: e.tensor_tensor(den, pd[0:NB, 0:4], en, ALU.add), [pdk, enk], [denk])
        dve(lambda e: e.tensor_tensor(den, den, self.esink[0:NB, l, :], ALU.add), [denk, "consts"], [denk])
        dve(lambda e: e.reciprocal(den, den), [denk], [denk])
        for kv in range(2):
            dve(lambda e, kv=kv: e.tensor_tensor(tmp[:, :, kv * 2:(kv + 1) * 2, :], Vc[:, :, kv * 64:(kv + 1) * 64].unsqueeze(2).broadcast_to([128, 16, 2, 64]),
                                                  e_.rearrange("p (j h) -> p j h", h=4)[:, :, kv * 2:(kv + 1) * 2].unsqueeze(3).broadcast_to([128, 16, 2, 64]), ALU.mult),
                [Vck, e_k, tmpk], [tmpk])
        op_, op_k = A.alloc("dop", 128, [4, 64], F32)
        dve(lambda e: e.tensor_reduce(op_, tmp.rearrange("p j h d -> p h d j"), AX.X, ALU.add), [tmpk], [op_k])
        po, pok = self.ps()
        self.mm_group(po[0:NB, 0:256], [(self.selT[:, :], op_.rearrange("p h d -> p (h d)"))], [op_k, "consts"], pok)
        for kv in range(2):
            dve(lambda e, kv=kv: e.tensor_tensor(t2[:, kv * 2:(kv + 1) * 2, :], vnew[:, kv * 64:(kv + 1) * 64].unsqueeze(1).broadcast_to([NB, 2, 64]),
                                                  en[:, kv * 2:(kv + 1) * 2].unsqueeze(2).broadcast_to([NB, 2, 64]), ALU.mult), [qkvk, enk, t2k], [t2k])
        dve(lambda e: e.tensor_tensor(t2, t2, po[0:NB, 0:256].rearrange("p (h d) -> p h d", h=4), ALU.add), [t2k, pok], [t2k])
        dve(lambda e: e.tensor_tensor(t2, t2, den.unsqueeze(2).broadcast_to([NB, 4, 64]), ALU.mult), [t2k, denk], [t2k])
        pt_ = self.ps()
        self.tok_to_fm(t2.rearrange("p h d -> p (h d)"), t2k, NB, 64, 4, pt_)
        act(lambda e: e.activation(mixs.rearrange("p h b -> p (h b)"), pt_[0][0:64, 0:4 * NB], AF.Copy), [pt_[1]], [mixsk])
        A.release(m1)
        wo = i["w_out"][l]
        wog, wogk = self.wload(wo[0:384, :].rearrange("(k p) n -> p k n", p=96), 96, [4, 1024])
        wor, work = self.wload(wo[384:768, :].rearrange("(k p) n -> p k n", p=96), 96, [4, 1024])
        wos, wosk = self.wload(wo[768:1024, :].rearrange("(k p) n -> p k n", p=64), 64, [4, 1024])
        self.proj_residual([(wog, wogk, mixg, mixgk, 4), (wor, work, mixr, mixrk, 4), (wos, wosk, mixs, mixsk, 4)], 1, NB, xt=xsT, xkf=xsk)
        A.release(m0)
        self.rmsnorm_fm(xsT, xskeys, self.g4[:, 1, l, :], xn[:, :, 0:NB], ["xn0"], NB)
        m0 = A.mark()
        wq, wqk = self.wload(i["mem_w_q"][l].rearrange("(kc p) n -> p kc n", p=128), 128, [8, 256])
        wmo, wmok = self.wload(i["mem_w_o"][l].rearrange("(k p) n -> p k n", p=64), 64, [4, 1024])
        pst, pk = self.ps()
        self.mm_group(pst[0:NB, 0:256], [(xnb(kc), wq[:, kc, :]) for kc in range(8)], [xk, wqk], pk)
        qraw, qrawk = A.alloc("mqraw", NB, [256], F32)
        act(lambda e, pst=pst: e.activation(qraw, pst[0:NB, 0:256], AF.Copy), [pk], [qrawk])
        m_qn, m_qnk = A.alloc("mqn", NB, [4, 64], F32)
        self.tok_qknorm(qraw.rearrange("p (h d) -> p h d", h=4), qrawk, NB, 4, self.g64row[0:NB, 2, l, :], m_qn, m_qnk, "dmq")
        m_pq, m_pqk = self.ps()
        self.mm_group(m_pq[:, 0:256], [(self.sel[:, :], m_qn.rearrange("p h d -> p (h d)"))], [m_qnk, "consts"], m_pqk)
        m_qrep, m_qrepk = A.alloc("mqrep", 128, [256], F32)
        act(lambda e: e.activation(m_qrep, m_pq[:, 0:256], AF.Copy), [m_pqk], [m_qrepk])
        opa, opak = A.alloc("mopa", 128, [2, 256], F32)
        dpa, dpak = A.alloc("mdpa", 128, [2, 4], F32)
        ckv = lambda name: i[name][l].rearrange("b (jc jj) n -> (b jc) jj n", jc=8)
        for half in range(2):
            m2 = A.mark()
            m_Kc, m_Kck = A.alloc("mKc", 128, [16, 256], F32)
            m_Vc, m_Vck = A.alloc("mVc", 128, [16, 256], F32)
            self.load(m_Kc, ckv("cache_mem_k")[:, half * 16:(half + 1) * 16, :], [m_Kck])
            self.load(m_Vc, ckv("cache_mem_v")[:, half * 16:(half + 1) * 16, :], [m_Vck])
            m_tmp, m_tmpk = A.alloc("mtmp", 128, [16, 256], F32)
            dve(lambda e, m_Kc=m_Kc, m_tmp=m_tmp: e.tensor_tensor(m_tmp, m_Kc, m_qrep.unsqueeze(1).broadcast_to([128, 16, 256]), ALU.mult), [m_Kck, m_qrepk], [m_tmpk])
            m_s_, m_s_k = A.alloc("ms", 128, [64], F32)
            dve(lambda e, m_tmp=m_tmp, m_s_=m_s_: e.tensor_reduce(m_s_, m_tmp.rearrange("p j (h d) -> p (j h) d", h=4), AX.X, ALU.add), [m_tmpk], [m_s_k])
            act(lambda e, m_s_=m_s_: e.activation(m_s_, m_s_, AF.Exp, scale=0.125), [m_s_k], [m_s_k])
            dve(lambda e, m_s_=m_s_, half=half: e.tensor_reduce(dpa[:, half, :], m_s_.rearrange("p (j h) -> p h j", h=4), AX.X, ALU.add), [m_s_k], [dpak])
            dve(lambda e, m_Vc=m_Vc, m_tmp=m_tmp, m_s_=m_s_: e.tensor_tensor(m_tmp.rearrange("p j (h d) -> p j h d", h=4), m_Vc.rearrange("p j (h d) -> p j h d", h=4),
                                                              m_s_.rearrange("p (j h) -> p j h", h=4).unsqueeze(3).broadcast_to([128, 16, 4, 64]), ALU.mult),
                [m_Vck, m_s_k, m_tmpk], [m_tmpk])
            dve(lambda e, m_tmp=m_tmp, half=half: e.tensor_reduce(opa[:, half, :].rearrange("p (h d) -> p h d", h=4), m_tmp.rearrange("p j (h d) -> p h d j", h=4), AX.X, ALU.add),
                [m_tmpk], [opak])
            A.release(m2)
        dve(lambda e: e.tensor_tensor(opa[:, 0, :], opa[:, 0, :], opa[:, 1, :], ALU.add), [opak], [opak])
        dve(lambda e: e.tensor_tensor(dpa[:, 0, :], dpa[:, 0, :], dpa[:, 1, :], ALU.add), [dpak], [dpak])
        m_po, m_pok = self.ps()
        self.mm_group(m_po[0:NB, 0:256], [(self.selT[:, :], opa[:, 0, :])], [opak, "consts"], m_pok)
        m_pd, m_pdk = self.ps()
        self.mm_group(m_pd[0:NB, 0:4], [(self.selT[:, :], dpa[:, 0, :])], [dpak, "consts"], m_pdk)
        m_den, m_denk = A.alloc("mden", NB, [4], F32)
        dve(lambda e: e.reciprocal(m_den, m_pd[0:NB, 0:4]), [m_pdk], [m_denk])
        m_ot, m_otk = A.alloc("mot", NB, [4, 64], F32)
        dve(lambda e: e.tensor_tensor(m_ot, m_po[0:NB, 0:256].rearrange("p (h d) -> p h d", h=4), m_den.unsqueeze(2).broadcast_to([NB, 4, 64]), ALU.mult), [m_pok, m_denk], [m_otk])
        m_pt_ = self.ps()
        self.tok_to_fm(m_ot.rearrange("p h d -> p (h d)"), m_otk, NB, 64, 4, m_pt_)
        omT, omTk = A.alloc("momT", 64, [4, NB], BF16)
        act(lambda e: e.activation(omT.rearrange("p h b -> p (h b)"), m_pt_[0][0:64, 0:4 * NB], AF.Copy), [m_pt_[1]], [omTk])
        self.proj_residual([(wmo, wmok, omT, omTk, 4)], 1, NB, xt=xsT, xkf=xsk)
        A.release(m0)
        self.rmsnorm_fm(xsT, xskeys, self.g4[:, 3, l, :], xn[:, :, 0:NB], ["xn0"], NB)
        m0 = A.mark()
        hT, hTk = A.alloc("dhT", 128, [32, NB], BF16)
        w1d = i["ffn_w1"][l].rearrange("(kc p) n -> p kc n", p=128)
        w2d = i["ffn_w2"][l]
        ph, phk = self.ps()
        for p8 in range(8):
            wp, wpk = self.wload(w1d[:, :, p8 * 512:(p8 + 1) * 512], 128, [8, 512])
            for s4 in range(4):
                sub = p8 * 4 + s4
                self.mm_group(ph[:, sub * NB:(sub + 1) * NB], [(wp[:, kc, s4 * 128:(s4 + 1) * 128], xnb(kc)) for kc in range(8)], [wpk, xk], phk)
        fsq, fsqk = A.alloc("dfsq", 128, [32 * NB], F32)
        act(lambda e: e.activation(fsq, ph[:, 0:32 * NB], AF.Square), [phk], [fsqk])
        dve(lambda e: e.scalar_tensor_tensor(hT.rearrange("p s b -> p (s b)"), ph[:, 0:32 * NB], 0.0, fsq, ALU.is_gt, ALU.mult), [phk, fsqk], [hTk])
        allxs = ["xs_%d" % c for c in range(8)]
        for p8 in range(8):
            w2p, w2pk = self.wload(w2d[p8 * 512:(p8 + 1) * 512, :].rearrange("(k p) n -> p k n", p=128), 128, [4, 1024])
            py, pyk = self.ps()
            for mc in range(8):
                self.mm_group(py[:, mc * NB:(mc + 1) * NB], [(w2p[:, k, mc * 128:(mc + 1) * 128], hT[:, p8 * 4 + k, :]) for k in range(4)], [w2pk, hTk], pyk)
            dve(lambda e, py=py: e.tensor_tensor(xsT.rearrange("p c b -> p (c b)"), xsT.rearrange("p c b -> p (c b)"), py[:, 0:8 * NB], ALU.add), [pyk] + allxs, allxs)
        A.release(m0)

    def build(self):
        self.arena_n = 18688
        self.declare()
        self.setup()
        if self.do_decode:
            self.decode()
        if self.do_prompt:
            self.prompt()
        self.P.emit()
        self.es.close()
        return self.nc


def shared_inputs(inp, L):
    f = lambda a: np.ascontiguousarray(np.asarray(a, dtype=np.float32))
    m = {}
    for k in ["w_in", "w_out", "mem_w_q", "mem_w_kv", "mem_w_o", "ffn_w1", "ffn_w2", "gla_w_gate2", "gla_b_gate", "rg_w_a", "rg_w_x"]:
        m[k] = f(inp[k][:L])
    fm = lambda v: np.asarray(v[:L]).reshape(L, 8, 128).transpose(2, 0, 1)
    m["v_g4"] = f(np.stack([fm(inp["g_mix"]), fm(inp["g_mem_x"]), fm(inp["g_mem_m"]), fm(inp["g_ffn"])], axis=1))
    m["v_bgT"] = f(np.asarray(inp["gla_b_gate"][:L]).reshape(L, 4, 48).transpose(2, 0, 1))
    m["v_gout"] = f(np.asarray(inp["gla_g_out"][:L]).T)
    m["v_convw"] = f(np.asarray(inp["rg_conv_w"][:L]).reshape(L, 4, 4, 96).transpose(3, 0, 2, 1))
    r4 = lambda v: np.asarray(v[:L]).reshape(L, 4, 96).transpose(2, 0, 1)
    m["v_rg4"] = f(np.stack([r4(inp["rg_conv_b"]), r4(inp["rg_b_a"]), r4(inp["rg_b_x"]), r4(inp["rg_lam"])], axis=1))
    g64 = np.stack([np.asarray(inp["swa_g_q"][:L]), np.asarray(inp["swa_g_k"][:L]), np.asarray(inp["mem_g_q"][:L]), np.asarray(inp["mem_g_k"][:L])], axis=0)
    m["v_g64"] = f(g64.transpose(2, 0, 1))
    m["v_g64row"] = f(g64.reshape(-1))
    m["v_sinks"] = f(np.asarray(inp["swa_sinks"][:L]).reshape(-1))
    m.update(host_consts())
    return m


_T, _L, _NB, _NC = 8192, 4, 16, 8
_CACHE = {}


def kernel(**inputs):
    inp = {k: np.asarray(v) for k, v in inputs.items()}
    if "nc" not in _CACHE:
        _CACHE["nc"] = KB(_T, _L, _NB).build()
    nc = _CACHE["nc"]
    shared = shared_inputs(inp, _L)
    f = lambda a: np.ascontiguousarray(a, dtype=np.float32)
    in_maps = []
    for c in range(_NC):
        b = c // 4
        s0 = c * _NB
        m = dict(shared)
        m["xp"] = f(inp["x_prompt"][b])
        m["xs"] = f(inp["x_sample"][s0:s0 + _NB, 0])
        m["memp"] = f(inp["mem_prompt"][b])
        m["state_gla"] = f(inp["state_gla"][:, s0:s0 + _NB])
        m["state_rg_h"] = f(inp["state_rg_h"][:, s0:s0 + _NB])
        m["state_rg_conv"] = f(inp["state_rg_conv"][:, s0:s0 + _NB])
        m["cache_swa_k"] = f(inp["cache_swa_k"][:, s0:s0 + _NB]).reshape(_L, _NB, 128, 128)
        m["cache_swa_v"] = f(inp["cache_swa_v"][:, s0:s0 + _NB]).reshape(_L, _NB, 128, 128)
        m["cache_mem_k"] = f(inp["cache_mem_k"][:, s0:s0 + _NB]).reshape(_L, _NB, 256, 256)
        m["cache_mem_v"] = f(inp["cache_mem_v"][:, s0:s0 + _NB]).reshape(_L, _NB, 256, 256)
        in_maps.append(m)
    res = run_bass_kernel_spmd(nc, in_maps, core_ids=list(range(_NC)))
    r = res.results
    pc = (0, 4)
    cat = lambda name, ax: np.concatenate([np.asarray(r[c][name]) for c in range(_NC)], axis=ax)
    stk = lambda name: np.stack([np.asarray(r[c][name]) for c in pc], axis=1)
    y_prompt = np.stack([np.asarray(r[c]["y_prompt"]) for c in pc], axis=0)
    y_sample = cat("y_sample", 0).reshape(128, 1, D)
    out = (
        y_prompt, y_sample,
        stk("gla_S_p"), cat("gla_S_s", 1),
        stk("rg_h_p"), cat("rg_h_s", 1),
        stk("rg_conv_p"), cat("rg_conv_s", 1),
        stk("swa_k_p").reshape(_L, 2, 128, 2, 64), cat("swa_k_s", 1).reshape(_L, 128, 128, 2, 64),
        stk("swa_v_p").reshape(_L, 2, 128, 2, 64), cat("swa_v_s", 1).reshape(_L, 128, 128, 2, 64),
        stk("mem_k_p").reshape(_L, 2, 256, 4, 64), stk("mem_v_p").reshape(_L, 2, 256, 4, 64),
    )
    return tuple(np.ascontiguousarray(o, dtype=np.float32) for o in out)
```

```python
import math
from contextlib import ExitStack
import numpy as np
import concourse.bass as bass
import concourse.mybir as mybir
from concourse.bass_utils import run_bass_kernel_spmd

F32 = mybir.dt.float32
BF16 = mybir.dt.bfloat16
AF = mybir.ActivationFunctionType
ALU = mybir.AluOpType
AX = mybir.AxisListType

D = 1024
DEPTH = 4
EPS = 1e-6
NEG = -30000.0
IN_W = 2448
C_GQ, C_GK, C_GV, C_GLR, C_GOG, C_RX, C_RY, C_SQ, C_SK, C_SV = 0, 192, 384, 768, 784, 1168, 1552, 1936, 2192, 2320
GLA_SCALE = 48 ** -0.5


class Prog:
    ENGS = ("pe", "act", "dve", "pool", "sp")

    def __init__(self, nc, n_dma_sems=8):
        self.nc = nc
        self.items = {e: [] for e in self.ENGS}
        self.cnt = {e: 0 for e in self.ENGS}
        self.last_w = {}
        self.readers = {}
        self.seen = {e: {} for e in self.ENGS}
        self.dma_cnt = {e: 0 for e in self.ENGS}
        self.n_dma_sems = n_dma_sems
        self.final_tokens = []

    @staticmethod
    def _merge(d, s, v):
        if d.get(s, 0) < v:
            d[s] = v

    def _deps(self, reads, writes):
        deps = {}
        for r in reads:
            for s, v in self.last_w.get(r, {}).items():
                self._merge(deps, s, v)
        for w in writes:
            for s, v in self.last_w.get(w, {}).items():
                self._merge(deps, s, v)
            for s, v in self.readers.get(w, {}).items():
                self._merge(deps, s, v)
        return deps

    def _commit(self, tok, reads, writes):
        s, v = tok
        for r in reads:
            self._merge(self.readers.setdefault(r, {}), s, v)
        for w in writes:
            self.last_w[w] = {s: v}
            self.readers[w] = {}

    def all_users(self, key):
        d = dict(self.last_w.get(key, {}))
        for s, v in self.readers.get(key, {}).items():
            self._merge(d, s, v)
        return d

    def inherit(self, key, deps):
        d = self.last_w.setdefault(key, {})
        for s, v in deps.items():
            self._merge(d, s, v)

    def op(self, eng, fn, reads=(), writes=()):
        own = "c_" + eng
        deps = {}
        for r in reads:
            for s, v in self.last_w.get(r, {}).items():
                self._merge(deps, s, v)
        for w in writes:
            for s, v in self.last_w.get(w, {}).items():
                if s != own:
                    self._merge(deps, s, v)
            for s, v in self.readers.get(w, {}).items():
                if s != own:
                    self._merge(deps, s, v)
        self.cnt[eng] += 1
        tok = ("c_" + eng, self.cnt[eng])
        waits = []
        for s, v in deps.items():
            if s == tok[0] and eng == "pe":
                continue
            if self.seen[eng].get(s, 0) >= v:
                continue
            self.seen[eng][s] = v
            waits.append((s, v))
        self.items[eng].append((waits, fn, tok[0], 1))
        self._commit(tok, reads, writes)
        return tok

    def dma(self, q, fn, reads=(), writes=(), final=False):
        deps = self._deps(reads, writes)
        j = self.dma_cnt[q]
        self.dma_cnt[q] += 1
        sname = "d_%s_%d" % (q, j % self.n_dma_sems)
        val = 16 * (j // self.n_dma_sems + 1)
        if val > 16:
            self._merge(deps, sname, val - 16)
        tok = (sname, val)
        waits = []
        for s, v in deps.items():
            if self.seen[q].get(s, 0) >= v:
                continue
            self.seen[q][s] = v
            waits.append((s, v))
        self.items[q].append((waits, fn, sname, 16))
        self._commit(tok, reads, writes)
        if final:
            self.final_tokens.append(tok)
        return tok

    def cc(self, fn, reads=(), writes=()):
        deps = self._deps(reads, writes)
        j = self.cc_cnt = getattr(self, "cc_cnt", 0) + 1
        sname = "cc_%d" % (j % 4)
        val = (j - 1) // 4 + 1
        if val > 1:
            self._merge(deps, sname, val - 1)
        tok = (sname, val)
        waits = []
        for s, v in deps.items():
            if self.seen["pool"].get(s, 0) >= v:
                continue
            self.seen["pool"][s] = v
            waits.append((s, v))
        self.items["pool"].append((waits, fn, sname, 1))
        self._commit(tok, reads, writes)
        return tok

    def emit(self):
        nc = self.nc
        names = set()
        for e in self.ENGS:
            for waits, fn, s, inc in self.items[e]:
                names.add(s)
                for ws, _ in waits:
                    names.add(ws)
        for s, _ in self.final_tokens:
            names.add(s)
        names = sorted(names)
        with ExitStack() as st:
            sem = {n: st.enter_context(nc.semaphore(n)) for n in names}
            block = st.enter_context(nc.Block())
            final = {}
            for s, v in self.final_tokens:
                self._merge(final, s, v)

            def run(engname, e):
                for waits, fn, s, inc in self.items[engname]:
                    for ws, wv in waits:
                        e.wait_ge(sem[ws], wv)
                    ins = fn(e)
                    ins.then_inc(sem[s], inc)
                if engname == "sp":
                    for s, v in final.items():
                        e.wait_ge(sem[s], v)

            @block.tensor
            def _(e):
                run("pe", e)

            @block.scalar
            def _(e):
                run("act", e)

            @block.vector
            def _(e):
                run("dve", e)

            @block.gpsimd
            def _(e):
                run("pool", e)

            @block.sync
            def _(e):
                run("sp", e)


class Arena:
    def __init__(self, P, tens, n_f32):
        self.P = P
        self.t = tens
        self.n = n_f32
        self.off = 0
        self.hist = []
        self.serial = 0

    def mark(self):
        return self.off

    def release(self, m):
        self.off = m

    def alloc(self, name, parts, shape, dt=F32):
        n = 1
        for s in shape:
            n *= s
        nf = n if dt == F32 else (n + 1) // 2
        nf = (nf + 15) // 16 * 16
        a = self.off
        b = a + nf
        assert b <= self.n, "arena overflow %s %d > %d" % (name, b, self.n)
        self.off = b
        self.serial += 1
        key = "%s#%d" % (name, self.serial)
        deps = {}
        keep = []
        for (k, ka, kb) in self.hist:
            if ka < b and a < kb:
                for s, v in self.P.all_users(k).items():
                    Prog._merge(deps, s, v)
                if not (a <= ka and kb <= b):
                    keep.append((k, ka, kb))
            else:
                keep.append((k, ka, kb))
        keep.append((key, a, b))
        self.hist = keep
        if deps:
            self.P.inherit(key, deps)
        ap = self.t[0:parts, a:a + nf]
        if dt != F32:
            ap = ap.bitcast(dt)[:, 0:n]
        else:
            ap = ap[:, 0:n]
        if len(shape) == 2:
            ap = ap.rearrange("p (a b) -> p a b", a=shape[0])
        elif len(shape) == 3:
            ap = ap.rearrange("p (a b c) -> p a b c", a=shape[0], b=shape[1])
        elif len(shape) == 4:
            ap = ap.rearrange("p (a b c d) -> p a b c d", a=shape[0], b=shape[1], c=shape[2])
        return ap, key


def host_consts():
    c = {}
    c["c_ident"] = np.eye(128, dtype=np.float32)
    s = np.arange(64)[:, None]
    t = np.arange(64)[None, :]
    c["c_u64"] = np.where(s <= t, -1.0 / 16.0, 0.0).astype(np.float32)
    c["c_l64"] = np.where(s > t, -1.0 / 16.0, 0.0).astype(np.float32)
    c["c_mask64"] = np.where(s <= t, GLA_SCALE, 0.0).astype(np.float32)
    slopes = np.array([2.0 ** (-2.0 * (h + 1)) for h in range(4)], np.float32)
    j = np.arange(128)[:, None, None, None]
    tt = np.arange(128)[None, None, None, :]
    h = slopes[None, None, :, None]
    bt = np.zeros((128, 2, 4, 128), np.float32)
    dist0 = (128 + tt - j).astype(np.float32)
    bt0 = np.where(dist0 <= 128, -h * dist0, NEG)
    dist1 = (tt - j).astype(np.float32)
    bt1 = np.where(dist1 >= 0, -h * dist1, NEG)
    bt[:, 0:1] = bt0
    bt[:, 1:2] = bt1
    c["c_bt"] = bt.astype(np.float32)
    bt_first = bt.copy()
    bt_first[:, 0] = NEG
    c["c_bt0"] = bt_first.astype(np.float32)
    c["c_btdelta"] = (bt_first - bt).astype(np.float32)
    p = np.arange(128)[:, None, None]
    jj = np.arange(16)[None, :, None]
    key = (p % 8) * 16 + jj
    c["c_alibi_dec"] = (-(128.0 - key) * slopes[None, None, :]).astype(np.float32)
    sel = np.zeros((16, 128), np.float32)
    for b in range(16):
        sel[b, b * 8:(b + 1) * 8] = 1.0
    c["c_sel"] = sel
    c["c_selT"] = np.ascontiguousarray(sel.T)
    return c


CONST_SHAPES = {"c_ident": [128, 128], "c_u64": [64, 64], "c_l64": [64, 64], "c_mask64": [64, 64],
                "c_bt": [128, 2, 4, 128], "c_bt0": [128, 2, 4, 128], "c_alibi_dec": [128, 16, 4],
                "c_sel": [16, 128], "c_selT": [128, 16]}


class KB:
    def __init__(self, T, L, NB, do_prompt=True, do_decode=True, par=False):
        self.T, self.L, self.NB = T, L, NB
        self.par = par
        self.do_prompt, self.do_decode = do_prompt, do_decode
        self.nc = bass.Bass("TRN2", target_bir_lowering=False)
        self.P = Prog(self.nc)
        self.ps_i = 0
        self.ws_i = 0
        self.es = ExitStack()

    def din(self, name, shape):
        return self.nc.dram_tensor(name, list(shape), F32, kind="ExternalInput").ap()

    def dout(self, name, shape):
        return self.nc.dram_tensor(name, list(shape), F32, kind="ExternalOutput").ap()

    def sb(self, name, shape, dt=F32):
        return self.es.enter_context(self.nc.sbuf_tensor(name, list(shape), dt))

    def declare(self):
        T, L, NB = self.T, self.L, self.NB
        i = {}
        i["xp"] = self.din("xp", [T, D])
        i["xs"] = self.din("xs", [NB, D])
        i["memp"] = self.din("memp", [256, D])
        i["state_gla"] = self.din("state_gla", [L, NB, 4, 48, 96])
        i["state_rg_h"] = self.din("state_rg_h", [L, NB, 384])
        i["state_rg_conv"] = self.din("state_rg_conv", [L, NB, 3, 384])
        i["cache_swa_k"] = self.din("cache_swa_k", [L, NB, 128, 128])
        i["cache_swa_v"] = self.din("cache_swa_v", [L, NB, 128, 128])
        i["cache_mem_k"] = self.din("cache_mem_k", [L, NB, 256, 256])
        i["cache_mem_v"] = self.din("cache_mem_v", [L, NB, 256, 256])
        i["w_in"] = self.din("w_in", [L, D, IN_W])
        i["w_out"] = self.din("w_out", [L, D, D])
        i["mem_w_q"] = self.din("mem_w_q", [L, D, 256])
        i["mem_w_kv"] = self.din("mem_w_kv", [L, D, 512])
        i["mem_w_o"] = self.din("mem_w_o", [L, 256, D])
        i["ffn_w1"] = self.din("ffn_w1", [L, D, 4096])
        i["ffn_w2"] = self.din("ffn_w2", [L, 4096, D])
        i["gla_w_gate2"] = self.din("gla_w_gate2", [L, 16, 192])
        i["gla_b_gate"] = self.din("gla_b_gate", [L, 192])
        i["rg_w_a"] = self.din("rg_w_a", [L, 4, 96, 96])
        i["rg_w_x"] = self.din("rg_w_x", [L, 4, 96, 96])
        i["v_g4"] = self.din("v_g4", [128, 4, L, 8])
        i["v_bgT"] = self.din("v_bgT", [48, L, 4])
        i["v_gout"] = self.din("v_gout", [96, L])
        i["v_convw"] = self.din("v_convw", [96, L, 4, 4])
        i["v_rg4"] = self.din("v_rg4", [96, 4, L, 4])
        i["v_g64"] = self.din("v_g64", [64, 4, L])
        i["v_g64row"] = self.din("v_g64row", [4 * L * 64])
        i["v_sinks"] = self.din("v_sinks", [L * 4])
        for k, shp in CONST_SHAPES.items():
            i[k] = self.din(k, shp)
        self.i = i
        o = {}
        o["y_prompt"] = self.dout("y_prompt", [T, D])
        o["y_sample"] = self.dout("y_sample", [NB, D])
        o["gla_S_p"] = self.dout("gla_S_p", [L, 4, 48, 96])
        o["gla_S_s"] = self.dout("gla_S_s", [L, NB, 4, 48, 96])
        o["rg_h_p"] = self.dout("rg_h_p", [L, 384])
        o["rg_h_s"] = self.dout("rg_h_s", [L, NB, 384])
        o["rg_conv_p"] = self.dout("rg_conv_p", [L, 3, 384])
        o["rg_conv_s"] = self.dout("rg_conv_s", [L, NB, 3, 384])
        o["swa_k_p"] = self.dout("swa_k_p", [L, 128, 128])
        o["swa_k_s"] = self.dout("swa_k_s", [L, NB, 128, 128])
        o["swa_v_p"] = self.dout("swa_v_p", [L, 128, 128])
        o["swa_v_s"] = self.dout("swa_v_s", [L, NB, 128, 128])
        o["mem_k_p"] = self.dout("mem_k_p", [L, 256, 256])
        o["mem_v_p"] = self.dout("mem_v_p", [L, 256, 256])
        self.o = o
        self.scr_v = self.nc.dram_tensor("scr_v", [L, NB * 384], F32).ap()
        if self.par:
            i["pmask"] = self.din("pmask", [8])
            i["prevmask"] = self.din("prevmask", [8])
            i["isfirst"] = self.din("isfirst", [1])
            i["c_btdelta"] = self.din("c_btdelta", [128, 2, 4, 128])
            nst = T // 1024
            self.xscr = self.nc.dram_tensor("xscr", [nst, 128, 8192], F32).ap()
            self.ag1_in = [self.nc.dram_tensor("ag1_in%d" % l, [128, 640], F32) for l in range(L)]
            self.ag1_out = [self.nc.dram_tensor("ag1_out%d" % l, [1024, 640], F32) for l in range(L)]
            self.ag2_in = [self.nc.dram_tensor("ag2_in%d" % l, [96, 396], F32) for l in range(L)]
            self.ag2_out = [self.nc.dram_tensor("ag2_out%d" % l, [768, 396], F32) for l in range(L)]

    def ps(self):
        k = self.ps_i
        self.ps_i = (k + 1) % 8
        return self.psb[k], "ps%d" % k

    def act(self, fn, reads, writes):
        return self.P.op("act", fn, reads, writes)

    def dve(self, fn, reads, writes):
        return self.P.op("dve", fn, reads, writes)

    def pe(self, fn, reads, writes):
        return self.P.op("pe", fn, reads, writes)

    def load(self, out_ap, in_ap, writes, reads=(), q="sp"):
        return self.P.dma(q, lambda e: e.dma_start(out=out_ap, in_=in_ap), reads=reads, writes=writes)

    def store(self, out_ap, in_ap, reads, key, slow=False):
        if slow:
            fn = lambda e: e.dma_start(out=out_ap, in_=in_ap, allow_slow_non_contiguous=True)
        else:
            fn = lambda e: e.dma_start(out=out_ap, in_=in_ap)
        return self.P.dma("sp", fn, reads=reads, writes=[key], final=True)

    def dbg_store(self, name, ap, key, dt=F32):
        if not getattr(self, "debug", False):
            return
        shp = list(ap.shape)
        d = self.nc.dram_tensor(name, shp, F32, kind="ExternalOutput").ap()
        q = "sp" if dt == F32 else "pool"
        self.P.dma(q, lambda e: e.dma_start(out=d, in_=ap), reads=[key], writes=["dbg_" + name], final=True)

    def wload(self, dram_ap, parts, shape):
        s = self.ws_i
        self.ws_i = (s + 1) % len(self.wslots)
        n = 1
        for x in shape:
            n *= x
        assert n <= 4608
        v = self.wslots[s][0:parts, 0:n]
        if len(shape) == 2:
            v = v.rearrange("p (a b) -> p a b", a=shape[0])
        key = "wslot%d" % s
        self.P.dma("pool", lambda e: e.dma_start(out=v, in_=dram_ap), writes=[key])
        return v, key

    def mm_group(self, out_ap, pairs, reads, pkey, first=True, last=True):
        def fn(e):
            n = len(pairs)
            ins = None
            for i, (l, r) in enumerate(pairs):
                ins = e.matmul(out_ap, l, r, start=(first and i == 0), stop=(last and i == n - 1))
            return ins
        return self.pe(fn, reads, [pkey])

    def rstd_from_ps(self, out_ap, ps_ap, pkey, okey, dim):
        self.act(lambda e: e.activation(out_ap, ps_ap, AF.Sqrt, bias=self.eps_ap[0:out_ap.shape[0], :], scale=1.0 / dim),
                 [pkey, "consts"], [okey])
        self.dve(lambda e: e.reciprocal(out_ap, out_ap), [okey], [okey])

    def rmsnorm_fm(self, x_ap, xkeys, g_ap, out_ap, okeys, N):
        A = self.arena
        ntt = (N + 511) // 512
        for tt in range(ntt):
            n0 = tt * 512
            n = min(512, N - n0)
            m = A.mark()
            sq, sqk = A.alloc("sq", 128, [8, n], BF16)
            rs, rsk = A.alloc("rs", 128, [n], F32)
            pst, pk = self.ps()
            for hh_ in range(2):
                sqh = sq[:, hh_ * 4:(hh_ + 1) * 4, :]
                xs_ = x_ap[:, hh_ * 4:(hh_ + 1) * 4, n0:n0 + n]
                kk = sqk + "_%d" % hh_
                self.P.inherit(kk, self.P.all_users(sqk))
                self.act(lambda e, sqh=sqh, xs_=xs_: e.activation(sqh, xs_, AF.Square), xkeys[tt], [kk])
                self.mm_group(pst[:, 0:n], [(self.ones_bf[:, 0:128], sq[:, c, :]) for c in range(hh_ * 4, hh_ * 4 + 4)], [kk, "consts"], pk,
                              first=(hh_ == 0), last=(hh_ == 1))
            for hh_ in range(2):
                self.P.inherit(sqk, self.P.all_users(sqk + "_%d" % hh_))
            self.rstd_from_ps(rs, pst[:, 0:n], pk, rsk, float(D))
            for c in range(8):
                self.dve(lambda e, c=c, rs=rs, n0=n0, n=n: e.scalar_tensor_tensor(
                    out_ap[:, c, n0:n0 + n], x_ap[:, c, n0:n0 + n], g_ap[:, c:c + 1], rs, ALU.mult, ALU.mult),
                    list(xkeys[tt]) + [rsk, "consts"], [okeys[tt]])
            A.release(m)

    def setup(self):
        nc, P, L, NB = self.nc, self.P, self.L, self.NB
        i = self.i
        self.psb = [self.es.enter_context(nc.psum_tensor("psb%d" % k, [128, 512], F32)) for k in range(8)]
        self.wslots = [self.sb("wslot%d" % k, [128, 4608], BF16) for k in range(4)]
        self.x = self.sb("x", [128, 8, 1024], F32)
        self.xn = self.sb("xn", [128, 8, 1024], BF16)
        self.ident = self.sb("ident", [128, 128], F32)
        self.ones_bf = self.sb("ones_bf", [128, 128], BF16)
        self.ones_f = self.sb("ones_f", [128, 128], F32)
        self.eps_ap = self.sb("eps_ap", [128, 1], F32)
        self.u64 = self.sb("u64", [64, 64], F32)
        self.l64 = self.sb("l64", [64, 64], F32)
        self.mask64 = self.sb("mask64", [64, 64], F32)
        self.bt = self.sb("bt", [128, 2, 4, 128], F32)
        self.bt0 = self.sb("bt0", [128, 2, 4, 128], F32)
        self.alibi_dec = self.sb("alibi_dec", [128, 16, 4], F32)
        self.sel = self.sb("sel", [16, 128], F32)
        self.selT = self.sb("selT", [128, 16], F32)
        self.g4 = self.sb("g4", [128, 4, L, 8], F32)
        self.bgT = self.sb("bgT", [48, L, 4], F32)
        self.nbgT = self.sb("nbgT", [48, L, 4], F32)
        self.gout = self.sb("gout", [96, L], F32)
        self.convw = self.sb("convw", [96, L, 4, 4], F32)
        self.rg4 = self.sb("rg4", [96, 4, L, 4], F32)
        self.c8 = self.sb("c8", [96, L, 4], F32)
        self.c16 = self.sb("c16", [96, L, 4], F32)
        self.g64 = self.sb("g64", [64, 4, L], F32)
        self.g64row = self.sb("g64row", [128, 4, L, 64], F32)
        self.sinks = self.sb("sinks", [128, L, 4], F32)
        self.esink = self.sb("esink", [128, L, 4], F32)
        self.w2g = self.sb("w2g", [16, L, 192], BF16)
        self.bgrow = self.sb("bgrow", [1, L, 192], BF16)
        self.wa = self.sb("wa", [96, L, 4, 96], BF16)
        self.wx = self.sb("wx", [96, L, 4, 96], BF16)
        ld = self.load
        ld(self.ident[:], i["c_ident"], ["consts"])
        ld(self.u64[:], i["c_u64"], ["consts"])
        ld(self.l64[:], i["c_l64"], ["consts"])
        ld(self.mask64[:], i["c_mask64"], ["consts"])
        ld(self.bt[:], i["c_bt"], ["consts"])
        ld(self.bt0[:], i["c_bt0"], ["consts"])
        ld(self.alibi_dec[:], i["c_alibi_dec"], ["consts"])
        ld(self.sel[:], i["c_sel"], ["consts"])
        ld(self.selT[:], i["c_selT"], ["consts"])
        ld(self.g4[:], i["v_g4"], ["consts"])
        ld(self.bgT[:], i["v_bgT"], ["consts"])
        ld(self.gout[:], i["v_gout"], ["consts"])
        ld(self.convw[:], i["v_convw"], ["consts"])
        ld(self.rg4[:], i["v_rg4"], ["consts"])
        ld(self.g64[:], i["v_g64"], ["consts"])
        ld(self.g64row[:].rearrange("p a l d -> p (a l d)"), i["v_g64row"].partition_broadcast(128), ["consts"])
        ld(self.sinks[:].rearrange("p l h -> p (l h)"), i["v_sinks"].partition_broadcast(128), ["consts"])
        ld(self.w2g[:], i["gla_w_gate2"].rearrange("l r n -> r l n"), ["consts"], q="pool")
        ld(self.bgrow[:].rearrange("p l n -> p (l n)"), i["gla_b_gate"].rearrange("l n -> (l n)").partition_broadcast(1),
           ["consts"], q="pool")
        ld(self.wa[:], i["rg_w_a"].rearrange("l n c d -> c l n d"), ["consts"], q="pool")
        ld(self.wx[:], i["rg_w_x"].rearrange("l n c d -> c l n d"), ["consts"], q="pool")
        dv, ac = self.dve, self.act
        dv(lambda e: e.memset(self.ones_bf[:], 1.0), [], ["consts"])
        dv(lambda e: e.memset(self.ones_f[:], 1.0), [], ["consts"])
        dv(lambda e: e.memset(self.eps_ap[:], EPS), [], ["consts"])
        dv(lambda e: e.tensor_scalar(self.nbgT[:], self.bgT[:], -1.0, None, ALU.mult), ["consts"], ["consts"])
        ac(lambda e: e.activation(self.c8[:], self.rg4[:, 3], AF.Exp, scale=-1.0), ["consts"], ["consts"])
        ac(lambda e: e.activation(self.c8[:], self.c8[:], AF.Ln, bias=1.0, scale=1.0), ["consts"], ["consts"])
        dv(lambda e: e.tensor_scalar(self.c16[:], self.c8[:], -16.0, None, ALU.mult), ["consts"], ["consts"])
        dv(lambda e: e.tensor_scalar(self.c8[:], self.c8[:], -8.0, None, ALU.mult), ["consts"], ["consts"])
        ac(lambda e: e.activation(self.esink[:], self.sinks[:], AF.Exp), ["consts"], ["consts"])
        self.arena_t = self.sb("arena", [128, self.arena_n], F32)
        self.arena = Arena(P, self.arena_t, self.arena_n)

    def prompt_setup(self):
        L = self.L
        self.mkT = self.sb("mkT", [64, L, 4, 256], BF16)
        self.mv = self.sb("mv", [128, L, 2, 256], BF16)
        self.S = self.sb("S", [48, L, 4, 96], F32)
        self.hst = self.sb("hst", [96, L, 4], F32)
        self.convh = self.sb("convh", [96, L, 4, 3], F32)
        self.kTh = self.sb("kTh", [64, L, 2, 128], BF16)
        self.vh = self.sb("vh", [128, L, 128], BF16)
        dv = self.dve
        dv(lambda e: e.memset(self.S[:], 0.0), [], ["S%d" % l for l in range(L)])
        dv(lambda e: e.memset(self.hst[:], 0.0), [], ["hst%d" % l for l in range(L)])
        dv(lambda e: e.memset(self.convh[:], 0.0), [], ["convh%d" % l for l in range(L)])
        dv(lambda e: e.memset(self.kTh[:], 0.0), [], ["kTh%d" % l for l in range(L)])
        dv(lambda e: e.memset(self.vh[:], 0.0), [], ["vh%d" % l for l in range(L)])

    def tok_qknorm(self, src, skey, np_, H, grow, out, okey, tmpname):
        A = self.arena
        sq, sqk = A.alloc(tmpname + "sq", np_, [H, 64], F32)
        ss, ssk = A.alloc(tmpname + "ss", np_, [H], F32)
        self.dve(lambda e: e.tensor_tensor(sq, src, src, ALU.mult), [skey], [sqk])
        self.dve(lambda e: e.tensor_reduce(ss, sq, AX.X, ALU.add), [sqk], [ssk])
        self.act(lambda e: e.activation(ss, ss, AF.Sqrt, bias=self.eps_ap[0:np_, :], scale=1.0 / 64.0), [ssk, "consts"], [ssk])
        self.dve(lambda e: e.reciprocal(ss, ss), [ssk], [ssk])
        self.dve(lambda e: e.tensor_tensor(out, src, ss.unsqueeze(2).broadcast_to([np_, H, 64]), ALU.mult), [skey, ssk], [okey])
        self.dve(lambda e: e.tensor_tensor(out, out, grow.unsqueeze(1).broadcast_to([np_, H, 64]), ALU.mult), [okey, "consts"], [okey])

    def prompt_memkv(self):
        A, L, i = self.arena, self.L, self.i
        m0 = A.mark()
        mt, mtk = A.alloc("memtok", 128, [2, 1024], F32)
        self.load(mt, i["memp"].rearrange("(jb p) d -> p jb d", p=128), [mtk])
        mT, mTk = A.alloc("memT", 128, [8, 256], F32)
        for c in range(8):
            pst, pk = self.ps()
            for jb in range(2):
                self.pe(lambda e, c=c, jb=jb, pst=pst: e.transpose(pst[:, jb * 128:(jb + 1) * 128], mt[:, jb, c * 128:(c + 1) * 128], self.ident[:, :]),
                        [mtk, "consts"], [pk])
            self.act(lambda e, c=c, pst=pst: e.activation(mT[:, c, :], pst[:, 0:256], AF.Copy), [pk], [mTk])
        mn, mnk = A.alloc("memn", 128, [8, 256], BF16)
        for l in range(L):
            self.rmsnorm_fm(mT, [[mTk]], self.g4[:, 2, l, :], mn, [mnk], 256)
            wv, wk = self.wload(i["mem_w_kv"][l].rearrange("(kc p) n -> p kc n", p=128), 128, [8, 512])
            for jb in range(2):
                m1 = A.mark()
                pst, pk = self.ps()
                self.mm_group(pst[:, 0:512], [(mn[:, kc, jb * 128:(jb + 1) * 128], wv[:, kc, :]) for kc in range(8)], [mnk, wk], pk)
                kv, kvk = A.alloc("kvtok", 128, [512], F32)
                self.act(lambda e, pst=pst, kv=kv: e.activation(kv, pst[:, 0:512], AF.Copy), [pk], [kvk])
                kn, knk = A.alloc("kn", 128, [4, 64], F32)
                self.tok_qknorm(kv[:, 0:256].rearrange("p (h d) -> p h d", h=4), kvk, 128, 4, self.g64row[:, 3, l, :], kn, knk, "mk")
                self.store(self.o["mem_k_p"][l, jb * 128:(jb + 1) * 128, :], kn.rearrange("p h d -> p (h d)"), [knk], "o_memk%d_%d" % (l, jb))
                self.store(self.o["mem_v_p"][l, jb * 128:(jb + 1) * 128, :], kv[:, 256:512], [kvk], "o_memv%d_%d" % (l, jb))
                self.act(lambda e, l=l, jb=jb, kv=kv: e.activation(self.mv[:, l, jb, :], kv[:, 256:512], AF.Copy), [kvk], ["mv%d" % l])
                pst2, pk2 = self.ps()
                for h in range(4):
                    self.pe(lambda e, h=h, pst2=pst2, kn=kn: e.transpose(pst2[0:64, h * 128:(h + 1) * 128], kn[:, h, :], self.ident[:, :]),
                            [knk, "consts"], [pk2])
                self.act(lambda e, l=l, jb=jb, pst2=pst2: e.activation(
                    self.mkT[:, l, :, jb * 128:(jb + 1) * 128], pst2[0:64, :].rearrange("p (h j) -> p h j", h=4), AF.Copy),
                    [pk2], ["mkT%d" % l])
                A.release(m1)
        A.release(m0)

    def gla_half(self, l, hf, w1, w1k, w3, w3k, w2, w2k, last_st, state_only=False):
        A, P = self.arena, self.P
        t0 = hf * 512
        xk = "xn%d" % hf
        xn = self.xn
        m0 = A.mark()
        glr, glrk = A.alloc("glrT", 16, [512], BF16)
        pst, pk = self.ps()
        self.mm_group(pst[0:16, :], [(w3[:, kc, 0:16], xn[:, kc, t0:t0 + 512]) for kc in range(8)], [w3k, xk], pk)
        self.act(lambda e, pst=pst: e.activation(glr, pst[0:16, :], AF.Copy), [pk], [glrk])
        sg, sgk = A.alloc("sg", 96, [4, 512], BF16)
        for h in range(0 if state_only else 4):
            pst, pk = self.ps()
            self.mm_group(pst[0:96, :], [(w3[:, kc, 16 + h * 96:16 + (h + 1) * 96], xn[:, kc, t0:t0 + 512]) for kc in range(8)], [w3k, xk], pk)
            self.act(lambda e, pst=pst, h=h: e.activation(sg[:, h, :], pst[0:96, :], AF.Silu), [pk], [sgk])
        sp, spk = A.alloc("sp", 64, [8, 192], F32)
        for cp in range(4):
            pst, pk = self.ps()
            for q in range(2):
                c = cp * 2 + q
                self.mm_group(pst[0:64, q * 192:(q + 1) * 192],
                              [(glr[:, c * 64:(c + 1) * 64], self.w2g[:, l, :]), (self.ones_bf[0:1, 0:64], self.bgrow[0:1, l, :])],
                              [glrk, "consts"], pk)
            spv = sp[:, cp * 2:cp * 2 + 2, :].rearrange("p a b -> p (a b)")
            self.act(lambda e, pst=pst, spv=spv: e.activation(spv, pst[0:64, 0:384], AF.Exp, scale=-1.0), [pk], [spk])
            self.act(lambda e, spv=spv: e.activation(spv, spv, AF.Ln, bias=1.0, scale=1.0), [spk], [spk])
        kh, khk = A.alloc("khat", 64, [8, 192], BF16)
        vt, vtk = A.alloc("vtok", 64, [8, 384], BF16)
        for c in range(8):
            pst, pk = self.ps()
            self.mm_group(pst[0:64, 0:192], [(self.l64[:, :], sp[:, c, :])], [spk, "consts"], pk)
            m1 = A.mark()
            ek, ekk = A.alloc("ek", 64, [192], F32)
            self.act(lambda e, pst=pst, ek=ek: e.activation(ek, pst[0:64, 0:192], AF.Exp), [pk], [ekk])
            pst2, pk2 = self.ps()
            self.mm_group(pst2[0:64, 0:192], [(xn[:, kc, t0 + c * 64:t0 + (c + 1) * 64], w2[:, kc, 0:192]) for kc in range(8)], [w2k, xk], pk2)
            self.dve(lambda e, c=c, pst2=pst2, ek=ek: e.tensor_tensor(kh[:, c, :], pst2[0:64, 0:192], ek, ALU.mult), [pk2, ekk], [khk])
            pst3, pk3 = self.ps()
            self.mm_group(pst3[0:64, 0:384], [(xn[:, kc, t0 + c * 64:t0 + (c + 1) * 64], w2[:, kc, 192:576]) for kc in range(8)], [w2k, xk], pk3)
            self.act(lambda e, c=c, pst3=pst3: e.activation(vt[:, c, :], pst3[0:64, 0:384], AF.Copy), [pk3], [vtk])
            A.release(m1)
        qt, qtk = A.alloc("qtil", 48, [4, 512], BF16)
        kt, ktk = A.alloc("ktil", 48, [4, 512], BF16)
        Dk, Dkk = A.alloc("Dk", 48, [4, 8], F32)
        if state_only:
            pst, pk = self.ps()
            for h in range(4):
                for c in range(8):
                    self.mm_group(pst[0:48, h * 8 + c:h * 8 + c + 1], [(sp[:, c, h * 48:(h + 1) * 48], self.u64[:, 63:64])], [spk, "consts"], pk)
            lg, lgk = A.alloc("lg", 48, [4, 8], F32)
            self.act(lambda e, pst=pst: e.activation(lg.rearrange("p h c -> p (h c)"), pst[0:48, 0:32], AF.Copy), [pk], [lgk])
            ls, lsk = A.alloc("ls", 48, [4], F32)
            self.dve(lambda e: e.tensor_reduce(ls, lg, AX.X, ALU.add), [lgk], [lsk])
            self.dve(lambda e: e.tensor_tensor(self.Ltot[:, :], self.Ltot[:, :], ls, ALU.add), [lsk, "Ltot"], ["Ltot"])
            self.act(lambda e: e.activation(Dk, lg, AF.Exp), [lgk], [Dkk])
        for h in range(0 if state_only else 4):
            m1 = A.mark()
            pst, pk = self.ps()
            for c in range(8):
                self.mm_group(pst[0:48, c * 64:(c + 1) * 64], [(sp[:, c, h * 48:(h + 1) * 48], self.u64[:, :])], [spk, "consts"], pk)
            ep, epk = A.alloc("Ep", 48, [512], F32)
            em, emk = A.alloc("Em", 48, [512], F32)
            self.act(lambda e, pst=pst, ep=ep: e.activation(ep, pst[0:48, :], AF.Exp), [pk], [epk])
            self.act(lambda e, pst=pst, em=em: e.activation(em, pst[0:48, :], AF.Exp, scale=-1.0), [pk], [emk])
            self.act(lambda e, ep=ep, h=h: e.activation(Dk[:, h, :], ep.rearrange("p (c t) -> p c t", c=8)[:, :, 63], AF.Copy), [epk], [Dkk])
            pq, pqk = self.ps()
            self.mm_group(pq[0:48, :], [(w1[:, kc, h * 48:(h + 1) * 48], xn[:, kc, t0:t0 + 512]) for kc in range(8)], [w1k, xk], pqk)
            self.dve(lambda e, pq=pq, ep=ep, h=h: e.tensor_tensor(qt[:, h, :], pq[0:48, :], ep, ALU.mult), [pqk, epk], [qtk])
            pk_, pkk = self.ps()
            self.mm_group(pk_[0:48, :], [(w1[:, kc, 192 + h * 48:192 + (h + 1) * 48], xn[:, kc, t0:t0 + 512]) for kc in range(8)], [w1k, xk], pkk)
            self.dve(lambda e, pk_=pk_, em=em, h=h: e.tensor_tensor(kt[:, h, :], pk_[0:48, :], em, ALU.mult), [pkk, emk], [ktk])
            A.release(m1)
        oraw, ork = A.alloc("oraw", 96, [4, 512], F32)
        Sk = "S%d" % l
        S = self.S[:, l]
        for c in range(8):
            if state_only:
                psu, psuk = self.ps()
                for h in range(4):
                    self.mm_group(psu[0:48, h * 96:(h + 1) * 96], [(kh[:, c, h * 48:(h + 1) * 48], vt[:, c, h * 96:(h + 1) * 96])], [khk, vtk], psuk)
                self.dve(lambda e, c=c: e.tensor_tensor(S, S, Dk[:, :, c:c + 1].broadcast_to([48, 4, 96]), ALU.mult), [Sk, Dkk], [Sk])
                self.dve(lambda e, psu=psu: e.tensor_tensor(S, S, psu[0:48, 0:384].rearrange("p (h v) -> p h v", h=4), ALU.add), [Sk, psuk], [Sk])
                continue
            m1 = A.mark()
            sbf, sbk = A.alloc("Sbf", 48, [4, 96], BF16)
            self.act(lambda e, sbf=sbf: e.activation(sbf, S, AF.Copy, scale=GLA_SCALE), [Sk], [sbk])
            pa, pak = self.ps()
            for h in range(4):
                self.mm_group(pa[0:64, h * 64:(h + 1) * 64], [(kt[:, h, c * 64:(c + 1) * 64], qt[:, h, c * 64:(c + 1) * 64])], [ktk, qtk], pak)
            atm, atk = A.alloc("ATm", 64, [4, 64], BF16)
            self.dve(lambda e, pa=pa, atm=atm: e.tensor_tensor(
                atm, pa[0:64, 0:256].rearrange("p (h t) -> p h t", h=4), self.mask64[:, :].unsqueeze(1).broadcast_to([64, 4, 64]), ALU.mult),
                [pak, "consts"], [atk])
            po, pok = self.ps()
            for h in range(4):
                self.mm_group(po[0:96, h * 64:(h + 1) * 64],
                              [(vt[:, c, h * 96:(h + 1) * 96], atm[:, h, :]), (sbf[:, h, :], qt[:, h, c * 64:(c + 1) * 64])],
                              [vtk, atk, sbk, qtk], pok)
            self.act(lambda e, po=po, c=c: e.activation(oraw[:, :, c * 64:(c + 1) * 64], po[0:96, 0:256].rearrange("p (h t) -> p h t", h=4), AF.Copy),
                     [pok], [ork])
            psu, psuk = self.ps()
            for h in range(4):
                self.mm_group(psu[0:48, h * 96:(h + 1) * 96], [(kh[:, c, h * 48:(h + 1) * 48], vt[:, c, h * 96:(h + 1) * 96])], [khk, vtk], psuk)
            self.dve(lambda e, c=c: e.tensor_tensor(S, S, Dk[:, :, c:c + 1].broadcast_to([48, 4, 96]), ALU.mult), [Sk, Dkk], [Sk])
            self.dve(lambda e, psu=psu: e.tensor_tensor(S, S, psu[0:48, 0:384].rearrange("p (h v) -> p h v", h=4), ALU.add), [Sk, psuk], [Sk])
            A.release(m1)
        if state_only:
            A.release(m0)
            return
        self.dbg_store("dbg_oraw%d" % hf, oraw, ork)
        self.dbg_store("dbg_qt%d" % hf, qt, qtk, BF16)
        self.dbg_store("dbg_kt%d" % hf, kt, ktk, BF16)
        self.dbg_store("dbg_vt%d" % hf, vt, vtk, BF16)
        self.dbg_store("dbg_kh%d" % hf, kh, khk, BF16)
        self.dbg_store("dbg_Dk%d" % hf, Dk, Dkk)
        if last_st and hf == 1:
            self.store(self.o["gla_S_p"][l].rearrange("h k v -> k h v"), S, [Sk], "o_glaS%d" % l)
        sqo, sqok = A.alloc("sqo", 96, [4, 512], BF16)
        self.act(lambda e: e.activation(sqo, oraw, AF.Square), [ork], [sqok])
        for h in range(4):
            m1 = A.mark()
            pst, pk = self.ps()
            self.mm_group(pst[0:96, :], [(self.ones_bf[0:96, 0:96], sqo[:, h, :])], [sqok, "consts"], pk)
            rs, rsk = A.alloc("rso", 96, [512], F32)
            self.rstd_from_ps(rs, pst[0:96, :], pk, rsk, 96.0)
            tmp, tmpk = A.alloc("otmp", 96, [512], F32)
            self.dve(lambda e, h=h, rs=rs, tmp=tmp: e.scalar_tensor_tensor(tmp, oraw[:, h, :], self.gout[:, l:l + 1], rs, ALU.mult, ALU.mult),
                     [ork, rsk, "consts"], [tmpk])
            self.dve(lambda e, h=h, tmp=tmp, mg=self.mixg: e.tensor_tensor(mg[:, h, t0:t0 + 512], tmp, sg[:, h, :], ALU.mult), [tmpk, sgk], [self.mixgk])
            if h == 0:
                self.dbg_store("dbg_tmp%d" % hf, tmp, tmpk)
                self.dbg_store("dbg_rs%d" % hf, rs, rsk)
            A.release(m1)
        self.dbg_store("dbg_sg%d" % hf, sg, sgk, BF16)
        self.dbg_store("dbg_mixg%d" % hf, self.mixg[:, :, t0:t0 + 512], self.mixgk, BF16)
        A.release(m0)

    def rg_phase(self, l, w4, w4k, w5, w5k, last_st, state_only=False):
        A = self.arena
        xn = self.xn
        m0 = A.mark()
        rxb = [A.alloc("rxT%d" % p, 96, [1027], F32) for p in range(2)]
        gyb = [A.alloc("gy%d" % p, 96, [1024], BF16) for p in range(2)]
        cw = self.convw
        hs = "hst%d" % l

        def stage_a(n):
            rx, rxk = rxb[n % 2]
            gy, gyk = gyb[n % 2]
            self.act(lambda e: e.activation(rx[:, 0:3], self.convh[:, l, n, :], AF.Copy), ["convh%d" % l], [rxk])
            for tt in range(2):
                pst, pk = self.ps()
                self.mm_group(pst[0:96, :], [(w4[:, kc, n * 96:(n + 1) * 96], xn[:, kc, tt * 512:(tt + 1) * 512]) for kc in range(8)], [w4k, "xn%d" % tt], pk)
                self.act(lambda e, pst=pst, tt=tt: e.activation(rx[:, 3 + tt * 512:3 + (tt + 1) * 512], pst[0:96, :], AF.Copy), [pk], [rxk])
                if state_only:
                    continue
                pst2, pk2 = self.ps()
                self.mm_group(pst2[0:96, :], [(w5[:, kc, n * 96:(n + 1) * 96], xn[:, kc, tt * 512:(tt + 1) * 512]) for kc in range(8)], [w5k, "xn%d" % tt], pk2)
                self.act(lambda e, pst2=pst2, tt=tt: e.activation(gy[:, tt * 512:(tt + 1) * 512], pst2[0:96, :], AF.Gelu_apprx_tanh), [pk2], [gyk])
            self.act(lambda e: e.activation(self.convh[:, l, n, :], rx[:, 1024:1027], AF.Copy), [rxk], ["convh%d" % l])

        xcd = [A.alloc("xc%d" % p, 96, [1024], F32) for p in range(2)]
        xcbd = [A.alloc("xcb%d" % p, 96, [1024], BF16) for p in range(2)]
        r, rk = A.alloc("r", 96, [1024], F32)
        ig, igk = A.alloc("ig", 96, [1024], F32)
        a, ak = A.alloc("a", 96, [1024], F32)
        s2, s2k = A.alloc("s2", 96, [1024], F32)
        hh, hk = A.alloc("h", 96, [1024], F32)
        rsum, rsumk = A.alloc("rsum", 96, [1], F32)

        def stage_b1(n):
            rx, rxk = rxb[n % 2]
            xc, xck = xcd[n % 2]
            xcb, xcbk = xcbd[n % 2]
            self.dve(lambda e: e.tensor_scalar(xc, rx[:, 0:1024], cw[:, l, n, 0:1], self.rg4[:, 0, l, n:n + 1], ALU.mult, ALU.add), [rxk, "consts"], [xck])
            for j in range(1, 4):
                self.dve(lambda e, j=j: e.scalar_tensor_tensor(xc, rx[:, j:j + 1024], cw[:, l, n, j:j + 1], xc, ALU.mult, ALU.add), [rxk, xck, "consts"], [xck])
            self.act(lambda e: e.activation(xcb, xc, AF.Copy), [xck], [xcbk])

        def stage_gates(n):
            xcb, xcbk = xcbd[n % 2]
            for tt in range(2):
                pst, pk = self.ps()
                self.mm_group(pst[0:96, :], [(self.wa[:, l, n, :], xcb[:, tt * 512:(tt + 1) * 512])], [xcbk, "consts"], pk)
                self.act(lambda e, pst=pst, tt=tt: e.activation(r[:, tt * 512:(tt + 1) * 512], pst[0:96, :], AF.Sigmoid, bias=self.rg4[:, 1, l, n:n + 1]), [pk, "consts"], [rk])
                pst2, pk2 = self.ps()
                self.mm_group(pst2[0:96, :], [(self.wx[:, l, n, :], xcb[:, tt * 512:(tt + 1) * 512])], [xcbk, "consts"], pk2)
                self.act(lambda e, pst2=pst2, tt=tt: e.activation(ig[:, tt * 512:(tt + 1) * 512], pst2[0:96, :], AF.Sigmoid, bias=self.rg4[:, 2, l, n:n + 1]), [pk2, "consts"], [igk])
            self.act(lambda e: e.activation(a, r, AF.Exp, scale=self.c8[:, l, n:n + 1]), [rk, "consts"], [ak])
            self.act(lambda e: e.activation(s2, r, AF.Exp, scale=self.c16[:, l, n:n + 1]), [rk, "consts"], [s2k])
            self.act(lambda e: e.activation(s2, s2, AF.Sqrt, bias=1.0, scale=-1.0), [s2k], [s2k])

        def stage_tail(n):
            rx, rxk = rxb[n % 2]
            gy, gyk = gyb[n % 2]
            xc, xck = xcd[n % 2]
            self.dve(lambda e: e.tensor_tensor(ig, ig, xc, ALU.mult), [igk, xck], [igk])
            self.dve(lambda e: e.tensor_tensor(ig, ig, s2, ALU.mult), [igk, s2k], [igk])
            self.dve(lambda e: e.tensor_tensor_scan(hh, a, ig, self.hst[:, l, n:n + 1], ALU.mult, ALU.add), [ak, igk, hs], [hk])
            self.act(lambda e: e.activation(self.hst[:, l, n:n + 1], hh[:, 1023:1024], AF.Copy), [hk], [hs])
            if state_only:
                self.dve(lambda e: e.tensor_reduce(rsum, r, AX.X, ALU.add), [rk], [rsumk])
                self.dve(lambda e: e.tensor_tensor(self.racc[:, n:n + 1], self.racc[:, n:n + 1], rsum, ALU.add), [rsumk, "racc"], ["racc"])
                return
            self.dve(lambda e, mr=self.mixr: e.tensor_tensor(mr[:, n, :], hh, gy, ALU.mult), [hk, gyk], [self.mixrk])
            if last_st:
                self.store(self.o["rg_conv_p"][l, :, n * 96:(n + 1) * 96].rearrange("j c -> c j"), rx[:, 1024:1027], [rxk], "o_rgconv%d_%d" % (l, n), slow=True)
                self.store(self.o["rg_h_p"][l, n * 96:(n + 1) * 96].rearrange("(c o) -> c o", o=1), hh[:, 1023:1024], [hk], "o_rgh%d_%d" % (l, n), slow=True)

        stage_a(0)
        stage_b1(0)
        for n in range(4):
            if n + 1 < 4:
                stage_a(n + 1)
            stage_gates(n)
            if n + 1 < 4:
                stage_b1(n + 1)
            stage_tail(n)
        A.release(m0)

    def fm_qknorm_s1(self, pst, pk, n, tag):
        A = self.arena
        raw, rawk = A.alloc(tag + "raw", 64, [n], F32)
        sq, sqk = A.alloc(tag + "sq", 64, [n], BF16)
        self.act(lambda e: e.activation(raw, pst[0:64, 0:n], AF.Copy), [pk], [rawk])
        self.act(lambda e: e.activation(sq, pst[0:64, 0:n], AF.Square), [pk], [sqk])
        return raw, rawk, sq, sqk

    def fm_qknorm_s2(self, st1, n, gain_ap, out_ap, okey, tag):
        A = self.arena
        raw, rawk, sq, sqk = st1
        p2, p2k = self.ps()
        self.mm_group(p2[0:64, 0:n], [(self.ones_bf[0:64, 0:64], sq)], [sqk, "consts"], p2k)
        m = A.mark()
        rs, rsk = A.alloc(tag + "rs", 64, [n], F32)
        self.rstd_from_ps(rs, p2[0:64, 0:n], p2k, rsk, 64.0)
        self.dve(lambda e: e.scalar_tensor_tensor(out_ap, raw, gain_ap, rs, ALU.mult, ALU.mult), [rawk, rsk, "consts"], [okey])
        A.release(m)

    def swa_phase(self, l, st, w6, w6k, last_st):
        A = self.arena
        xn = self.xn
        m0 = A.mark()
        qT, qTk = A.alloc("sqT", 64, [4, 1024], BF16)
        kT, kTk = A.alloc("skT", 64, [2, 1152], BF16)
        vt, vtk = A.alloc("svt", 128, [9, 128], BF16)
        self.act(lambda e: e.activation(kT[:, :, 0:128], self.kTh[:, l], AF.Copy), ["kTh%d" % l], [kTk])
        self.act(lambda e: e.activation(vt[:, 0, :], self.vh[:, l, :], AF.Copy), ["vh%d" % l], [vtk])
        for tt in range(2):
            mq = A.mark()
            items = []
            for j in range(6):
                c0 = j * 64 if j < 4 else 256 + (j - 4) * 64
                pst, pk = self.ps()
                self.mm_group(pst[0:64, :], [(w6[:, kc, c0:c0 + 64], xn[:, kc, tt * 512:(tt + 1) * 512]) for kc in range(8)], [w6k, "xn%d" % tt], pk)
                items.append(self.fm_qknorm_s1(pst, pk, 512, "sq%d" % j))
            for j in range(6):
                if j < 4:
                    self.fm_qknorm_s2(items[j], 512, self.g64[:, 0, l:l + 1], qT[:, j, tt * 512:(tt + 1) * 512], qTk, "sq")
                else:
                    self.fm_qknorm_s2(items[j], 512, self.g64[:, 1, l:l + 1], kT[:, j - 4, 128 + tt * 512:128 + (tt + 1) * 512], kTk, "sk")
            A.release(mq)
        for i in range(8):
            pst, pk = self.ps()
            self.mm_group(pst[:, 0:128], [(xn[:, kc, i * 128:(i + 1) * 128], w6[:, kc, 384:512]) for kc in range(8)], [w6k, "xn%d" % (i // 4)], pk)
            self.act(lambda e, pst=pst, i=i: e.activation(vt[:, 1 + i, :], pst[:, 0:128], AF.Copy), [pk], [vtk])
        if last_st:
            m1 = A.mark()
            pst, pk = self.ps()
            self.mm_group(pst[:, 0:256], [(xn[:, kc, 896:1024], w6[:, kc, 256:512]) for kc in range(8)], [w6k, "xn1"], pk)
            kvr, kvrk = A.alloc("kvr", 128, [256], F32)
            self.act(lambda e, pst=pst: e.activation(kvr, pst[:, 0:256], AF.Copy), [pk], [kvrk])
            kn, knk = A.alloc("knr", 128, [2, 64], F32)
            self.tok_qknorm(kvr[:, 0:128].rearrange("p (h d) -> p h d", h=2), kvrk, 128, 2, self.g64row[:, 1, l, :], kn, knk, "swk")
            self.store(self.o["swa_k_p"][l], kn.rearrange("p h d -> p (h d)"), [knk], "o_swak%d" % l)
            self.store(self.o["swa_v_p"][l], kvr[:, 128:256], [kvrk], "o_swav%d" % l)
            A.release(m1)
        self.act(lambda e: e.activation(self.kTh[:, l], kT[:, :, 1024:1152], AF.Copy), [kTk], ["kTh%d" % l])
        self.act(lambda e: e.activation(self.vh[:, l, :], vt[:, 8, :], AF.Copy), [vtk], ["vh%d" % l])
        esk = self.esink
        sbb = [A.alloc("ssb%d" % p, 128, [2, 512], F32) for p in range(2)]
        ptb = [A.alloc("sPT%d" % p, 128, [2, 4, 128], BF16) for p in range(2)]

        def blk_scores(i):
            bt = self.bt0 if (st == 0 and i == 0) else self.bt
            sb_, sbk = sbb[i % 2]
            pt, ptk = ptb[i % 2]
            pa, pak = self.ps()
            pb, pbk = self.ps()
            for h in range(4):
                kv = h // 2
                self.mm_group(pa[:, h * 128:(h + 1) * 128], [(kT[:, kv, i * 128:(i + 1) * 128], qT[:, h, i * 128:(i + 1) * 128])], [kTk, qTk], pak)
                self.mm_group(pb[:, h * 128:(h + 1) * 128], [(kT[:, kv, (i + 1) * 128:(i + 2) * 128], qT[:, h, i * 128:(i + 1) * 128])], [kTk, qTk], pbk)
            self.dve(lambda e: e.scalar_tensor_tensor(sb_[:, 0, :], pa[:, :], 0.125, bt[:, 0].rearrange("p h t -> p (h t)"), ALU.mult, ALU.add), [pak, "consts"], [sbk])
            self.dve(lambda e: e.scalar_tensor_tensor(sb_[:, 1, :], pb[:, :], 0.125, bt[:, 1].rearrange("p h t -> p (h t)"), ALU.mult, ALU.add), [pbk, "consts"], [sbk])
            self.act(lambda e: e.activation(pt.rearrange("p a h t -> p (a h t)"), sb_.rearrange("p a n -> p (a n)"), AF.Exp), [sbk], [ptk])

        def blk_pv(i):
            pt, ptk = ptb[i % 2]
            m1 = A.mark()
            po, pok = self.ps()
            pd, pdk = self.ps()
            for h in range(4):
                kv = h // 2
                self.mm_group(po[0:64, h * 128:(h + 1) * 128],
                              [(vt[:, i, kv * 64:(kv + 1) * 64], pt[:, 0, h, :]), (vt[:, i + 1, kv * 64:(kv + 1) * 64], pt[:, 1, h, :])], [vtk, ptk], pok)
                self.mm_group(pd[0:64, h * 128:(h + 1) * 128],
                              [(self.ones_bf[:, 0:64], pt[:, 0, h, :]), (self.ones_bf[:, 0:64], pt[:, 1, h, :])], [ptk, "consts"], pdk)
            dn, dnk = A.alloc("sdn", 64, [4, 128], F32)
            self.dve(lambda e: e.tensor_tensor(dn, pd[0:64, :].rearrange("p (h t) -> p h t", h=4),
                                               esk[0:64, l, :].unsqueeze(2).broadcast_to([64, 4, 128]), ALU.add), [pdk, "consts"], [dnk])
            self.dve(lambda e: e.reciprocal(dn, dn), [dnk], [dnk])
            self.dve(lambda e, ms=self.mixs: e.tensor_tensor(ms[:, :, i * 128:(i + 1) * 128], po[0:64, :].rearrange("p (h t) -> p h t", h=4), dn, ALU.mult),
                     [pok, dnk], [self.mixsk])
            A.release(m1)

        blk_scores(0)
        for i in range(8):
            if i + 1 < 8:
                blk_scores(i + 1)
            blk_pv(i)
        A.release(m0)

    def proj_residual(self, groups, n_tok_tiles, N, xt=None, xkf=None):
        if xt is None:
            xt = self.x
            xkf = lambda mc, tt: "x_%d_%d" % (mc, tt)
        for tt in range(n_tok_tiles):
            n0 = tt * 512
            n = min(512, N - n0)
            for mc in range(8):
                pst, pk = self.ps()
                pairs = []
                reads = []
                for (w, wk, r, rk, nk) in groups:
                    for k in range(nk):
                        pairs.append((w[:, k, mc * 128:(mc + 1) * 128], r[:, k, n0:n0 + n]))
                    reads += [wk, rk]
                self.mm_group(pst[:, 0:n], pairs, reads, pk)
                xk = xkf(mc, tt)
                self.dve(lambda e, pst=pst, mc=mc, n0=n0, n=n: e.tensor_tensor(xt[:, mc, n0:n0 + n], xt[:, mc, n0:n0 + n], pst[:, 0:n], ALU.add),
                         [pk, xk], [xk])

    def xkeys(self, ntt):
        return [["x_%d_%d" % (mc, tt) for mc in range(8)] for tt in range(ntt)]

    def prompt_layer(self, st, l, last_st):
        A, i = self.arena, self.i
        xn = self.xn
        self.rmsnorm_fm(self.x, self.xkeys(2), self.g4[:, 0, l, :], xn, ["xn0", "xn1"], 1024)
        m0 = A.mark()
        self.mixg, self.mixgk = A.alloc("mixg", 96, [4, 1024], BF16)
        self.mixr, self.mixrk = A.alloc("mixr", 96, [4, 1024], BF16)
        self.mixs, self.mixsk = A.alloc("mixs", 64, [4, 1024], BF16)
        win = i["w_in"][l].rearrange("(kc p) n -> p kc n", p=128)
        w1, w1k = self.wload(win[:, :, C_GQ:C_GQ + 384], 128, [8, 384])
        w2, w2k = self.wload(win[:, :, C_GK:C_GK + 576], 128, [8, 576])
        w3, w3k = self.wload(win[:, :, C_GLR:C_GLR + 400], 128, [8, 400])
        for hf in range(2):
            self.gla_half(l, hf, w1, w1k, w3, w3k, w2, w2k, last_st)
        w4, w4k = self.wload(win[:, :, C_RX:C_RX + 384], 128, [8, 384])
        w5, w5k = self.wload(win[:, :, C_RY:C_RY + 384], 128, [8, 384])
        self.rg_phase(l, w4, w4k, w5, w5k, last_st)
        w6, w6k = self.wload(win[:, :, C_SQ:C_SQ + 512], 128, [8, 512])
        self.swa_phase(l, st, w6, w6k, last_st)
        wo = i["w_out"][l]
        wog, wogk = self.wload(wo[0:384, :].rearrange("(k p) n -> p k n", p=96), 96, [4, 1024])
        wor, work = self.wload(wo[384:768, :].rearrange("(k p) n -> p k n", p=96), 96, [4, 1024])
        wos, wosk = self.wload(wo[768:1024, :].rearrange("(k p) n -> p k n", p=64), 64, [4, 1024])
        self.dbg_store("dbg_mixg_late", self.mixg, self.mixgk, BF16)
        self.dbg_store("dbg_wog", wog, wogk, BF16)
        self.proj_residual([(wog, wogk, self.mixg, self.mixgk, 4), (wor, work, self.mixr, self.mixrk, 4), (wos, wosk, self.mixs, self.mixsk, 4)], 2, 1024)
        A.release(m0)
        self.rmsnorm_fm(self.x, self.xkeys(2), self.g4[:, 1, l, :], xn, ["xn0", "xn1"], 1024)
        m0 = A.mark()
        wq, wqk = self.wload(i["mem_w_q"][l].rearrange("(kc p) n -> p kc n", p=128), 128, [8, 256])
        wmo, wmok = self.wload(i["mem_w_o"][l].rearrange("(k p) n -> p k n", p=64), 64, [4, 1024])
        qm, qmk = A.alloc("qmT", 64, [4, 1024], BF16)
        om, omk = A.alloc("omT", 64, [4, 1024], BF16)
        for tt in range(2):
            mq = A.mark()
            items = []
            for h in range(4):
                pst, pk = self.ps()
                self.mm_group(pst[0:64, :], [(wq[:, kc, h * 64:(h + 1) * 64], xn[:, kc, tt * 512:(tt + 1) * 512]) for kc in range(8)], [wqk, "xn%d" % tt], pk)
                items.append(self.fm_qknorm_s1(pst, pk, 512, "mq%d" % h))
            for h in range(4):
                self.fm_qknorm_s2(items[h], 512, self.g64[:, 2, l:l + 1], qm[:, h, tt * 512:(tt + 1) * 512], qmk, "mq")
            A.release(mq)
        for tt in range(2):
            for h in range(4):
                m1 = A.mark()
                pm, pmk = A.alloc("PmT", 128, [2, 512], BF16)
                for jb in range(2):
                    pst, pk = self.ps()
                    self.mm_group(pst[:, :], [(self.mkT[:, l, h, jb * 128:(jb + 1) * 128], qm[:, h, tt * 512:(tt + 1) * 512])], ["mkT%d" % l, qmk], pk)
                    self.act(lambda e, pst=pst, pm=pm, jb=jb: e.activation(pm[:, jb, :], pst[:, :], AF.Exp, scale=0.125), [pk], [pmk])
                po, pok = self.ps()
                self.mm_group(po[0:64, :], [(self.mv[:, l, jb, h * 64:(h + 1) * 64], pm[:, jb, :]) for jb in range(2)], ["mv%d" % l, pmk], pok)
                pd, pdk = self.ps()
                self.mm_group(pd[0:64, :], [(self.ones_bf[:, 0:64], pm[:, jb, :]) for jb in range(2)], ["consts", pmk], pdk)
                rc, rck = A.alloc("mrc", 64, [512], F32)
                self.dve(lambda e, pd=pd, rc=rc: e.reciprocal(rc, pd[0:64, :]), [pdk], [rck])
                self.dve(lambda e, po=po, rc=rc, h=h, tt=tt: e.tensor_tensor(om[:, h, tt * 512:(tt + 1) * 512], po[0:64, :], rc, ALU.mult), [pok, rck], [omk])
                A.release(m1)
        self.proj_residual([(wmo, wmok, om, omk, 4)], 2, 1024)
        A.release(m0)
        self.rmsnorm_fm(self.x, self.xkeys(2), self.g4[:, 3, l, :], xn, ["xn0", "xn1"], 1024)
        m0 = A.mark()
        hT, hTk0 = A.alloc("hT", 128, [16, 1024], BF16)
        w1d = i["ffn_w1"][l].rearrange("(kc p) n -> p kc n", p=128)
        w2d = i["ffn_w2"][l]
        for fc in range(2):
            hk = [hTk0 + "_%d" % tt for tt in range(2)]
            if fc == 0:
                for tt in range(2):
                    self.P.inherit(hk[tt], self.P.all_users(hTk0))
            for p4 in range(4):
                wp, wpk = self.wload(w1d[:, :, fc * 2048 + p4 * 512:fc * 2048 + (p4 + 1) * 512], 128, [8, 512])
                for tt in range(2):
                    for s4 in range(4):
                        sub = p4 * 4 + s4
                        m1 = A.mark()
                        pst, pk = self.ps()
                        self.mm_group(pst[:, :], [(wp[:, kc, s4 * 128:(s4 + 1) * 128], xn[:, kc, tt * 512:(tt + 1) * 512]) for kc in range(8)], [wpk, "xn%d" % tt], pk)
                        sq, sqk = A.alloc("fsq", 128, [512], F32)
                        self.act(lambda e, pst=pst, sq=sq: e.activation(sq, pst[:, :], AF.Square), [pk], [sqk])
                        self.dve(lambda e, pst=pst, sq=sq, sub=sub, tt=tt: e.scalar_tensor_tensor(
                            hT[:, sub, tt * 512:(tt + 1) * 512], pst[:, :], 0.0, sq, ALU.is_gt, ALU.mult), [pk, sqk], [hk[tt]])
                        A.release(m1)
            w2p = []
            for p4 in range(4):
                r0 = fc * 2048 + p4 * 512
                w2p.append(self.wload(w2d[r0:r0 + 512, :].rearrange("(k p) n -> p k n", p=128), 128, [4, 1024]))
            for tt in range(2):
                for mc in range(8):
                    pst, pk = self.ps()
                    pairs = []
                    for p4 in range(4):
                        for k in range(4):
                            pairs.append((w2p[p4][0][:, k, mc * 128:(mc + 1) * 128], hT[:, p4 * 4 + k, tt * 512:(tt + 1) * 512]))
                    self.mm_group(pst[:, :], pairs, [w2p[p4][1] for p4 in range(4)] + [hk[tt]], pk)
                    xk = "x_%d_%d" % (mc, tt)
                    self.dve(lambda e, pst=pst, mc=mc, tt=tt: e.tensor_tensor(
                        self.x[:, mc, tt * 512:(tt + 1) * 512], self.x[:, mc, tt * 512:(tt + 1) * 512], pst[:, :], ALU.add), [pk, xk], [xk])
        for tt in range(2):
            self.P.inherit(hTk0, self.P.all_users(hTk0 + "_%d" % tt))
        A.release(m0)

    def x_load_tok(self, st):
        A = self.arena
        for i8 in range(8):
            m = A.mark()
            xt, xtk = A.alloc("xtok", 128, [1024], F32)
            self.load(xt, self.i["xp"][st * 1024 + i8 * 128: st * 1024 + (i8 + 1) * 128, :], [xtk])
            for c2 in range(2):
                pst, pk = self.ps()
                for q in range(4):
                    c = c2 * 4 + q
                    self.pe(lambda e, pst=pst, q=q, c=c, xt=xt: e.transpose(pst[:, q * 128:(q + 1) * 128], xt[:, c * 128:(c + 1) * 128], self.ident[:, :]),
                            [xtk, "consts"], [pk])
                tt = i8 // 4
                self.act(lambda e, pst=pst, c2=c2, i8=i8: e.activation(
                    self.x[:, c2 * 4:(c2 + 1) * 4, i8 * 128:(i8 + 1) * 128], pst[:, :].rearrange("p (q t) -> p q t", q=4), AF.Copy),
                    [pk], ["x_%d_%d" % (c2 * 4 + q, tt) for q in range(4)])
            A.release(m)

    def y_store_tok(self, st):
        A = self.arena
        for i8 in range(8):
            m = A.mark()
            yt, ytk = A.alloc("ytok", 128, [1024], F32)
            tt = i8 // 4
            for c2 in range(2):
                pst, pk = self.ps()
                for q in range(4):
                    c = c2 * 4 + q
                    self.pe(lambda e, pst=pst, q=q, c=c, i8=i8: e.transpose(pst[:, q * 128:(q + 1) * 128], self.x[:, c, i8 * 128:(i8 + 1) * 128], self.ident[:, :]),
                            ["x_%d_%d" % (c, tt), "consts"], [pk])
                self.act(lambda e, pst=pst, c2=c2, yt=yt: e.activation(yt[:, c2 * 512:(c2 + 1) * 512], pst[:, :], AF.Copy), [pk], [ytk])
            self.store(self.o["y_prompt"][st * 1024 + i8 * 128: st * 1024 + (i8 + 1) * 128, :], yt, [ytk], "o_y_%d_%d" % (st, i8))
            A.release(m)

    def prompt(self):
        if self.par:
            return self.prompt_P()
        T, L = self.T, self.L
        nst = T // 1024
        self.prompt_setup()
        self.prompt_memkv()
        for st in range(nst):
            self.x_load_tok(st)
            for l in range(L):
                self.prompt_layer(st, l, st == nst - 1)
            self.y_store_tok(st)

    def allx(self):
        return [k for ks in self.xkeys(2) for k in ks]

    def x_load_fm(self, st):
        self.P.dma("sp", lambda e: e.dma_start(out=self.x[:].rearrange("p c t -> p (c t)"), in_=self.xscr[st]),
                   reads=["xscr%d" % st], writes=self.allx())

    def x_store_fm(self, st):
        self.P.dma("sp", lambda e: e.dma_start(out=self.xscr[st], in_=self.x[:].rearrange("p c t -> p (c t)")),
                   reads=self.allx(), writes=["xscr%d" % st])

    def allgather(self, src, dst, rkey, wkey):
        self.P.cc(lambda e: e.collective_compute("AllGather", ALU.bypass, replica_groups=[list(range(8))],
                                                 ins=[src.ap().opt()], outs=[dst.ap().opt()]), reads=[rkey], writes=[wkey])

    def tail_pass(self, l):
        A, i = self.arena, self.i
        nst = self.T // 1024
        m = A.mark()
        xt, xtk = A.alloc("tl_x", 128, [8, 128], F32)
        self.load(xt, self.xscr[nst - 1].rearrange("p (c t) -> p c t", c=8)[:, :, 896:1024], [xtk], reads=["xscr%d" % (nst - 1)])
        xnt, xntk = A.alloc("tl_xn", 128, [8, 128], BF16)
        self.rmsnorm_fm(xt, [[xtk]], self.g4[:, 0, l, :], xnt, [xntk], 128)
        win = i["w_in"][l].rearrange("(kc p) n -> p kc n", p=128)
        w4, w4k = self.wload(win[:, :, C_RX:C_RX + 384], 128, [8, 384])
        w6, w6k = self.wload(win[:, :, C_SK:C_SK + 256], 128, [8, 256])
        pay, payk = A.alloc("tl_pay", 128, [640], F32)
        pst, pk = self.ps()
        self.mm_group(pst[:, 0:384], [(xnt[:, kc, :], w4[:, kc, :]) for kc in range(8)], [xntk, w4k], pk)
        self.act(lambda e: e.activation(pay[:, 0:384], pst[:, 0:384], AF.Copy), [pk], [payk])
        pst2, pk2 = self.ps()
        self.mm_group(pst2[:, 0:256], [(xnt[:, kc, :], w6[:, kc, :]) for kc in range(8)], [xntk, w6k], pk2)
        kvr, kvrk = A.alloc("tl_kvr", 128, [256], F32)
        self.act(lambda e: e.activation(kvr, pst2[:, 0:256], AF.Copy), [pk2], [kvrk])
        self.act(lambda e: e.activation(pay[:, 512:640], pst2[:, 128:256], AF.Copy), [pk2, payk], [payk])
        self.tok_qknorm(kvr[:, 0:128].rearrange("p (h d) -> p h d", h=2), kvrk, 128, 2, self.g64row[:, 1, l, :],
                        pay[:, 384:512].rearrange("p (h d) -> p h d", h=2), payk, "tlk")
        k1, k2 = "ag1in%d" % l, "ag1out%d" % l
        self.P.dma("sp", lambda e: e.dma_start(out=self.ag1_in[l].ap(), in_=pay), reads=[payk], writes=[k1])
        self.allgather(self.ag1_in[l], self.ag1_out[l], k1, k2)
        A.release(m)

    def tail_recv(self, l):
        A = self.arena
        m = A.mark()
        k2 = "ag1out%d" % l
        g1, g1k = A.alloc("tl_g1", 128, [8, 640], F32)
        self.load(g1, self.ag1_out[l].ap().rearrange("(r p) n -> p r n", p=128), [g1k], reads=[k2])
        self.dve(lambda e: e.tensor_tensor(g1, g1, self.prevmask_bc[:, :].unsqueeze(2).broadcast_to([128, 8, 640]), ALU.mult), [g1k, "consts"], [g1k])
        hal, halk = A.alloc("tl_hal", 128, [640], F32)
        self.dve(lambda e: e.tensor_reduce(hal, g1.rearrange("p r n -> p n r"), AX.X, ALU.add), [g1k], [halk])
        self.act(lambda e: e.activation(self.vh[:, l, :], hal[:, 512:640], AF.Copy), [halk], ["vh%d" % l])
        pT, pTk = self.ps()
        for kv in range(2):
            self.pe(lambda e, kv=kv: e.transpose(pT[0:64, kv * 128:(kv + 1) * 128], hal[:, 384 + kv * 64:384 + (kv + 1) * 64], self.ident[:, :]), [halk, "consts"], [pTk])
        self.act(lambda e: e.activation(self.kTh[:, l], pT[0:64, 0:256].rearrange("p (k t) -> p k t", k=2), AF.Copy), [pTk], ["kTh%d" % l])
        pC, pCk = self.ps()
        for n in range(4):
            self.pe(lambda e, n=n: e.transpose(pC[0:96, n * 128:(n + 1) * 128], hal[:, n * 96:(n + 1) * 96], self.ident[:, :]), [halk, "consts"], [pCk])
        src = pC[0:96, :].rearrange("p (n t) -> p n t", n=4)[:, :, 125:128]
        self.act(lambda e: e.activation(self.convh[:, l], src, AF.Copy), [pCk], ["convh%d" % l])
        self.act(lambda e: e.activation(self.convh0[:, :, :], src, AF.Copy), [pCk], ["convh0"])
        A.release(m)

    def exchange2(self, l):
        A = self.arena
        m = A.mark()
        pay, payk = A.alloc("x2_pay", 96, [396], F32)
        Sk, hs = "S%d" % l, "hst%d" % l
        self.dve(lambda e: e.memset(pay, 0.0), [], [payk])
        self.act(lambda e: e.activation(pay[0:48, 0:384], self.S[:, l].rearrange("p h v -> p (h v)"), AF.Copy), [Sk, payk], [payk])
        self.act(lambda e: e.activation(pay[0:48, 384:388], self.Ltot[:, :], AF.Exp), ["Ltot", payk], [payk])
        self.act(lambda e: e.activation(pay[:, 388:392], self.hst[:, l, :], AF.Copy), [hs, payk], [payk])
        self.dve(lambda e: e.tensor_tensor(self.racc[:, :], self.racc[:, :], self.c8[:, l, :], ALU.mult), ["racc", "consts"], ["racc"])
        self.act(lambda e: e.activation(pay[:, 392:396], self.racc[:, :], AF.Exp), ["racc", payk], [payk])
        k1, k2 = "ag2in%d" % l, "ag2out%d" % l
        self.P.dma("sp", lambda e: e.dma_start(out=self.ag2_in[l].ap(), in_=pay), reads=[payk], writes=[k1])
        self.allgather(self.ag2_in[l], self.ag2_out[l], k1, k2)
        g2, g2k = A.alloc("x2_g2", 96, [8, 396], F32)
        self.load(g2, self.ag2_out[l].ap().rearrange("(r p) n -> p r n", p=96), [g2k], reads=[k2])
        pm = self.pmask_bc
        dv = self.dve
        dv(lambda e: e.tensor_tensor(g2[0:48, :, 0:384], g2[0:48, :, 0:384], pm[0:48, :].unsqueeze(2).broadcast_to([48, 8, 384]), ALU.mult), [g2k, "consts"], [g2k])
        for (c0, np_) in ((384, 48), (392, 96)):
            dv(lambda e, c0=c0, np_=np_: e.tensor_scalar(g2[0:np_, :, c0:c0 + 4], g2[0:np_, :, c0:c0 + 4], -1.0, None, ALU.add), [g2k], [g2k])
            dv(lambda e, c0=c0, np_=np_: e.tensor_tensor(g2[0:np_, :, c0:c0 + 4], g2[0:np_, :, c0:c0 + 4], pm[0:np_, :].unsqueeze(2).broadcast_to([np_, 8, 4]), ALU.mult), [g2k, "consts"], [g2k])
            dv(lambda e, c0=c0, np_=np_: e.tensor_scalar(g2[0:np_, :, c0:c0 + 4], g2[0:np_, :, c0:c0 + 4], 1.0, None, ALU.add), [g2k], [g2k])
        dv(lambda e: e.tensor_tensor(g2[:, :, 388:392], g2[:, :, 388:392], pm[0:96, :].unsqueeze(2).broadcast_to([96, 8, 4]), ALU.mult), [g2k, "consts"], [g2k])
        S = self.S[:, l]
        hst = self.hst[:, l, :]
        dv(lambda e: e.memset(S, 0.0), [], [Sk])
        dv(lambda e: e.memset(hst, 0.0), [], [hs])
        for r in range(8):
            dv(lambda e, r=r: e.tensor_tensor(S, S, g2[0:48, r, 384:388].unsqueeze(2).broadcast_to([48, 4, 96]), ALU.mult), [Sk, g2k], [Sk])
            dv(lambda e, r=r: e.tensor_tensor(S, S, g2[0:48, r, 0:384].rearrange("p (h v) -> p h v", h=4), ALU.add), [Sk, g2k], [Sk])
            dv(lambda e, r=r: e.tensor_tensor(hst, hst, g2[:, r, 392:396], ALU.mult), [hs, g2k], [hs])
            dv(lambda e, r=r: e.tensor_tensor(hst, hst, g2[:, r, 388:392], ALU.add), [hs, g2k], [hs])
        self.act(lambda e: e.activation(self.convh[:, l], self.convh0[:, :, :], AF.Copy), ["convh0"], ["convh%d" % l])
        A.release(m)

    def prompt_P(self):
        T, L, i = self.T, self.L, self.i
        nst = T // 1024
        self.prompt_setup()
        self.pmask_bc = self.sb("pmask_bc", [128, 8], F32)
        self.prevmask_bc = self.sb("prevmask_bc", [128, 8], F32)
        self.isfirst = self.sb("isfirst_sb", [128, 1], F32)
        self.convh0 = self.sb("convh0", [96, 4, 3], F32)
        self.Ltot = self.sb("Ltot", [48, 4], F32)
        self.racc = self.sb("racc", [96, 4], F32)
        self.load(self.pmask_bc[:], i["pmask"].partition_broadcast(128), ["consts"])
        self.load(self.prevmask_bc[:], i["prevmask"].partition_broadcast(128), ["consts"])
        self.load(self.isfirst[:], i["isfirst"].partition_broadcast(128), ["consts"])
        self.load(self.bt0[:], i["c_btdelta"], ["consts"])
        self.dve(lambda e: e.scalar_tensor_tensor(self.bt0[:].rearrange("p a h t -> p (a h t)"), self.bt0[:].rearrange("p a h t -> p (a h t)"),
                                                  self.isfirst[:, 0:1], self.bt[:].rearrange("p a h t -> p (a h t)"), ALU.mult, ALU.add), ["consts"], ["consts"])
        self.prompt_memkv()
        for st in range(nst):
            self.x_load_tok(st)
            self.x_store_fm(st)
        for l in range(L):
            Sk, hs = "S%d" % l, "hst%d" % l
            self.tail_pass(l)
            self.dve(lambda e, l=l: e.memset(self.S[:, l], 0.0), [], [Sk])
            self.dve(lambda e, l=l: e.memset(self.hst[:, l, :], 0.0), [], [hs])
            self.dve(lambda e: e.memset(self.Ltot[:, :], 0.0), [], ["Ltot"])
            self.dve(lambda e: e.memset(self.racc[:, :], 0.0), [], ["racc"])
            win = i["w_in"][l].rearrange("(kc p) n -> p kc n", p=128)
            for st in range(nst):
                self.x_load_fm(st)
                self.rmsnorm_fm(self.x, self.xkeys(2), self.g4[:, 0, l, :], self.xn, ["xn0", "xn1"], 1024)
                w2, w2k = self.wload(win[:, :, C_GK:C_GK + 576], 128, [8, 576])
                w3, w3k = self.wload(win[:, :, C_GLR:C_GLR + 400], 128, [8, 400])
                for hf in range(2):
                    self.gla_half(l, hf, None, None, w3, w3k, w2, w2k, False, state_only=True)
                if st == 0:
                    self.tail_recv(l)
                w4, w4k = self.wload(win[:, :, C_RX:C_RX + 384], 128, [8, 384])
                self.rg_phase(l, w4, w4k, None, None, False, state_only=True)
            self.exchange2(l)
            for st in range(nst):
                self.x_load_fm(st)
                self.prompt_layer(st, l, st == nst - 1)
                if l == L - 1:
                    self.y_store_tok(st)
                else:
                    self.x_store_fm(st)

    def decode(self):
        A, i, o, L, NB = self.arena, self.i, self.o, self.L, self.NB
        self.xsT = self.sb("xsT", [128, 8, NB], F32)
        xsT = self.xsT
        idn = self.ident
        m = A.mark()
        xt, xtk = A.alloc("xstok", NB, [1024], F32)
        self.load(xt, i["xs"], [xtk])
        for c2 in range(2):
            pst, pk = self.ps()
            for q in range(4):
                c = c2 * 4 + q
                self.pe(lambda e, pst=pst, q=q, c=c: e.transpose(pst[:, q * NB:(q + 1) * NB], xt[:, c * 128:(c + 1) * 128], idn[0:NB, 0:NB]), [xtk, "consts"], [pk])
            self.act(lambda e, pst=pst, c2=c2: e.activation(xsT[:, c2 * 4:(c2 + 1) * 4, :], pst[:, 0:4 * NB].rearrange("p (q t) -> p q t", q=4), AF.Copy),
                     [pk], ["xs_%d" % (c2 * 4 + q) for q in range(4)])
        A.release(m)
        for l in range(L):
            self.decode_layer(l)
        m = A.mark()
        yt, ytk = A.alloc("ystok", NB, [1024], F32)
        for c2 in range(2):
            pst, pk = self.ps()
            for q in range(4):
                c = c2 * 4 + q
                self.pe(lambda e, pst=pst, q=q, c=c: e.transpose(pst[0:NB, q * 128:(q + 1) * 128], xsT[:, c, :], idn[:, :]), ["xs_%d" % c, "consts"], [pk])
            self.act(lambda e, pst=pst, c2=c2: e.activation(yt[:, c2 * 512:(c2 + 1) * 512], pst[0:NB, :], AF.Copy), [pk], [ytk])
        self.store(o["y_sample"], yt, [ytk], "o_ys")
        A.release(m)

    def dram_copy(self, out_ap, in_ap, key):
        return self.P.dma("sp", lambda e: e.dma_start(out=out_ap, in_=in_ap), reads=[], writes=[key], final=True)

    def tok_to_fm(self, src, skey, np_, width, nblk, dst_ps):
        pst, pk = dst_ps
        for b in range(nblk):
            self.pe(lambda e, b=b: e.transpose(pst[0:width, b * np_:(b + 1) * np_], src[:, b * width:(b + 1) * width], self.ident[0:np_, 0:np_]),
                    [skey, "consts"], [pk])

    def decode_layer(self, l):
        A, i, o, NB = self.arena, self.i, self.o, self.NB
        xn, xk, xsT = self.xn, "xn0", self.xsT
        xskeys = [["xs_%d" % c for c in range(8)]]
        xsk = lambda mc, tt: "xs_%d" % mc
        xnb = lambda kc: xn[:, kc, 0:NB]
        dve, act = self.dve, self.act
        self.rmsnorm_fm(xsT, xskeys, self.g4[:, 0, l, :], xn[:, :, 0:NB], ["xn0"], NB)
        m0 = A.mark()
        mixg, mixgk = A.alloc("dmixg", 96, [4, NB], BF16)
        mixr, mixrk = A.alloc("dmixr", 96, [4, NB], BF16)
        mixs, mixsk = A.alloc("dmixs", 64, [4, NB], BF16)
        win = i["w_in"][l].rearrange("(kc p) n -> p kc n", p=128)
        w1, w1k = self.wload(win[:, :, C_GQ:C_GQ + 384], 128, [8, 384])
        w2, w2k = self.wload(win[:, :, C_GK:C_GK + 576], 128, [8, 576])
        w3, w3k = self.wload(win[:, :, C_GLR:C_GLR + 400], 128, [8, 400])
        m1 = A.mark()
        q32, q32k = A.alloc("q32", 48, [NB, 4], F32)
        k32, k32k = A.alloc("k32", 48, [NB, 4], F32)
        for (dst, dk_, c0) in ((q32, q32k, 0), (k32, k32k, 192)):
            pst, pk = self.ps()
            for h in range(4):
                self.mm_group(pst[0:48, h * NB:(h + 1) * NB], [(w1[:, kc, c0 + h * 48:c0 + (h + 1) * 48], xnb(kc)) for kc in range(8)], [w1k, xk], pk)
            act(lambda e, pst=pst, dst=dst: e.activation(dst.rearrange("p b h -> p h b"), pst[0:48, 0:4 * NB].rearrange("p (h b) -> p h b", h=4), AF.Copy), [pk], [dk_])
        glr, glrk = A.alloc("dglr", 16, [NB], BF16)
        pst, pk = self.ps()
        self.mm_group(pst[0:16, 0:NB], [(w3[:, kc, 0:16], xnb(kc)) for kc in range(8)], [w3k, xk], pk)
        act(lambda e, pst=pst: e.activation(glr, pst[0:16, 0:NB], AF.Copy), [pk], [glrk])
        pst, pk = self.ps()
        for h in range(4):
            self.mm_group(pst[0:48, h * NB:(h + 1) * NB], [(self.w2g[:, l, h * 48:(h + 1) * 48], glr)], [glrk, "consts"], pk)
        Dd, Ddk = A.alloc("Dd", 48, [NB, 4], F32)
        for h in range(4):
            act(lambda e, pst=pst, h=h: e.activation(Dd[:, :, h], pst[0:48, h * NB:(h + 1) * NB], AF.Exp, bias=self.nbgT[:, l, h:h + 1], scale=-1.0), [pk, "consts"], [Ddk])
        act(lambda e: e.activation(Dd, Dd, AF.Ln, bias=1.0, scale=1.0), [Ddk], [Ddk])
        act(lambda e: e.activation(Dd, Dd, AF.Exp, scale=-1.0 / 16.0), [Ddk], [Ddk])
        pst, pk = self.ps()
        self.mm_group(pst[0:NB, 0:384], [(xnb(kc), w2[:, kc, 192:576]) for kc in range(8)], [xk, w2k], pk)
        vtok, vtokk = A.alloc("dvtok", NB, [384], F32)
        act(lambda e, pst=pst: e.activation(vtok, pst[0:NB, 0:384], AF.Copy), [pk], [vtokk])
        scrk = "scr_v%d" % l
        self.P.dma("sp", lambda e: e.dma_start(out=self.scr_v[l].rearrange("(b n) -> b n", b=NB), in_=vtok), reads=[vtokk], writes=[scrk])
        vb, vbk = A.alloc("dvb", 48, [NB * 4, 96], F32)
        self.load(vb.rearrange("p a v -> p (a v)"), self.scr_v[l].partition_broadcast(48), [vbk], reads=[scrk])
        S0, S0k = A.alloc("dS0", 48, [NB, 4, 96], F32)
        self.load(S0, i["state_gla"][l].rearrange("b h k v -> k b h v"), [S0k])
        S3 = S0.rearrange("p b h v -> p (b h) v")
        bc = lambda t: t.rearrange("p b h -> p (b h)").unsqueeze(2).broadcast_to([48, NB * 4, 96])
        dve(lambda e: e.tensor_tensor(vb, vb, bc(k32), ALU.mult), [vbk, k32k], [vbk])
        dve(lambda e: e.tensor_tensor(S3, S3, bc(Dd), ALU.mult), [S0k, Ddk], [S0k])
        dve(lambda e: e.tensor_tensor(S3, S3, vb, ALU.add), [S0k, vbk], [S0k])
        self.store(o["gla_S_s"][l].rearrange("b h k v -> k b h v"), S0, [S0k], "o_glaSs%d" % l)
        g_po, g_pok = self.ps()

        def fn_o(e):
            ins = None
            for b in range(NB):
                for h in range(4):
                    j = b * 4 + h
                    ins = e.matmul(g_po[0:96, j:j + 1], S0[:, b, h, :], q32[:, b, h:h + 1], start=True, stop=True)
            return ins
        self.pe(fn_o, [S0k, q32k], [g_pok])
        o32, o32k = A.alloc("do32", 96, [NB * 4], F32)
        act(lambda e: e.activation(o32, g_po[0:96, 0:NB * 4], AF.Copy, scale=GLA_SCALE), [g_pok], [o32k])
        sqo, sqok = A.alloc("dsqo", 96, [NB * 4], BF16)
        act(lambda e: e.activation(sqo, o32, AF.Square), [o32k], [sqok])
        pn, pnk = self.ps()
        self.mm_group(pn[0:96, 0:NB * 4], [(self.ones_bf[0:96, 0:96], sqo)], [sqok, "consts"], pnk)
        rs, rsk = A.alloc("drs", 96, [NB * 4], F32)
        self.rstd_from_ps(rs, pn[0:96, 0:NB * 4], pnk, rsk, 96.0)
        pg, pgk = self.ps()
        for h in range(4):
            self.mm_group(pg[0:96, h * NB:(h + 1) * NB], [(w3[:, kc, 16 + h * 96:16 + (h + 1) * 96], xnb(kc)) for kc in range(8)], [w3k, xk], pgk)
        sg, sgk = A.alloc("dsg", 96, [4, NB], F32)
        act(lambda e: e.activation(sg.rearrange("p h b -> p (h b)"), pg[0:96, 0:4 * NB], AF.Silu), [pgk], [sgk])
        dve(lambda e: e.scalar_tensor_tensor(o32, o32, self.gout[:, l:l + 1], rs, ALU.mult, ALU.mult), [o32k, rsk, "consts"], [o32k])
        dve(lambda e: e.tensor_tensor(mixg, o32.rearrange("p (b h) -> p h b", h=4), sg, ALU.mult), [o32k, sgk], [mixgk])
        A.release(m1)
        w4, w4k = self.wload(win[:, :, C_RX:C_RX + 384], 128, [8, 384])
        w5, w5k = self.wload(win[:, :, C_RY:C_RY + 384], 128, [8, 384])
        m1 = A.mark()
        rx32, rx32k = A.alloc("drx", 96, [4, NB], F32)
        gy, gyk = A.alloc("dgy", 96, [4, NB], F32)
        for (dst, dk_, w_, wk_, fnc) in ((rx32, rx32k, w4, w4k, AF.Copy), (gy, gyk, w5, w5k, AF.Gelu_apprx_tanh)):
            pst, pk = self.ps()
            for n in range(4):
                self.mm_group(pst[0:96, n * NB:(n + 1) * NB], [(w_[:, kc, n * 96:(n + 1) * 96], xnb(kc)) for kc in range(8)], [wk_, xk], pk)
            act(lambda e, pst=pst, dst=dst, fnc=fnc: e.activation(dst.rearrange("p n b -> p (n b)"), pst[0:96, 0:4 * NB], fnc), [pk], [dk_])
        pst, pk = self.ps()
        self.mm_group(pst[0:NB, 0:384], [(xnb(kc), w4[:, kc, :]) for kc in range(8)], [xk, w4k], pk)
        rxtok, rxtokk = A.alloc("drxtok", NB, [384], F32)
        act(lambda e, pst=pst: e.activation(rxtok, pst[0:NB, 0:384], AF.Copy), [pk], [rxtokk])
        self.store(o["rg_conv_s"][l, :, 2, :], rxtok, [rxtokk], "o_rgconvs_new%d" % l)
        self.dram_copy(o["rg_conv_s"][l, :, 0:2, :], i["state_rg_conv"][l, :, 1:3, :], "o_rgconvs_old%d" % l)
        cbt, cbtk = A.alloc("dcbt", NB, [3 * 384], F32)
        self.load(cbt, i["state_rg_conv"][l].rearrange("b j c -> b (j c)"), [cbtk])
        pcb = self.ps()
        self.tok_to_fm(cbt, cbtk, NB, 96, 12, pcb)
        cbT, cbTk = A.alloc("dcbT", 96, [3, 4, NB], F32)
        act(lambda e: e.activation(cbT.rearrange("p j n b -> p (j n b)"), pcb[0][0:96, 0:12 * NB], AF.Copy), [pcb[1]], [cbTk])
        h0t, h0tk = A.alloc("dh0t", NB, [384], F32)
        self.load(h0t, i["state_rg_h"][l], [h0tk])
        ph0 = self.ps()
        self.tok_to_fm(h0t, h0tk, NB, 96, 4, ph0)
        h0, h0k = A.alloc("dh0", 96, [4, NB], F32)
        act(lambda e: e.activation(h0.rearrange("p n b -> p (n b)"), ph0[0][0:96, 0:4 * NB], AF.Copy), [ph0[1]], [h0k])
        xc, xck = A.alloc("dxc", 96, [4, NB], F32)
        cw = self.convw
        for n in range(4):
            dve(lambda e, n=n: e.tensor_scalar(xc[:, n, :], rx32[:, n, :], cw[:, l, n, 3:4], self.rg4[:, 0, l, n:n + 1], ALU.mult, ALU.add), [rx32k, "consts"], [xck])
            for j in range(3):
                dve(lambda e, n=n, j=j: e.scalar_tensor_tensor(xc[:, n, :], cbT[:, j, n, :], cw[:, l, n, j:j + 1], xc[:, n, :], ALU.mult, ALU.add), [cbTk, xck, "consts"], [xck])
        xcb, xcbk = A.alloc("dxcb", 96, [4, NB], BF16)
        act(lambda e: e.activation(xcb, xc, AF.Copy), [xck], [xcbk])
        r, rk = A.alloc("dr", 96, [4, NB], F32)
        ig, igk = A.alloc("dig", 96, [4, NB], F32)
        for (dst, dk_, w_, kind) in ((r, rk, self.wa, 1), (ig, igk, self.wx, 2)):
            pst, pk = self.ps()
            for n in range(4):
                self.mm_group(pst[0:96, n * NB:(n + 1) * NB], [(w_[:, l, n, :], xcb[:, n, :])], [xcbk, "consts"], pk)
            for n in range(4):
                act(lambda e, pst=pst, dst=dst, n=n, kind=kind: e.activation(dst[:, n, :], pst[0:96, n * NB:(n + 1) * NB], AF.Sigmoid, bias=self.rg4[:, kind, l, n:n + 1]),
                    [pk, "consts"], [dk_])
        a, ak = A.alloc("da", 96, [4, NB], F32)
        s2, s2k = A.alloc("ds2", 96, [4, NB], F32)
        for n in range(4):
            act(lambda e, n=n: e.activation(a[:, n, :], r[:, n, :], AF.Exp, scale=self.c8[:, l, n:n + 1]), [rk, "consts"], [ak])
            act(lambda e, n=n: e.activation(s2[:, n, :], r[:, n, :], AF.Exp, scale=self.c16[:, l, n:n + 1]), [rk, "consts"], [s2k])
        act(lambda e: e.activation(s2, s2, AF.Sqrt, bias=1.0, scale=-1.0), [s2k], [s2k])
        dve(lambda e: e.tensor_tensor(ig, ig, xc, ALU.mult), [igk, xck], [igk])
        dve(lambda e: e.tensor_tensor(ig, ig, s2, ALU.mult), [igk, s2k], [igk])
        dve(lambda e: e.tensor_tensor(a, a, h0, ALU.mult), [ak, h0k], [ak])
        dve(lambda e: e.tensor_tensor(a, a, ig, ALU.add), [ak, igk], [ak])
        dve(lambda e: e.tensor_tensor(mixr, a, gy, ALU.mult), [ak, gyk], [mixrk])
        pht, phtk = self.ps()
        for n in range(4):
            self.pe(lambda e, n=n: e.transpose(pht[0:NB, n * 96:(n + 1) * 96], a[:, n, :], self.ident[0:96, 0:96]), [ak, "consts"], [phtk])
        htok, htokk = A.alloc("dhtok", NB, [384], F32)
        act(lambda e: e.activation(htok, pht[0:NB, 0:384], AF.Copy), [phtk], [htokk])
        self.store(o["rg_h_s"][l], htok, [htokk], "o_rghs%d" % l)
        A.release(m1)
        w6, w6k = self.wload(win[:, :, C_SQ:C_SQ + 512], 128, [8, 512])
        m1 = A.mark()
        pst, pk = self.ps()
        self.mm_group(pst[0:NB, 0:512], [(xnb(kc), w6[:, kc, :]) for kc in range(8)], [xk, w6k], pk)
        qkv, qkvk = A.alloc("dqkv", NB, [512], F32)
        act(lambda e, pst=pst: e.activation(qkv, pst[0:NB, 0:512], AF.Copy), [pk], [qkvk])
        qn, qnk = A.alloc("dqn", NB, [4, 64], F32)
        kn, knk = A.alloc("dkn", NB, [2, 64], F32)
        self.tok_qknorm(qkv[:, 0:256].rearrange("p (h d) -> p h d", h=4), qkvk, NB, 4, self.g64row[0:NB, 0, l, :], qn, qnk, "dsq")
        self.tok_qknorm(qkv[:, 256:384].rearrange("p (h d) -> p h d", h=2), qkvk, NB, 2, self.g64row[0:NB, 1, l, :], kn, knk, "dsk")
        vnew = qkv[:, 384:512]
        self.store(o["swa_k_s"][l, :, 127, :], kn.rearrange("p h d -> p (h d)"), [knk], "o_swaks_new%d" % l)
        self.store(o["swa_v_s"][l, :, 127, :], vnew, [qkvk], "o_swavs_new%d" % l)
        self.dram_copy(o["swa_k_s"][l, :, 0:127, :], i["cache_swa_k"][l, :, 1:128, :], "o_swaks_old%d" % l)
        self.dram_copy(o["swa_v_s"][l, :, 0:127, :], i["cache_swa_v"][l, :, 1:128, :], "o_swavs_old%d" % l)
        pq, pqk = self.ps()
        self.mm_group(pq[:, 0:256], [(self.sel[:, :], qn.rearrange("p h d -> p (h d)"))], [qnk, "consts"], pqk)
        qrep, qrepk = A.alloc("dqrep", 128, [4, 64], F32)
        act(lambda e: e.activation(qrep.rearrange("p h d -> p (h d)"), pq[:, 0:256], AF.Copy), [pqk], [qrepk])
        Kc, Kck = A.alloc("dKc", 128, [16, 128], F32)
        Vc, Vck = A.alloc("dVc", 128, [16, 128], F32)
        self.load(Kc, i["cache_swa_k"][l].rearrange("b (jc j) n -> (b jc) j n", jc=8), [Kck])
        self.load(Vc, i["cache_swa_v"][l].rearrange("b (jc j) n -> (b jc) j n", jc=8), [Vck])
        tmp, tmpk = A.alloc("dtmp", 128, [16, 4, 64], F32)
        for kv in range(2):
            dve(lambda e, kv=kv: e.tensor_tensor(tmp[:, :, kv * 2:(kv + 1) * 2, :], Kc[:, :, kv * 64:(kv + 1) * 64].unsqueeze(2).broadcast_to([128, 16, 2, 64]),
                                                  qrep[:, kv * 2:(kv + 1) * 2, :].unsqueeze(1).broadcast_to([128, 16, 2, 64]), ALU.mult), [Kck, qrepk], [tmpk])
        s_, s_k = A.alloc("ds", 128, [64], F32)
        dve(lambda e: e.tensor_reduce(s_, tmp.rearrange("p j h d -> p (j h) d"), AX.X, ALU.add), [tmpk], [s_k])
        dve(lambda e: e.scalar_tensor_tensor(s_, s_, 0.125, self.alibi_dec[:, :, :].rearrange("p j h -> p (j h)"), ALU.mult, ALU.add), [s_k, "consts"], [s_k])
        e_, e_k = A.alloc("de", 128, [64], F32)
        act(lambda e: e.activation(e_, s_, AF.Exp), [s_k], [e_k])
        t2, t2k = A.alloc("dt2", NB, [4, 64], F32)
        for kv in range(2):
            dve(lambda e, kv=kv: e.tensor_tensor(t2[:, kv * 2:(kv + 1) * 2, :], qn[:, kv * 2:(kv + 1) * 2, :], kn[:, kv:kv + 1, :].broadcast_to([NB, 2, 64]), ALU.mult), [qnk, knk], [t2k])
        en, enk = A.alloc("den", NB, [4], F32)
        dve(lambda e: e.tensor_reduce(en, t2, AX.X, ALU.add), [t2k], [enk])
        act(lambda e: e.activation(en, en, AF.Exp, scale=0.125), [enk], [enk])
        dp, dpk = A.alloc("ddp", 128, [4], F32)
        dve(lambda e: e.tensor_reduce(dp, e_.rearrange("p (j h) -> p h j", h=4), AX.X, ALU.add), [e_k], [dpk])
        pd, pdk = self.ps()
        self.mm_group(pd[0:NB, 0:4], [(self.selT[:, :], dp)], [dpk, "consts"], pdk)
        den, denk = A.alloc("dden", NB, [4], F32)
        dve(lambda e: e.tensor_tensor(den, pd[0:NB, 0:4], en, ALU.add), [pdk, enk], [denk])
        dve(lambda e: e.tensor_tensor(den, den, self.esink[0:NB, l, :], ALU.add), [denk, "consts"], [denk])
        dve(lambda e: e.reciprocal(den, den), [denk], [denk])
        for kv in range(2):
            dve(lambda e, kv=kv: e.tensor_tensor(tmp[:, :, kv * 2:(kv + 1) * 2, :], Vc[:, :, kv * 64:(kv + 1) * 64].unsqueeze(2).broadcast_to([128, 16, 2, 64]),
                                                  e_.rearrange("p (j h) -> p j h", h=4)[:, :, kv * 2:(kv + 1) * 2].unsqueeze(3).broadcast_to([128, 16, 2, 64]), ALU.mult),
                [Vck, e_k, tmpk], [tmpk])
        op_, op_k = A.alloc("dop", 128, [4, 64], F32)
        dve(lambda e: e.tensor_reduce(op_, tmp.rearrange("p j h d -> p h d j"), AX.X, ALU.add), [tmpk], [op_k])
        po, pok = self.ps()
        self.mm_group(po[0:NB, 0:256], [(self.selT[:, :], op_.rearrange("p h d -> p (h d)"))], [op_k, "consts"], pok)
        for kv in range(2):
            dve(lambda e, kv=kv: e.tensor_tensor(t2[:, kv * 2:(kv + 1) * 2, :], vnew[:, kv * 64:(kv + 1) * 64].unsqueeze(1).broadcast_to([NB, 2, 64]),
                                                  en[:, kv * 2:(kv + 1) * 2].unsqueeze(2).broadcast_to([NB, 2, 64]), ALU.mult), [qkvk, enk, t2k], [t2k])
        dve(lambda e: e.tensor_tensor(t2, t2, po[0:NB, 0:256].rearrange("p (h d) -> p h d", h=4), ALU.add), [t2k, pok], [t2k])
        dve(lambda e: e.tensor_tensor(t2, t2, den.unsqueeze(2).broadcast_to([NB, 4, 64]), ALU.mult), [t2k, denk], [t2k])
        pt_ = self.ps()
        self.tok_to_fm(t2.rearrange("p h d -> p (h d)"), t2k, NB, 64, 4, pt_)
        act(lambda e: e.activation(mixs.rearrange("p h b -> p (h b)"), pt_[0][0:64, 0:4 * NB], AF.Copy), [pt_[1]], [mixsk])
        A.release(m1)
        wo = i["w_out"][l]
        wog, wogk = self.wload(wo[0:384, :].rearrange("(k p) n -> p k n", p=96), 96, [4, 1024])
        wor, work = self.wload(wo[384:768, :].rearrange("(k p) n -> p k n", p=96), 96, [4, 1024])
        wos, wosk = self.wload(wo[768:1024, :].rearrange("(k p) n -> p k n", p=64), 64, [4, 1024])
        self.proj_residual([(wog, wogk, mixg, mixgk, 4), (wor, work, mixr, mixrk, 4), (wos, wosk, mixs, mixsk, 4)], 1, NB, xt=xsT, xkf=xsk)
        A.release(m0)
        self.rmsnorm_fm(xsT, xskeys, self.g4[:, 1, l, :], xn[:, :, 0:NB], ["xn0"], NB)
        m0 = A.mark()
        wq, wqk = self.wload(i["mem_w_q"][l].rearrange("(kc p) n -> p kc n", p=128), 128, [8, 256])
        wmo, wmok = self.wload(i["mem_w_o"][l].rearrange("(k p) n -> p k n", p=64), 64, [4, 1024])
        pst, pk = self.ps()
        self.mm_group(pst[0:NB, 0:256], [(xnb(kc), wq[:, kc, :]) for kc in range(8)], [xk, wqk], pk)
        qraw, qrawk = A.alloc("mqraw", NB, [256], F32)
        act(lambda e, pst=pst: e.activation(qraw, pst[0:NB, 0:256], AF.Copy), [pk], [qrawk])
        m_qn, m_qnk = A.alloc("mqn", NB, [4, 64], F32)
        self.tok_qknorm(qraw.rearrange("p (h d) -> p h d", h=4), qrawk, NB, 4, self.g64row[0:NB, 2, l, :], m_qn, m_qnk, "dmq")
        m_pq, m_pqk = self.ps()
        self.mm_group(m_pq[:, 0:256], [(self.sel[:, :], m_qn.rearrange("p h d -> p (h d)"))], [m_qnk, "consts"], m_pqk)
        m_qrep, m_qrepk = A.alloc("mqrep", 128, [256], F32)
        act(lambda e: e.activation(m_qrep, m_pq[:, 0:256], AF.Copy), [m_pqk], [m_qrepk])
        opa, opak = A.alloc("mopa", 128, [2, 256], F32)
        dpa, dpak = A.alloc("mdpa", 128, [2, 4], F32)
        ckv = lambda name: i[name][l].rearrange("b (jc jj) n -> (b jc) jj n", jc=8)
        for half in range(2):
            m2 = A.mark()
            m_Kc, m_Kck = A.alloc("mKc", 128, [16, 256], F32)
            m_Vc, m_Vck = A.alloc("mVc", 128, [16, 256], F32)
            self.load(m_Kc, ckv("cache_mem_k")[:, half * 16:(half + 1) * 16, :], [m_Kck])
            self.load(m_Vc, ckv("cache_mem_v")[:, half * 16:(half + 1) * 16, :], [m_Vck])
            m_tmp, m_tmpk = A.alloc("mtmp", 128, [16, 256], F32)
            dve(lambda e, m_Kc=m_Kc, m_tmp=m_tmp: e.tensor_tensor(m_tmp, m_Kc, m_qrep.unsqueeze(1).broadcast_to([128, 16, 256]), ALU.mult), [m_Kck, m_qrepk], [m_tmpk])
            m_s_, m_s_k = A.alloc("ms", 128, [64], F32)
            dve(lambda e, m_tmp=m_tmp, m_s_=m_s_: e.tensor_reduce(m_s_, m_tmp.rearrange("p j (h d) -> p (j h) d", h=4), AX.X, ALU.add), [m_tmpk], [m_s_k])
            act(lambda e, m_s_=m_s_: e.activation(m_s_, m_s_, AF.Exp, scale=0.125), [m_s_k], [m_s_k])
            dve(lambda e, m_s_=m_s_, half=half: e.tensor_reduce(dpa[:, half, :], m_s_.rearrange("p (j h) -> p h j", h=4), AX.X, ALU.add), [m_s_k], [dpak])
            dve(lambda e, m_Vc=m_Vc, m_tmp=m_tmp, m_s_=m_s_: e.tensor_tensor(m_tmp.rearrange("p j (h d) -> p j h d", h=4), m_Vc.rearrange("p j (h d) -> p j h d", h=4),
                                                              m_s_.rearrange("p (j h) -> p j h", h=4).unsqueeze(3).broadcast_to([128, 16, 4, 64]), ALU.mult),
                [m_Vck, m_s_k, m_tmpk], [m_tmpk])
            dve(lambda e, m_tmp=m_tmp, half=half: e.tensor_reduce(opa[:, half, :].rearrange("p (h d) -> p h d", h=4), m_tmp.rearrange("p j (h d) -> p h d j", h=4), AX.X, ALU.add),
                [m_tmpk], [opak])
            A.release(m2)
        dve(lambda e: e.tensor_tensor(opa[:, 0, :], opa[:, 0, :], opa[:, 1, :], ALU.add), [opak], [opak])
        dve(lambda e: e.tensor_tensor(dpa[:, 0, :], dpa[:, 0, :], dpa[:, 1, :], ALU.add), [dpak], [dpak])
        m_po, m_pok = self.ps()
        self.mm_group(m_po[0:NB, 0:256], [(self.selT[:, :], opa[:, 0, :])], [opak, "consts"], m_pok)
        m_pd, m_pdk = self.ps()
        self.mm_group(m_pd[0:NB, 0:4], [(self.selT[:, :], dpa[:, 0, :])], [dpak, "consts"], m_pdk)
        m_den, m_denk = A.alloc("mden", NB, [4], F32)
        dve(lambda e: e.reciprocal(m_den, m_pd[0:NB, 0:4]), [m_pdk], [m_denk])
        m_ot, m_otk = A.alloc("mot", NB, [4, 64], F32)
        dve(lambda e: e.tensor_tensor(m_ot, m_po[0:NB, 0:256].rearrange("p (h d) -> p h d", h=4), m_den.unsqueeze(2).broadcast_to([NB, 4, 64]), ALU.mult), [m_pok, m_denk], [m_otk])
        m_pt_ = self.ps()
        self.tok_to_fm(m_ot.rearrange("p h d -> p (h d)"), m_otk, NB, 64, 4, m_pt_)
        omT, omTk = A.alloc("momT", 64, [4, NB], BF16)
        act(lambda e: e.activation(omT.rearrange("p h b -> p (h b)"), m_pt_[0][0:64, 0:4 * NB], AF.Copy), [m_pt_[1]], [omTk])
        self.proj_residual([(wmo, wmok, omT, omTk, 4)], 1, NB, xt=xsT, xkf=xsk)
        A.release(m0)
        self.rmsnorm_fm(xsT, xskeys, self.g4[:, 3, l, :], xn[:, :, 0:NB], ["xn0"], NB)
        m0 = A.mark()
        hT, hTk = A.alloc("dhT", 128, [32, NB], BF16)
        w1d = i["ffn_w1"][l].rearrange("(kc p) n -> p kc n", p=128)
        w2d = i["ffn_w2"][l]
        ph, phk = self.ps()
        for p8 in range(8):
            wp, wpk = self.wload(w1d[:, :, p8 * 512:(p8 + 1) * 512], 128, [8, 512])
            for s4 in range(4):
                sub = p8 * 4 + s4
                self.mm_group(ph[:, sub * NB:(sub + 1) * NB], [(wp[:, kc, s4 * 128:(s4 + 1) * 128], xnb(kc)) for kc in range(8)], [wpk, xk], phk)
        fsq, fsqk = A.alloc("dfsq", 128, [32 * NB], F32)
        act(lambda e: e.activation(fsq, ph[:, 0:32 * NB], AF.Square), [phk], [fsqk])
        dve(lambda e: e.scalar_tensor_tensor(hT.rearrange("p s b -> p (s b)"), ph[:, 0:32 * NB], 0.0, fsq, ALU.is_gt, ALU.mult), [phk, fsqk], [hTk])
        allxs = ["xs_%d" % c for c in range(8)]
        for p8 in range(8):
            w2p, w2pk = self.wload(w2d[p8 * 512:(p8 + 1) * 512, :].rearrange("(k p) n -> p k n", p=128), 128, [4, 1024])
            py, pyk = self.ps()
            for mc in range(8):
                self.mm_group(py[:, mc * NB:(mc + 1) * NB], [(w2p[:, k, mc * 128:(mc + 1) * 128], hT[:, p8 * 4 + k, :]) for k in range(4)], [w2pk, hTk], pyk)
            dve(lambda e, py=py: e.tensor_tensor(xsT.rearrange("p c b -> p (c b)"), xsT.rearrange("p c b -> p (c b)"), py[:, 0:8 * NB], ALU.add), [pyk] + allxs, allxs)
        A.release(m0)

    def build(self):
        self.arena_n = 18688
        self.declare()
        self.setup()
        if self.do_decode:
            self.decode()
        if self.do_prompt:
            self.prompt()
        self.P.emit()
        self.es.close()
        return self.nc


def shared_inputs(inp, L, par=True):
    f = lambda a: np.ascontiguousarray(np.asarray(a, dtype=np.float32))
    m = {}
    for k in ["w_in", "w_out", "mem_w_q", "mem_w_kv", "mem_w_o", "ffn_w1", "ffn_w2", "gla_w_gate2", "gla_b_gate", "rg_w_a", "rg_w_x"]:
        m[k] = f(inp[k][:L])
    fm = lambda v: np.asarray(v[:L]).reshape(L, 8, 128).transpose(2, 0, 1)
    m["v_g4"] = f(np.stack([fm(inp["g_mix"]), fm(inp["g_mem_x"]), fm(inp["g_mem_m"]), fm(inp["g_ffn"])], axis=1))
    m["v_bgT"] = f(np.asarray(inp["gla_b_gate"][:L]).reshape(L, 4, 48).transpose(2, 0, 1))
    m["v_gout"] = f(np.asarray(inp["gla_g_out"][:L]).T)
    m["v_convw"] = f(np.asarray(inp["rg_conv_w"][:L]).reshape(L, 4, 4, 96).transpose(3, 0, 2, 1))
    r4 = lambda v: np.asarray(v[:L]).reshape(L, 4, 96).transpose(2, 0, 1)
    m["v_rg4"] = f(np.stack([r4(inp["rg_conv_b"]), r4(inp["rg_b_a"]), r4(inp["rg_b_x"]), r4(inp["rg_lam"])], axis=1))
    g64 = np.stack([np.asarray(inp["swa_g_q"][:L]), np.asarray(inp["swa_g_k"][:L]), np.asarray(inp["mem_g_q"][:L]), np.asarray(inp["mem_g_k"][:L])], axis=0)
    m["v_g64"] = f(g64.transpose(2, 0, 1))
    m["v_g64row"] = f(g64.reshape(-1))
    m["v_sinks"] = f(np.asarray(inp["swa_sinks"][:L]).reshape(-1))
    m.update(host_consts())
    if not par:
        m.pop("c_btdelta")
    return m


_T, _L, _NB, _NC = 8192, 4, 16, 8
_TC = _T // 4
_CACHE = {}


def core_masks(c):
    b, seg = c // 4, c % 4
    pm = np.array([1.0 if (r // 4 == b and r % 4 < seg) else 0.0 for r in range(8)], np.float32)
    pv = np.array([1.0 if (r == c - 1 and seg > 0) else 0.0 for r in range(8)], np.float32)
    return {"pmask": pm, "prevmask": pv, "isfirst": np.array([1.0 if seg == 0 else 0.0], np.float32)}


def kernel(**inputs):
    inp = {k: np.asarray(v) for k, v in inputs.items()}
    if "nc" not in _CACHE:
        _CACHE["nc"] = KB(_TC, _L, _NB, par=True).build()
    nc = _CACHE["nc"]
    shared = shared_inputs(inp, _L)
    f = lambda a: np.ascontiguousarray(a, dtype=np.float32)
    in_maps = []
    for c in range(_NC):
        b, seg = c // 4, c % 4
        s0 = c * _NB
        m = dict(shared)
        m.update(core_masks(c))
        m["xp"] = f(inp["x_prompt"][b, seg * _TC:(seg + 1) * _TC])
        m["xs"] = f(inp["x_sample"][s0:s0 + _NB, 0])
        m["memp"] = f(inp["mem_prompt"][b])
        m["state_gla"] = f(inp["state_gla"][:, s0:s0 + _NB])
        m["state_rg_h"] = f(inp["state_rg_h"][:, s0:s0 + _NB])
        m["state_rg_conv"] = f(inp["state_rg_conv"][:, s0:s0 + _NB])
        m["cache_swa_k"] = f(inp["cache_swa_k"][:, s0:s0 + _NB]).reshape(_L, _NB, 128, 128)
        m["cache_swa_v"] = f(inp["cache_swa_v"][:, s0:s0 + _NB]).reshape(_L, _NB, 128, 128)
        m["cache_mem_k"] = f(inp["cache_mem_k"][:, s0:s0 + _NB]).reshape(_L, _NB, 256, 256)
        m["cache_mem_v"] = f(inp["cache_mem_v"][:, s0:s0 + _NB]).reshape(_L, _NB, 256, 256)
        in_maps.append(m)
    res = run_bass_kernel_spmd(nc, in_maps, core_ids=list(range(_NC)))
    r = res.results
    pc = (3, 7)
    cat = lambda name, ax: np.concatenate([np.asarray(r[c][name]) for c in range(_NC)], axis=ax)
    stk = lambda name: np.stack([np.asarray(r[c][name]) for c in pc], axis=1)
    y_prompt = cat("y_prompt", 0).reshape(2, _T, D)
    y_sample = cat("y_sample", 0).reshape(128, 1, D)
    out = (
        y_prompt, y_sample,
        stk("gla_S_p"), cat("gla_S_s", 1),
        stk("rg_h_p"), cat("rg_h_s", 1),
        stk("rg_conv_p"), cat("rg_conv_s", 1),
        stk("swa_k_p").reshape(_L, 2, 128, 2, 64), cat("swa_k_s", 1).reshape(_L, 128, 128, 2, 64),
        stk("swa_v_p").reshape(_L, 2, 128, 2, 64), cat("swa_v_s", 1).reshape(_L, 128, 128, 2, 64),
        stk("mem_k_p").reshape(_L, 2, 256, 4, 64), stk("mem_v_p").reshape(_L, 2, 256, 4, 64),
    )
    return tuple(np.ascontiguousarray(o, dtype=np.float32) for o in out)
```
